# Optimizing a Trainium2 kernel written in Bass

```python
import math
import jax, jax.numpy as jnp
from jax import lax
import numpy as np

D_MODEL = 1024
BATCH = 8
SEQ = 4096
DEPTH = 4

MLA_HEADS = 8
QK_NOPE_DIM = 64
QK_ROPE_DIM = 32
QK_HEAD_DIM = QK_NOPE_DIM + QK_ROPE_DIM
V_HEAD_DIM = 64
Q_LORA_RANK = 256
KV_LORA_RANK = 256
CONV_CHANNELS = D_MODEL // 2
SHORT_CONV_WIDTH = 3
MIX_IN_DIM = Q_LORA_RANK + KV_LORA_RANK + QK_ROPE_DIM + 3 * CONV_CHANNELS
MIX_OUT_DIM = MLA_HEADS * V_HEAD_DIM + CONV_CHANNELS
ROPE_THETA = 10000.0
Q_BLOCK = 128
SSM_WIDTH = D_MODEL
SSM_GROUP = 16
SSM_GROUPS = SSM_WIDTH // SSM_GROUP
SSM_STATE = 64
DT_MIN = 1e-3
DT_MAX = 1e-1
FFN_HIDDEN = 2816
FFN_CONV_WIDTH = 3
N_EVEN = (DEPTH + 1) // 2
N_ODD = DEPTH // 2
EPS = 1e-6

kernel_name = "hybrid_mla_shortconv_s5_convffn"


def rms_norm(x, g):
    x32 = x.astype(jnp.float32)
    y = x32 * lax.rsqrt(jnp.mean(x32 * x32, axis=-1, keepdims=True) + EPS)
    return (y * g.astype(jnp.float32)).astype(x.dtype)


def causal_depthwise_conv(x, w):
    k_width, channels = w.shape
    return lax.conv_general_dilated(
        x, w[:, None, :].astype(x.dtype), window_strides=(1,), padding=[(k_width - 1, 0)],
        dimension_numbers=("NWC", "WIO", "NWC"), feature_group_count=channels)


def rope_tables(seq):
    inv_freq = 1.0 / (ROPE_THETA ** (jnp.arange(0, QK_ROPE_DIM, 2, dtype=jnp.float32) / QK_ROPE_DIM))
    ang = jnp.arange(seq, dtype=jnp.float32)[:, None] * inv_freq[None, :]
    return jnp.cos(ang)[:, None, :], jnp.sin(ang)[:, None, :]


def apply_rope(x, cos, sin):
    x1, x2 = jnp.split(x.astype(jnp.float32), 2, axis=-1)
    return jnp.concatenate([x1 * cos - x2 * sin, x2 * cos + x1 * sin], axis=-1).astype(x.dtype)


def causal_block_attention(q, k, v):
    seq = q.shape[1]
    scale = q.shape[-1] ** -0.5
    outs = []
    for i in range(seq // Q_BLOCK):
        lo, hi = i * Q_BLOCK, (i + 1) * Q_BLOCK
        s = jnp.einsum("bqhd,bkhd->bhqk", q[:, lo:hi], k[:, :hi]).astype(jnp.float32) * scale
        causal = jnp.arange(hi)[None, :] <= jnp.arange(lo, hi)[:, None]
        s = jnp.where(causal, s, -jnp.inf)
        p = jax.nn.softmax(s, axis=-1).astype(v.dtype)
        outs.append(jnp.einsum("bhqk,bkhd->bqhd", p, v[:, :hi]))
    return jnp.concatenate(outs, axis=1)


def mla_shortconv_mixer(h, w_in, cq_norm, ckv_norm, w_uq, w_ukv, q_gain, k_gain, sconv_w, w_out, cos, sin):
    bsz, seq, _ = h.shape
    proj = h @ w_in
    splits = np.cumsum([Q_LORA_RANK, KV_LORA_RANK, QK_ROPE_DIM, CONV_CHANNELS, CONV_CHANNELS]).tolist()
    c_q, c_kv, k_rope, gate_b, gate_c, conv_in = jnp.split(proj, splits, axis=-1)
    q = (rms_norm(c_q, cq_norm) @ w_uq).reshape(bsz, seq, MLA_HEADS, QK_HEAD_DIM)
    kv = (rms_norm(c_kv, ckv_norm) @ w_ukv).reshape(bsz, seq, MLA_HEADS, QK_NOPE_DIM + V_HEAD_DIM)
    k_nope, v = kv[..., :QK_NOPE_DIM], kv[..., QK_NOPE_DIM:]
    k = jnp.concatenate(
        [k_nope, jnp.broadcast_to(k_rope[:, :, None, :], (bsz, seq, MLA_HEADS, QK_ROPE_DIM))], axis=-1)
    q = rms_norm(q, q_gain)
    k = rms_norm(k, k_gain)
    q = jnp.concatenate([q[..., :QK_NOPE_DIM], apply_rope(q[..., QK_NOPE_DIM:], cos, sin)], axis=-1)
    k = jnp.concatenate([k[..., :QK_NOPE_DIM], apply_rope(k[..., QK_NOPE_DIM:], cos, sin)], axis=-1)
    attn = causal_block_attention(q, k, v).reshape(bsz, seq, MLA_HEADS * V_HEAD_DIM)
    conv = gate_b * causal_depthwise_conv(gate_c * conv_in, sconv_w)
    return jnp.concatenate([attn, conv], axis=-1) @ w_out


def _ssm_combine(earlier, later):
    ar1, ai1, br1, bi1 = earlier
    ar2, ai2, br2, bi2 = later
    return (ar2 * ar1 - ai2 * ai1,
            ar2 * ai1 + ai2 * ar1,
            ar2 * br1 - ai2 * bi1 + br2,
            ar2 * bi1 + ai2 * br1 + bi2)


def s5_mixer(h, w_in, lambda_re, lambda_im, log_step, b_re, b_im, c_re, c_im, d_skip, w_glu):
    bsz, seq, _ = h.shape
    f32 = jnp.float32
    u = (h @ w_in).astype(f32)
    ug = u.reshape(bsz, seq, SSM_GROUPS, SSM_GROUP)
    lr, li = lambda_re.astype(f32), lambda_im.astype(f32)
    dt = jnp.exp(log_step.astype(f32))[:, None]
    mag = jnp.exp(lr * dt)
    ar, ai = mag * jnp.cos(li * dt), mag * jnp.sin(li * dt)
    nr, ni = ar - 1.0, ai
    den = lr * lr + li * li
    zr, zi = (nr * lr + ni * li) / den, (ni * lr - nr * li) / den
    br_, bi_ = b_re.astype(f32), b_im.astype(f32)
    bbar_r = zr[..., None] * br_ - zi[..., None] * bi_
    bbar_i = zr[..., None] * bi_ + zi[..., None] * br_
    bu_r = jnp.einsum("gpc,bsgc->bsgp", bbar_r, ug)
    bu_i = jnp.einsum("gpc,bsgc->bsgp", bbar_i, ug)
    a_r = jnp.broadcast_to(ar, (1, seq, SSM_GROUPS, SSM_STATE))
    a_i = jnp.broadcast_to(ai, (1, seq, SSM_GROUPS, SSM_STATE))
    _, _, st_r, st_i = lax.associative_scan(_ssm_combine, (a_r, a_i, bu_r, bu_i), axis=1)
    y = (jnp.einsum("gcp,bsgp->bsgc", c_re.astype(f32), st_r)
         - jnp.einsum("gcp,bsgp->bsgc", c_im.astype(f32), st_i)).reshape(bsz, seq, SSM_WIDTH)
    y = y + d_skip.astype(f32) * u
    g = jax.nn.gelu(y).astype(h.dtype)
    a, b = jnp.split(g @ w_glu, 2, axis=-1)
    return a * jax.nn.sigmoid(b)


def conv_ffn(h, w_up, conv_w, w_down):
    up = causal_depthwise_conv(h @ w_up, conv_w)
    gate, val = jnp.split(up, 2, axis=-1)
    return (jax.nn.silu(gate) * val) @ w_down


def setup_inputs(seed: int = 0) -> dict:
    key = jax.random.key(seed)
    ks = jax.random.split(key, 32)
    f32 = jnp.float32

    def nrm(k, shape, scale):
        return jax.random.normal(k, shape, f32) * scale

    def gain(k, shape):
        return 1.0 + 0.02 * jax.random.normal(k, shape, f32)

    lam_im_base = jnp.pi * jnp.arange(SSM_STATE, dtype=f32)
    return {
        "x": nrm(ks[0], (BATCH, SEQ, D_MODEL), 1.0),
        "attn_norm": gain(ks[1], (N_EVEN, D_MODEL)),
        "mix_w_in": nrm(ks[2], (N_EVEN, D_MODEL, MIX_IN_DIM), D_MODEL ** -0.5),
        "cq_norm": gain(ks[3], (N_EVEN, Q_LORA_RANK)),
        "ckv_norm": gain(ks[4], (N_EVEN, KV_LORA_RANK)),
        "w_uq": nrm(ks[5], (N_EVEN, Q_LORA_RANK, MLA_HEADS * QK_HEAD_DIM), Q_LORA_RANK ** -0.5),
        "w_ukv": nrm(ks[6], (N_EVEN, KV_LORA_RANK, MLA_HEADS * (QK_NOPE_DIM + V_HEAD_DIM)), KV_LORA_RANK ** -0.5),
        "q_gain": gain(ks[7], (N_EVEN, QK_HEAD_DIM)),
        "k_gain": gain(ks[8], (N_EVEN, QK_HEAD_DIM)),
        "sconv_w": nrm(ks[9], (N_EVEN, SHORT_CONV_WIDTH, CONV_CHANNELS), SHORT_CONV_WIDTH ** -0.5),
        "mix_w_out": nrm(ks[10], (N_EVEN, MIX_OUT_DIM, D_MODEL), MIX_OUT_DIM ** -0.5),
        "ssm_norm": gain(ks[11], (N_ODD, D_MODEL)),
        "ssm_w_in": nrm(ks[12], (N_ODD, D_MODEL, SSM_WIDTH), D_MODEL ** -0.5),
        "lambda_re": -0.5 + 0.01 * jax.random.normal(ks[13], (N_ODD, SSM_GROUPS, SSM_STATE), f32),
        "lambda_im": lam_im_base + 0.01 * jax.random.normal(ks[14], (N_ODD, SSM_GROUPS, SSM_STATE), f32),
        "log_step": jax.random.uniform(ks[15], (N_ODD, SSM_GROUPS), f32,
                                       minval=math.log(DT_MIN), maxval=math.log(DT_MAX)),
        "b_re": nrm(ks[16], (N_ODD, SSM_GROUPS, SSM_STATE, SSM_GROUP), (2 * SSM_GROUP) ** -0.5),
        "b_im": nrm(ks[17], (N_ODD, SSM_GROUPS, SSM_STATE, SSM_GROUP), (2 * SSM_GROUP) ** -0.5),
        "c_re": nrm(ks[18], (N_ODD, SSM_GROUPS, SSM_GROUP, SSM_STATE), (2 * SSM_STATE) ** -0.5),
        "c_im": nrm(ks[19], (N_ODD, SSM_GROUPS, SSM_GROUP, SSM_STATE), (2 * SSM_STATE) ** -0.5),
        "d_skip": nrm(ks[20], (N_ODD, SSM_WIDTH), 1.0),
        "w_glu": nrm(ks[21], (N_ODD, SSM_WIDTH, 2 * D_MODEL), SSM_WIDTH ** -0.5),
        "ffn_norm": gain(ks[22], (DEPTH, D_MODEL)),
        "ffn_w_up": nrm(ks[23], (DEPTH, D_MODEL, 2 * FFN_HIDDEN), D_MODEL ** -0.5),
        "ffn_conv_w": nrm(ks[24], (DEPTH, FFN_CONV_WIDTH, 2 * FFN_HIDDEN), FFN_CONV_WIDTH ** -0.5),
        "ffn_w_down": nrm(ks[25], (DEPTH, FFN_HIDDEN, D_MODEL), FFN_HIDDEN ** -0.5),
    }


def reference(x, attn_norm, mix_w_in, cq_norm, ckv_norm, w_uq, w_ukv, q_gain, k_gain, sconv_w, mix_w_out,
              ssm_norm, ssm_w_in, lambda_re, lambda_im, log_step, b_re, b_im, c_re, c_im, d_skip, w_glu,
              ffn_norm, ffn_w_up, ffn_conv_w, ffn_w_down):
    cos, sin = rope_tables(x.shape[1])
    for layer in range(DEPTH):
        i = layer // 2
        if layer % 2 == 0:
            x = x + mla_shortconv_mixer(rms_norm(x, attn_norm[i]), mix_w_in[i], cq_norm[i], ckv_norm[i],
                                        w_uq[i], w_ukv[i], q_gain[i], k_gain[i], sconv_w[i], mix_w_out[i],
                                        cos, sin)
        else:
            x = x + s5_mixer(rms_norm(x, ssm_norm[i]), ssm_w_in[i], lambda_re[i], lambda_im[i], log_step[i],
                             b_re[i], b_im[i], c_re[i], c_im[i], d_skip[i], w_glu[i]).astype(x.dtype)
        x = x + conv_ffn(rms_norm(x, ffn_norm[layer]), ffn_w_up[layer], ffn_conv_w[layer], ffn_w_down[layer])
    return x
```

```python
import math
from contextlib import ExitStack

import numpy as np
import concourse.bass as bass
import concourse.mybir as mybir
from concourse.bass_utils import run_bass_kernel_spmd

F32 = mybir.dt.float32
BF16 = mybir.dt.bfloat16
I32 = mybir.dt.int32
AF = mybir.ActivationFunctionType
ALU = mybir.AluOpType
AX = mybir.AxisListType

D = 1024
SEQ = 4096
NCORES = 8
DEPTH = 4
FH = 2816
EPS = 1e-6
P = 128


class Buf:
    __slots__ = ("name", "w", "r", "sem", "dcnt")

    def __init__(self, name):
        self.name = name
        self.w = None
        self.r = {}
        self.sem = None
        self.dcnt = 0


class _Dummy:
    def then_inc(self, *a, **k):
        return self


class _Rec:
    def __init__(self):
        self.call = None

    def __getattr__(self, name):
        def f(*a, **kw):
            self.call = (name, a, kw)
            return _Dummy()
        return f


class Ctx:
    def __init__(self, nc):
        self.nc = nc
        self.E = {"pe": nc.tensor, "act": nc.scalar, "dve": nc.vector, "pool": nc.gpsimd, "sp": nc.sync}
        self.sem = {}
        self.cnt = {}
        self.pending = {}
        for e in self.E:
            self.sem[e] = nc.alloc_semaphore("prog_" + e)
            self.cnt[e] = 0
            self.pending[e] = False
        self.waited = {}
        self.dma_sems = []
        self.sem_pool = []
        self.n_dsem = 0
        self.n_ins = 0
        self.rec = None

    def _wait(self, eng, toks):
        need = {}
        for t in toks:
            if t is None:
                continue
            sem, val, src = t
            if src == eng and eng in ("pe", "sp"):
                continue
            k = id(sem)
            if k not in need or need[k][1] < val:
                need[k] = (sem, val)
        for k, (sem, val) in need.items():
            if self.waited.get((eng, k), 0) >= val:
                continue
            self.E[eng].wait_ge(sem, val)
            self.n_ins += 1
            self.waited[(eng, k)] = val

    def _deps(self, reads, writes):
        deps = []
        for b in reads:
            deps.append(b.w)
        for b in writes:
            deps.append(b.w)
            deps.extend(b.r.values())
        return deps

    def op(self, eng, fn, reads=(), writes=(), inc=True):
        if self.rec is not None:
            r = _Rec()
            fn(r)
            self.rec.append(("op", eng, r.call, list(reads), list(writes), inc))
            return None
        self._wait(eng, self._deps(reads, writes))
        ins = fn(self.E[eng])
        self.n_ins += 1
        if inc:
            self.cnt[eng] += 1
            ins.then_inc(self.sem[eng], 1)
            self.pending[eng] = False
            tok = (self.sem[eng], self.cnt[eng], eng)
        else:
            self.pending[eng] = True
            tok = (self.sem[eng], self.cnt[eng] + 1, eng)
        for b in reads:
            b.r[eng] = tok
        for b in writes:
            b.w = tok
            b.r = {}
        return ins

    def dma(self, q, out, in_, slot, reads=(), writes=(), chain=False, **kw):
        if self.rec is not None:
            self.rec.append(("dma", q, out, in_, slot, list(reads), list(writes), chain, kw))
            return None
        if slot.sem is None:
            if self.sem_pool:
                slot.sem, slot.dcnt = self.sem_pool.pop()
            else:
                self.n_dsem += 1
                slot.sem = self.nc.alloc_semaphore(f"dsem{self.n_dsem}")
                slot.dcnt = 0
            self.dma_sems.append(slot)
        deps = self._deps(reads, writes)
        if chain:
            deps = [d for d in deps if d is None or d[2] != "dma" or d[0] is not slot.sem]
        self._wait(q, deps)
        ins = self.E[q].dma_start(out=out, in_=in_, **kw)
        self.n_ins += 1
        slot.dcnt += 16
        ins.then_inc(slot.sem, 16)
        tok = (slot.sem, slot.dcnt, "dma")
        for b in reads:
            b.r["dma" + str(id(slot))] = tok
        for b in writes:
            b.w = tok
            b.r = {}
        return tok

    def pump(self, items, n):
        rec, self.rec = self.rec, None
        for _ in range(min(n, len(items))):
            it = items.pop(0)
            if it[0] == "op":
                _, eng, (name, a, kw), reads, writes, inc = it
                self.op(eng, lambda e: getattr(e, name)(*a, **kw), reads=reads, writes=writes, inc=inc)
            else:
                _, q, out, in_, slot, reads, writes, chain, kw = it
                self.dma(q, out, in_, slot, reads=reads, writes=writes, chain=chain, **kw)
        self.rec = rec

    def barrier(self, exclude=()):
        toks = []
        for e in self.E:
            assert not self.pending[e], e
            if self.cnt[e]:
                toks.append((self.sem[e], self.cnt[e], "x"))
        keep = [s for s in self.dma_sems if s in exclude]
        for s in self.dma_sems:
            if s.dcnt and s not in exclude:
                toks.append((s.sem, s.dcnt, "dma"))
        for e in self.E:
            self._wait(e, [t for t in toks if t[0] is not self.sem[e]])
        for s in self.dma_sems:
            if s not in exclude:
                self.sem_pool.append((s.sem, s.dcnt))
                s.sem = None
        self.dma_sems = keep


def rms_scale(cx, x_ap, ss_ap, rstd_ap, sq_ap, xb, ssb, sqb, n):
    cx.op("act", lambda e: e.activation(out=sq_ap, in_=x_ap, func=AF.Square, accum_out=ss_ap),
          reads=[xb], writes=[sqb, ssb])
    cx.op("act", lambda e: e.activation(out=rstd_ap, in_=ss_ap, func=AF.Ln, bias=EPS_AP[0], scale=1.0 / n),
          reads=[ssb], writes=[ssb])
    cx.op("act", lambda e: e.activation(out=rstd_ap, in_=rstd_ap, func=AF.Exp, scale=-0.5), reads=[ssb], writes=[ssb])


EPS_AP = [None]
DEBUG_NO_INTERLEAVE = False
PUMP_FROM = 0


class TB:
    def __init__(self, t, name):
        self.t = t
        self.b = Buf(name)

    def __getitem__(self, k):
        return self.t[k]


class Ring:
    def __init__(self, items):
        self.items = items
        self.i = 0

    def next(self):
        x = self.items[self.i % len(self.items)]
        self.i += 1
        return x


class Prog:
    def __init__(self, S, layers, first_reads_x=True):
        self.S = S
        self.layers = layers
        nc = bass.Bass("TRN2", target_bir_lowering=False)
        self.nc = nc
        self.cx = Ctx(nc)
        self.din = {}
        self.NT = S // P

    def inp(self, name, shape, dt=F32):
        t = self.nc.dram_tensor(name, list(shape), dt, kind="ExternalInput")
        self.din[name] = t
        return t

    def build(self):
        nc, cx, S = self.nc, self.cx, self.S
        self.x_in = self.inp("x", [S, D])
        self.ident_d = self.inp("ident", [P, P])
        self.w = {}
        for name, shape in WEIGHT_SHAPES.items():
            self.w[name] = self.inp(name, shape)
        self.out = nc.dram_tensor("out", [S, D], F32, kind="ExternalOutput")
        self.rot_d = self.inp("rot", [96, 96])
        self.tri_d = self.inp("tri", [P, P])
        self.cos_d = self.inp("cos_t", [96, S])
        self.sin_d = self.inp("sin_t", [96, S])
        self.qT_d = nc.dram_tensor("qT_s", [8, 96, S], BF16, kind="Internal")
        self.kT_d = nc.dram_tensor("kT_s", [8, 96, S], BF16, kind="Internal")
        self.v_d = nc.dram_tensor("v_s", [S, 512], BF16, kind="Internal")
        self.convT_d = nc.dram_tensor("convT_s", [512, S], BF16, kind="Internal")
        self.attnT_d = nc.dram_tensor("attnT_s", [512, S], BF16, kind="Internal")
        self.uT_d = nc.dram_tensor("uT_s", [D, 8, S // 8], BF16, kind="Internal")
        self.gT_d = nc.dram_tensor("gT_s", [D, S], BF16, kind="Internal")
        self.i2_d = self.inp("i2", [P, P])
        self.pmask_d = self.inp("pmask", [P, 2])
        self.bmask_d = self.inp("bmask", [P, P])
        self.xt_in = [Buf(f"xin{t}") for t in range(self.NT)]
        self.xt = [Buf(f"xo{t}") for t in range(self.NT)]
        self.first = True

        with ExitStack() as gs:
            self.ident = gs.enter_context(nc.sbuf_tensor("identb", [P, P], BF16))
            self.identf = gs.enter_context(nc.sbuf_tensor("identf", [P, P], F32))
            self.epsc = gs.enter_context(nc.sbuf_tensor("epsc", [P, 1], F32))
            self.b_const = Buf("const")
            cx.dma("sp", self.identf[:], self.ident_d.ap(), self.b_const, writes=[self.b_const])
            cx.op("dve", lambda e: e.tensor_copy(out=self.ident[:], in_=self.identf[:]),
                  reads=[self.b_const], writes=[self.b_const])
            cx.op("dve", lambda e: e.memset(self.epsc[:], EPS), writes=[self.b_const])
            EPS_AP[0] = self.epsc[:]
            phs = list(self.layers)
            k_ = 0
            while k_ < len(phs) - 1:
                if phs[k_][0] in ("mixc", "ssmc") and phs[k_ + 1][0] == "ffn":
                    phs[k_:k_ + 2] = [("ffn+", phs[k_ + 1][1], phs[k_])]
                k_ += 1
            for ph in phs:
                kind, l = ph[0], ph[1]
                if kind == "ffn":
                    self.ffn_phase(l)
                elif kind == "ffn+":
                    pk, pl = ph[2]
                    self.ffn_phase(l, pre=lambda: getattr(self, {"mixc": "mixer_c", "ssmc": "ssm_c"}[pk])(pl // 2))
                elif kind == "mixa":
                    self.mixer_a(l // 2)
                elif kind == "mixb":
                    self.mixer_b(l // 2)
                elif kind == "mixc":
                    self.mixer_c(l // 2)
                elif kind == "ssm":
                    self.ssm_phase(l // 2)
                elif kind in ("ssma", "ssmb", "ssmc"):
                    getattr(self, "ssm_" + kind[-1])(l // 2)
                elif kind == "ssmab":
                    self.ssm_b(l // 2, with_a=True)
                cx.barrier()
            cx.barrier()
        return nc

    def src_x(self, t):
        if self.first:
            return self.x_in.ap()[t * P:(t + 1) * P, :], self.xt_in[t]
        return self.out.ap()[t * P:(t + 1) * P, :], self.xt[t]

    def ffn_phase(self, l, pre=None):
        nc, cx, S = self.nc, self.cx, self.S
        T = 512
        NS = S // T
        KC = D // P
        CT = FH // P
        with ExitStack() as st:
            sb, ps = self.mk(st)
            wup = sb("wup", [P, KC, 2 * FH], BF16)
            wdn = sb("wdn", [P, CT, D], BF16)
            self.load_cast(wup, self.w["ffn_w_up"].ap()[l].rearrange("(kc p) n -> p kc n", p=P), KC)
            self.load_cast(wdn, self.w["ffn_w_down"].ap()[l].rearrange("(kt p) n -> p kt n", p=P), CT)
            if pre is not None:
                pre()
                cx.barrier(exclude=(wup.b, wdn.b))
            gbc = sb("gbc", [P, D])
            cw = sb("cw", [P, 3, 2 * CT])
            xn = [sb(f"xn{i}", [P, D]) for i in range(1)]
            xs = [[xn[0], xn[0], xn[0], xn[0]]] * 2
            xr = Ring([sb(f"xr{i}", [P, D]) for i in range(2)])
            hn = [sb(f"hn{i}", [P, D], BF16) for i in range(2)]
            sq = sb("sq", [P, D], BF16)
            ss = [sb(f"ss{i}", [P, 1]) for i in range(2)]
            hnTs = [sb(f"hnT{i}", [P, KC, T], BF16) for i in range(2)]
            act = sb("act", [P, CT, T], BF16)
            U = [[sb(f"U{i}{j}", [P, T + 2], BF16) for j in range(2)] for i in range(2)]
            A = [[sb(f"A{i}{j}", [P, T]) for j in range(2)] for i in range(2)]
            halo = sb("halo", [P, 2 * CT, 2])
            pT = [ps(f"pT{i}", [P, KC, P], BF16) for i in range(2)]
            pU = [[ps(f"pU{i}{j}", [P, T]) for j in range(2)] for i in range(2)]
            pD = Ring([ps(f"pD{i}", [P, 512]) for i in range(2)])

            cx.dma("sp", gbc[:], self.w["ffn_norm"].ap()[l:l + 1, :].partition_broadcast(P), gbc.b, writes=[gbc.b])
            for k in range(3):
                cx.dma("sp", cw[:, k, :], self.w["ffn_conv_w"].ap()[l, k].rearrange("(c p) -> p c", p=P), cw.b,
                       writes=[cw.b], chain=True, allow_slow_non_contiguous=True)
            cx.op("dve", lambda e: e.memset(halo[:], 0.0), writes=[halo.b])

            self.load_norm_T(0, xs[0], hn, sq, ss, pT, hnTs[0], gbc)
            for s in range(NS):
                xcur = xs[s % 2]
                hnT = hnTs[s % 2]
                for ct in range(CT + 1):
                    if s + 1 < NS and ct in (3, 9, 15):
                        nx = (s + 1, xs[(s + 1) % 2], hn, sq, ss, pT, hnTs[(s + 1) % 2], gbc)
                        if ct == 3:
                            self.load_norm_T(*nx, part="A", subs=(0, 1))
                        elif ct == 9:
                            self.load_norm_T(*nx, part="B", subs=(0, 1))
                            self.load_norm_T(*nx, part="A", subs=(2, 3))
                        else:
                            self.load_norm_T(*nx, part="B", subs=(2, 3))
                    if ct < CT:
                        bi = ct % 2
                        cs_ = [ct, ct + CT]
                        for hv in range(2):
                            c, pu = cs_[hv], pU[bi][hv]
                            for kc in range(KC):
                                cx.op("pe", lambda e: e.matmul(out=pu[:], lhsT=wup[:, kc, c * P:(c + 1) * P],
                                                               rhs=hnT[:, kc, :], start=(kc == 0), stop=(kc == KC - 1)),
                                      reads=[wup.b, hnT.b], writes=[pu.b], inc=(kc == KC - 1))
                        for hv in range(2):
                            c, u = cs_[hv], U[bi][hv]
                            cx.op("pool", lambda e: e.tensor_copy(out=u[:, 0:2], in_=halo[:, c, :]),
                                  reads=[halo.b], writes=[u.b])
                        for hv in range(2):
                            c, pu, u, a = cs_[hv], pU[bi][hv], U[bi][hv], A[bi][hv]
                            cx.op("act", lambda e: e.copy(out=u[:, 2:T + 2], in_=pu[:]), reads=[pu.b], writes=[u.b])
                            cx.op("act", lambda e: e.activation(out=a[:], in_=pu[:], func=AF.Copy, scale=cw[:, 2, c:c + 1]),
                                  reads=[pu.b, cw.b], writes=[a.b])
                        for hv in range(2):
                            c, u = cs_[hv], U[bi][hv]
                            cx.op("pool", lambda e: e.tensor_copy(out=halo[:, c, :], in_=u[:, T:T + 2]),
                                  reads=[u.b], writes=[halo.b])
                        for (k, off) in ((1, 1), (0, 0)):
                            for hv in range(2):
                                c, u, a = cs_[hv], U[bi][hv], A[bi][hv]
                                cx.op("dve", lambda e: e.scalar_tensor_tensor(out=a[:], in0=u[:, off:T + off],
                                                                              scalar=cw[:, k, c:c + 1], in1=a[:],
                                                                              op0=ALU.mult, op1=ALU.add),
                                      reads=[u.b, a.b, cw.b], writes=[a.b])
                    if ct >= 1:
                        pc = ct - 1
                        ag, av = A[pc % 2][0], A[pc % 2][1]
                        cx.op("act", lambda e: e.activation(out=ag[:], in_=ag[:], func=AF.Silu), reads=[ag.b], writes=[ag.b])
                        cx.op("pool", lambda e: e.tensor_tensor(out=act[:, pc, :], in0=ag[:], in1=av[:], op=ALU.mult),
                              reads=[ag.b, av.b], writes=[act.b])
                for i in range(4):
                    t = s * 4 + i
                    x = xr.next()
                    src, sbuf = self.src_x(t)
                    cx.dma("sp", x[:], src, x.b, reads=[sbuf], writes=[x.b])
                    for h in range(2):
                        pd = pD.next()
                        for kt in range(CT):
                            cx.op("pe", lambda e: e.matmul(out=pd[:], lhsT=act[:, kt, i * P:(i + 1) * P],
                                                           rhs=wdn[:, kt, h * 512:(h + 1) * 512],
                                                           start=(kt == 0), stop=(kt == CT - 1)),
                                  reads=[act.b, wdn.b], writes=[pd.b], inc=(kt == CT - 1))
                        cx.op("dve", lambda e: e.tensor_tensor(out=x[:, h * 512:(h + 1) * 512],
                                                               in0=x[:, h * 512:(h + 1) * 512], in1=pd[:], op=ALU.add),
                              reads=[pd.b, x.b], writes=[x.b])
                    cx.dma("sp", self.out.ap()[t * P:(t + 1) * P, :], x[:], x.b, reads=[x.b], writes=[self.xt[t]])
        self.first = False

    def mk(self, st):
        nc = self.nc
        self.uid = getattr(self, "uid", 0) + 1
        pre = f"u{self.uid}_"
        sb = lambda name, shape, dt=F32: TB(st.enter_context(nc.sbuf_tensor(pre + name, list(shape), dt)), pre + name)
        ps = lambda name, shape, dt=F32: TB(st.enter_context(nc.psum_tensor(pre + name, list(shape), dt)), pre + name)
        return sb, ps

    def load_cast(self, dst, src_ap, nk):
        for k in range(nk):
            self.cx.dma("pool", dst[:, k, :], src_ap[:, k, :], dst.b, writes=[dst.b], chain=True,
                        max_dma_last_dim=4096)

    def load_norm_T(self, s, xs, hn, sq, ss, pT, hnT, gbc, T=512, KC=8, part="AB", subs=None):
        cx = self.cx
        nsub = T // P
        for i in (range(nsub) if subs is None else subs):
            t = s * nsub + i
            x = xs[i]
            j = i % 2
            hj = hn[i % len(hn)]
            if "A" in part:
                src, sbuf = self.src_x(t)
                cx.dma("sp", x[:], src, x.b, reads=[sbuf], writes=[x.b])
                rms_scale(cx, x[:], ss[j][:], ss[j][:], sq[:], x.b, ss[j].b, sq.b, D)
                cx.op("dve", lambda e: e.scalar_tensor_tensor(out=hj[:], in0=x[:], scalar=ss[j][:],
                                                              in1=gbc[:], op0=ALU.mult, op1=ALU.mult),
                      reads=[x.b, ss[j].b, gbc.b], writes=[hj.b])
            if "B" in part:
                for kc in range(KC):
                    cx.op("pe", lambda e: e.transpose(out=pT[j][:, kc, :], in_=hj[:, kc * P:(kc + 1) * P],
                                                      identity=self.ident[:]),
                          reads=[hj.b, self.b_const], writes=[pT[j].b], inc=(kc == KC - 1))
                cx.op("act", lambda e: e.copy(out=hnT[:, :, i * P:(i + 1) * P], in_=pT[j][:]),
                      reads=[pT[j].b], writes=[hnT.b])

    def mixer_a(self, li):
        nc, cx, S = self.nc, self.cx, self.S
        T = 512
        NS = S // T
        KC = 8
        W = self.w
        with ExitStack() as st:
            sb, ps = self.mk(st)
            win = sb("win", [P, KC, 2080], BF16)
            wuq = sb("wuq", [P, 2, 768], BF16)
            wukv = sb("wukv", [P, 2, 1024], BF16)
            wkr = sb("wkr", [P, KC, 96], BF16)
            wkn = sb("wkn", [P, 2, 8, 96], BF16)
            gbc = sb("gbc", [P, D])
            cqg = sb("cqg", [P, 2])
            ckvg = sb("ckvg", [P, 2])
            qg = sb("qg", [96, 1])
            kg = sb("kg", [96, 1])
            scw = sb("scw", [P, 3, 4])
            ones = sb("ones", [P, P], BF16)
            rotf = sb("rotf", [96, 96])
            rot = sb("rot", [96, 96], BF16)
            cosb = [sb(f"cos{i}", [96, T]) for i in range(2)]
            sinb = [sb(f"sin{i}", [96, T]) for i in range(2)]
            xs = [sb(f"xs{i}", [P, D]) for i in range(4)]
            hn = [sb(f"hn{i}", [P, D], BF16) for i in range(2)]
            sq = sb("sq", [P, D], BF16)
            ss = [sb(f"ss{i}", [P, 1]) for i in range(2)]
            hnT = sb("hnT", [P, KC, T], BF16)
            cn = [sb(f"cn{i}", [P, 2, T], BF16) for i in range(2)]
            sqb = Ring([sb(f"sqb{i}", [P, T], BF16) for i in range(3)])
            rs = Ring([sb(f"rs{i}", [P, T]) for i in range(3)])
            qn = Ring([sb(f"qn{i}", [96, T], BF16) for i in range(3)])
            t1 = Ring([sb(f"t1{i}", [96, T]) for i in range(2)])
            t2 = Ring([sb(f"t2{i}", [96, T]) for i in range(2)])
            qf = Ring([sb(f"qf{i}", [96, T], BF16) for i in range(3)])
            vt = Ring([sb(f"vt{i}", [P, 512], BF16) for i in range(2)])
            ci = Ring([sb(f"ci{i}", [P, T]) for i in range(2)])
            GC = [sb(f"GC{i}", [P, T + 2]) for i in range(4)]
            ca = Ring([sb(f"ca{i}", [P, T]) for i in range(2)])
            cv = Ring([sb(f"cv{i}", [P, T], BF16) for i in range(2)])
            pT = [ps(f"pT{i}", [P, KC, P], BF16) for i in range(2)]
            pM = Ring([ps(f"pM{i}", [P, T]) for i in range(3)])
            pS = Ring([ps(f"pS{i}", [P, T]) for i in range(2)])
            pR = Ring([ps(f"pR{i}", [P, T]) for i in range(1)])

            cx.dma("sp", gbc[:], W["attn_norm"].ap()[li:li + 1, :].partition_broadcast(P), gbc.b, writes=[gbc.b])
            cx.dma("sp", cqg[:], W["cq_norm"].ap()[li].rearrange("(c p) -> p c", p=P), cqg.b, writes=[cqg.b],
                   allow_slow_non_contiguous=True)
            cx.dma("sp", ckvg[:], W["ckv_norm"].ap()[li].rearrange("(c p) -> p c", p=P), ckvg.b, writes=[ckvg.b],
                   allow_slow_non_contiguous=True)
            cx.dma("sp", qg[:], W["q_gain"].ap()[li].rearrange("(p o) -> p o", o=1), qg.b, writes=[qg.b])
            cx.dma("sp", kg[:], W["k_gain"].ap()[li].rearrange("(p o) -> p o", o=1), kg.b, writes=[kg.b])
            for k in range(3):
                cx.dma("sp", scw[:, k, :], W["sconv_w"].ap()[li, k].rearrange("(c p) -> p c", p=P), scw.b,
                       writes=[scw.b], chain=True, allow_slow_non_contiguous=True)
            cx.dma("sp", rotf[:], self.rot_d.ap(), rotf.b, writes=[rotf.b])
            cx.op("dve", lambda e: e.tensor_copy(out=rot[:], in_=rotf[:]), reads=[rotf.b], writes=[rot.b])
            cx.op("dve", lambda e: e.memset(ones[:], 1.0), writes=[ones.b])
            self.load_cast(win, W["mix_w_in"].ap()[li].rearrange("(k p) n -> p k n", p=P), KC)
            self.load_cast(wuq, W["w_uq"].ap()[li].rearrange("(k p) n -> p k n", p=P), 2)
            self.load_cast(wukv, W["w_ukv"].ap()[li].rearrange("(k p) n -> p k n", p=P), 2)
            cx.op("pool", lambda e: e.memset(wkr[:], 0.0), writes=[wkr.b])
            cx.op("pool", lambda e: e.memset(wkn[:], 0.0), writes=[wkn.b])
            cx.op("pool", lambda e: e.tensor_copy(out=wkr[:, :, 64:96], in_=win[:, :, 512:544]),
                  reads=[win.b], writes=[wkr.b])
            for kc in range(2):
                cx.op("pool", lambda e: e.tensor_copy(
                    out=wkn[:, kc, :, 0:64],
                    in_=wukv[:, kc, :].rearrange("p (h c) -> p h c", c=128)[:, :, 0:64]),
                      reads=[wukv.b], writes=[wkn.b])
            for c in range(4):
                cx.op("dve", lambda e: e.memset(GC[c][:, 0:2], 0.0), writes=[GC[c].b])

            def lat_norm(col0, gcol, dst):
                pp = []
                sqs = []
                for c2 in range(2):
                    pm = pM.next()
                    for kc in range(KC):
                        cx.op("pe", lambda e: e.matmul(out=pm[:], lhsT=win[:, kc, col0 + c2 * P: col0 + (c2 + 1) * P],
                                                       rhs=hnT[:, kc, :], start=(kc == 0), stop=(kc == KC - 1)),
                              reads=[win.b, hnT.b], writes=[pm.b], inc=(kc == KC - 1))
                    q2 = sqb.next()
                    cx.op("act", lambda e: e.activation(out=q2[:], in_=pm[:], func=AF.Square),
                          reads=[pm.b], writes=[q2.b])
                    pp.append(pm)
                    sqs.append(q2)
                p1 = pS.next()
                for c2 in range(2):
                    cx.op("pe", lambda e: e.matmul(out=p1[:], lhsT=ones[:], rhs=sqs[c2][:], start=(c2 == 0),
                                                   stop=(c2 == 1)),
                          reads=[ones.b, sqs[c2].b], writes=[p1.b], inc=(c2 == 1))
                r = rs.next()
                cx.op("act", lambda e: e.activation(out=r[:], in_=p1[:], func=AF.Ln, bias=self.epsc[:],
                                                    scale=1.0 / 256),
                      reads=[p1.b, self.b_const], writes=[r.b])
                cx.op("act", lambda e: e.activation(out=r[:], in_=r[:], func=AF.Exp, scale=-0.5), reads=[r.b], writes=[r.b])
                for c2 in range(2):
                    cx.op("dve", lambda e: e.scalar_tensor_tensor(out=dst[:, c2, :], in0=pp[c2][:],
                                                                  scalar=gcol[:, c2:c2 + 1], in1=r[:],
                                                                  op0=ALU.mult, op1=ALU.mult),
                          reads=[pp[c2].b, gcol.b, r.b], writes=[dst.b])

            def head_stages(proj, gain, dst_ap, cs, sn):
                st_ = {}

                def A():
                    pm = proj()
                    q2 = sqb.next()
                    cx.op("act", lambda e: e.activation(out=q2[0:96, :], in_=pm[0:96, :], func=AF.Square),
                          reads=[pm.b], writes=[q2.b])
                    st_["pm"], st_["q2"] = pm, q2

                def B():
                    pm, q2 = st_["pm"], st_["q2"]
                    p1 = pS.next()
                    cx.op("pe", lambda e: e.matmul(out=p1[0:96, :], lhsT=ones[0:96, 0:96], rhs=q2[0:96, :],
                                                   start=True, stop=True),
                          reads=[ones.b, q2.b], writes=[p1.b])
                    r = rs.next()
                    cx.op("act", lambda e: e.activation(out=r[0:96, :], in_=p1[0:96, :], func=AF.Ln,
                                                        bias=self.epsc[0:96, :], scale=1.0 / 96),
                          reads=[p1.b, self.b_const], writes=[r.b])
                    cx.op("act", lambda e: e.activation(out=r[0:96, :], in_=r[0:96, :], func=AF.Exp, scale=-0.5),
                          reads=[r.b], writes=[r.b])
                    n = qn.next()
                    cx.op("dve", lambda e: e.scalar_tensor_tensor(out=n[:], in0=pm[0:96, :], scalar=gain[:],
                                                                  in1=r[0:96, :], op0=ALU.mult, op1=ALU.mult),
                          reads=[pm.b, gain.b, r.b], writes=[n.b])
                    st_["n"] = n

                def C():
                    n = st_["n"]
                    pr = pR.next()
                    cx.op("pe", lambda e: e.matmul(out=pr[0:96, :], lhsT=rot[:], rhs=n[:], start=True, stop=True),
                          reads=[rot.b, n.b], writes=[pr.b])
                    a1 = t1.next()
                    cx.op("pool", lambda e: e.tensor_tensor(out=a1[:], in0=n[:], in1=cs[:], op=ALU.mult),
                          reads=[n.b, cs.b], writes=[a1.b])
                    a2 = t2.next()
                    cx.op("dve", lambda e: e.tensor_tensor(out=a2[:], in0=pr[0:96, :], in1=sn[:], op=ALU.mult),
                          reads=[pr.b, sn.b], writes=[a2.b])
                    f = qf.next()
                    cx.op("pool", lambda e: e.tensor_tensor(out=f[:], in0=a1[:], in1=a2[:], op=ALU.add),
                          reads=[a1.b, a2.b], writes=[f.b])
                    cx.dma("sp", dst_ap, f[:], f.b, reads=[f.b])
                return A, B, C

            for s in range(NS):
                tsl = slice(s * T, (s + 1) * T)
                cs, sn = cosb[s % 2], sinb[s % 2]
                cx.dma("sp", cs[:], self.cos_d.ap()[:, tsl], cs.b, writes=[cs.b])
                cx.dma("sp", sn[:], self.sin_d.ap()[:, tsl], sn.b, writes=[sn.b])
                self.load_norm_T(s, xs, hn, sq, ss, pT, hnT, gbc)
                lat_norm(0, cqg, cn[0])
                lat_norm(256, ckvg, cn[1])
                heads = []
                for h in range(8):
                    def projq(h=h):
                        pm = pM.next()
                        for kc in range(2):
                            cx.op("pe", lambda e: e.matmul(out=pm[0:96, :], lhsT=wuq[:, kc, h * 96:(h + 1) * 96],
                                                           rhs=cn[0][:, kc, :], start=(kc == 0), stop=(kc == 1)),
                                  reads=[wuq.b, cn[0].b], writes=[pm.b], inc=(kc == 1))
                        return pm
                    heads.append(head_stages(projq, qg, self.qT_d.ap()[h, :, tsl], cs, sn))
                for h in range(8):
                    def projk(h=h):
                        pm = pM.next()
                        for kc in range(2):
                            cx.op("pe", lambda e: e.matmul(out=pm[0:96, :], lhsT=wkn[:, kc, h, :],
                                                           rhs=cn[1][:, kc, :], start=(kc == 0), stop=False),
                                  reads=[wkn.b, cn[1].b], writes=[pm.b], inc=False)
                        for kc in range(KC):
                            cx.op("pe", lambda e: e.matmul(out=pm[0:96, :], lhsT=wkr[:, kc, :], rhs=hnT[:, kc, :],
                                                           start=False, stop=(kc == KC - 1)),
                                  reads=[wkr.b, hnT.b], writes=[pm.b], inc=(kc == KC - 1))
                        return pm
                    heads.append(head_stages(projk, kg, self.kT_d.ap()[h, :, tsl], cs, sn))
                nh = len(heads)
                for i in range(nh + 2):
                    if i < nh:
                        heads[i][0]()
                    if 0 <= i - 1 < nh:
                        heads[i - 1][1]()
                    if 0 <= i - 2 < nh:
                        heads[i - 2][2]()
                for i in range(4):
                    pm = pM.next()
                    for kc in range(2):
                        cx.op("pe", lambda e: e.matmul(
                            out=pm[:], lhsT=cn[1][:, kc, i * P:(i + 1) * P],
                            rhs=wukv[:, kc, :].rearrange("p (h c) -> p h c", c=128)[:, :, 64:128],
                            start=(kc == 0), stop=(kc == 1)),
                              reads=[wukv.b, cn[1].b], writes=[pm.b], inc=(kc == 1))
                    v1 = vt.next()
                    cx.op("act", lambda e: e.copy(out=v1[:], in_=pm[:]), reads=[pm.b], writes=[v1.b])
                    t = s * 4 + i
                    cx.dma("sp", self.v_d.ap()[t * P:(t + 1) * P, :], v1[:], v1.b, reads=[v1.b])
                for c in range(4):
                    def proj(col0):
                        pm = pM.next()
                        for kc in range(KC):
                            cx.op("pe", lambda e: e.matmul(out=pm[:], lhsT=win[:, kc, col0 + c * P: col0 + (c + 1) * P],
                                                           rhs=hnT[:, kc, :], start=(kc == 0), stop=(kc == KC - 1)),
                                  reads=[win.b, hnT.b], writes=[pm.b], inc=(kc == KC - 1))
                        return pm
                    pci = proj(1568)
                    c1 = ci.next()
                    cx.op("act", lambda e: e.copy(out=c1[:], in_=pci[:]), reads=[pci.b], writes=[c1.b])
                    pgc = proj(1056)
                    g = GC[c]
                    cx.op("dve", lambda e: e.tensor_tensor(out=g[:, 2:T + 2], in0=pgc[:], in1=c1[:], op=ALU.mult),
                          reads=[pgc.b, c1.b], writes=[g.b])
                    a = ca.next()
                    cx.op("act", lambda e: e.activation(out=a[:], in_=g[:, 2:T + 2], func=AF.Copy,
                                                        scale=scw[:, 2, c:c + 1]),
                          reads=[g.b, scw.b], writes=[a.b])
                    cx.op("dve", lambda e: e.scalar_tensor_tensor(out=a[:], in0=g[:, 1:T + 1], scalar=scw[:, 1, c:c + 1],
                                                                  in1=a[:], op0=ALU.mult, op1=ALU.add),
                          reads=[g.b, a.b, scw.b], writes=[a.b])
                    cx.op("dve", lambda e: e.scalar_tensor_tensor(out=a[:], in0=g[:, 0:T], scalar=scw[:, 0, c:c + 1],
                                                                  in1=a[:], op0=ALU.mult, op1=ALU.add),
                          reads=[g.b, a.b, scw.b], writes=[a.b])
                    cx.op("pool", lambda e: e.tensor_copy(out=g[:, 0:2], in_=g[:, T:T + 2]), reads=[g.b], writes=[g.b])
                    pgb = proj(544)
                    o = cv.next()
                    cx.op("dve", lambda e: e.tensor_tensor(out=o[:], in0=pgb[:], in1=a[:], op=ALU.mult),
                          reads=[pgb.b, a.b], writes=[o.b])
                    cx.dma("sp", self.convT_d.ap()[c * P:(c + 1) * P, tsl], o[:], o.b, reads=[o.b])

    def mixer_b(self, li):
        nc, cx, S = self.nc, self.cx, self.S
        NT = S // P
        QB = 512
        NQ = S // QB
        scale = 96.0 ** -0.5
        with ExitStack() as st:
            sb, ps = self.mk(st)
            qT = [sb(f"qT{i}", [96, S], BF16) for i in range(2)]
            kT = [sb(f"kT{i}", [96, S], BF16) for i in range(2)]
            vv = [sb(f"vv{i}", [P, NT, 65], BF16) for i in range(2)]
            trif = sb("trif", [P, P])
            tri = sb("tri", [P, P], BF16)
            onesf = sb("onesf", [P, 64])
            pTb = Ring([sb(f"pTb{i}", [P, QB], BF16) for i in range(6)])
            rec = Ring([sb(f"rec{i}", [P, QB]) for i in range(2)])
            osb = Ring([sb(f"osb{i}", [64, QB]) for i in range(2)])
            ao = Ring([sb(f"ao{i}", [64, QB], BF16) for i in range(3)])
            pSc = Ring([ps(f"pSc{i}", [P, QB]) for i in range(5)])
            pO = Ring([ps(f"pO{i}", [P, QB]) for i in range(2)])
            pB = Ring([ps(f"pB{i}", [P, QB]) for i in range(1)])

            cx.dma("sp", trif[:], self.tri_d.ap(), trif.b, writes=[trif.b])
            cx.op("dve", lambda e: e.tensor_copy(out=tri[:], in_=trif[:]), reads=[trif.b], writes=[tri.b])
            cx.op("dve", lambda e: e.memset(onesf[:], 1.0), writes=[onesf.b])
            for i in range(2):
                cx.op("pool", lambda e: e.memset(vv[i][:, :, 64:65], 1.0), writes=[vv[i].b])

            for h in range(8):
                q, k, v = qT[h % 2], kT[h % 2], vv[h % 2]
                cx.dma("sp", q[:], self.qT_d.ap()[h], q.b, writes=[q.b])
                cx.dma("sp", k[:], self.kT_d.ap()[h], k.b, writes=[k.b])
                cx.dma("sp", v[:, :, 0:64],
                       self.v_d.ap().rearrange("(t p) (h c) -> p t h c", p=P, c=64)[:, :, h, :], v.b, writes=[v.b])
                items = [(qb, j) for qb in range(NQ) for j in range(4 * qb + 4)]
                LOOK = 3
                scs = {}
                pos = {}
                deferred = []

                def stage_qk(idx):
                    qb, j = items[idx]
                    c0 = P * max(j - 4 * qb, 0)
                    psc = pSc.next()
                    diag = (j - 4 * qb) >= 0
                    cx.op("pe", lambda e: e.matmul(out=psc[:, c0:QB], lhsT=k[:, j * P:(j + 1) * P],
                                                   rhs=q[:, qb * QB + c0:(qb + 1) * QB], start=True, stop=not diag),
                          reads=[k.b, q.b], writes=[psc.b], inc=not diag)
                    if diag:
                        cx.op("pe", lambda e: e.matmul(out=psc[:, c0:c0 + P], lhsT=self.ident[:], rhs=tri[:],
                                                       start=False, stop=True),
                              reads=[tri.b, self.b_const], writes=[psc.b])
                    scs[idx] = psc

                for idx in range(min(LOOK, len(items))):
                    stage_qk(idx)
                for idx, (qb, j) in enumerate(items):
                    if idx + LOOK < len(items):
                        stage_qk(idx + LOOK)
                    nj = 4 * qb + 4
                    r = j - 4 * qb
                    c0 = P * max(r, 0)
                    if j == 0:
                        pos[qb] = pO.next()
                    po = pos[qb]
                    psc = scs.pop(idx)
                    pt = pTb.next()
                    cx.op("act", lambda e: e.activation(out=pt[:, c0:QB], in_=psc[:, c0:QB], func=AF.Exp, scale=scale),
                          reads=[psc.b], writes=[pt.b])
                    cx.op("pe", lambda e: e.matmul(out=po[0:65, c0:QB], lhsT=v[:, j, :], rhs=pt[:, c0:QB],
                                                   start=(j == 0), stop=(j == nj - 1)),
                          reads=[v.b, pt.b], writes=[po.b], inc=(j == nj - 1))
                    if j == nj - 1:
                        def norm(po=po, qb=qb):
                            rc = rec.next()
                            cx.op("act", lambda e: e.activation(out=rc[64:65, :], in_=po[64:65, :], func=AF.Ln),
                                  reads=[po.b], writes=[rc.b])
                            cx.op("act", lambda e: e.activation(out=rc[64:65, :], in_=rc[64:65, :], func=AF.Exp, scale=-1.0),
                                  reads=[rc.b], writes=[rc.b])
                            o1 = osb.next()
                            cx.op("act", lambda e: e.copy(out=o1[:], in_=po[0:64, :]), reads=[po.b], writes=[o1.b])

                            def fin():
                                pb = pB.next()
                                cx.op("pe", lambda e: e.matmul(out=pb[0:64, :], lhsT=onesf[64:65, :], rhs=rc[64:65, :],
                                                               start=True, stop=True),
                                      reads=[onesf.b, rc.b], writes=[pb.b])
                                a = ao.next()
                                cx.op("dve", lambda e: e.tensor_tensor(out=a[:], in0=pb[0:64, :], in1=o1[:], op=ALU.mult),
                                      reads=[pb.b, o1.b], writes=[a.b])
                                cx.dma("sp", self.attnT_d.ap()[h * 64:(h + 1) * 64, qb * QB:(qb + 1) * QB], a[:], a.b,
                                       reads=[a.b])
                            return fin
                        deferred.append((idx + 3, norm()))
                    while deferred and deferred[0][0] <= idx:
                        deferred.pop(0)[1]()
                while deferred:
                    deferred.pop(0)[1]()

    def mixer_c(self, li):
        nc, cx, S = self.nc, self.cx, self.S
        T = 512
        NS = S // T
        with ExitStack() as st:
            sb, ps = self.mk(st)
            wout = sb("wout", [P, 8, D], BF16)
            cat = [sb(f"cat{i}", [P, 8, T], BF16) for i in range(2)]
            xs = [sb(f"xs{i}", [P, D]) for i in range(4)]
            pD = Ring([ps(f"pD{i}", [P, 512]) for i in range(4)])
            self.load_cast(wout, self.w["mix_w_out"].ap()[li].rearrange("(k p) n -> p k n", p=P), 8)
            for s in range(NS):
                tsl = slice(s * T, (s + 1) * T)
                c = cat[s % 2]
                cx.dma("sp", c[:, 0:4, :], self.attnT_d.ap()[:, tsl].rearrange("(k p) t -> p k t", p=P), c.b,
                       writes=[c.b])
                cx.dma("sp", c[:, 4:8, :], self.convT_d.ap()[:, tsl].rearrange("(k p) t -> p k t", p=P), c.b,
                       writes=[c.b], chain=True)
                for i in range(4):
                    t = s * 4 + i
                    src, sbuf = self.src_x(t)
                    x = xs[i]
                    cx.dma("sp", x[:], src, x.b, reads=[sbuf], writes=[x.b])
                    for hh in range(2):
                        pd = pD.next()
                        for kc in range(8):
                            cx.op("pe", lambda e: e.matmul(out=pd[:], lhsT=c[:, kc, i * P:(i + 1) * P],
                                                           rhs=wout[:, kc, hh * 512:(hh + 1) * 512],
                                                           start=(kc == 0), stop=(kc == 7)),
                                  reads=[c.b, wout.b], writes=[pd.b], inc=(kc == 7))
                        cx.op("dve", lambda e: e.tensor_tensor(out=x[:, hh * 512:(hh + 1) * 512],
                                                               in0=x[:, hh * 512:(hh + 1) * 512], in1=pd[:],
                                                               op=ALU.add),
                              reads=[pd.b, x.b], writes=[x.b])
                    cx.dma("sp", self.out.ap()[t * P:(t + 1) * P, :], x[:], x.b, reads=[x.b], writes=[self.xt[t]])
        self.first = False

    def ssm_phase(self, li):
        self.ssm_a(li)
        self.cx.barrier()
        self.ssm_b(li)
        self.cx.barrier()
        self.ssm_c(li)

    def ssm_a(self, li):
        with ExitStack() as st:
            sb, ps = self.mk(st)
            self._ssm_a_body(li, sb, ps, None)

    def _ssm_a_body(self, li, sb, ps, pump):
        nc, cx, S = self.nc, self.cx, self.S
        T = 512
        NS = S // T
        KC = 8
        win = sb("win", [P, KC, D], BF16)
        gbc = sb("gbc", [P, D])
        xs = [sb(f"xs{i}", [P, D]) for i in range(4)]
        hn = [sb(f"hn{i}", [P, D], BF16) for i in range(2)]
        sq = sb("sq", [P, D], BF16)
        ss = [sb(f"ss{i}", [P, 1]) for i in range(2)]
        hnT = sb("hnT", [P, KC, T], BF16)
        ub = Ring([sb(f"ub{i}", [P, T], BF16) for i in range(3)])
        pT = [ps(f"pT{i}", [P, KC, P], BF16) for i in range(2)]
        pM = Ring([ps(f"pM{i}", [P, T]) for i in range(3)])
        cx.dma("sp", gbc[:], self.w["ssm_norm"].ap()[li:li + 1, :].partition_broadcast(P), gbc.b, writes=[gbc.b])
        self.load_cast(win, self.w["ssm_w_in"].ap()[li].rearrange("(k p) n -> p k n", p=P), KC)
        for s in range(NS):
            self.load_norm_T(s, xs, hn, sq, ss, pT, hnT, gbc)
            for j in range(8):
                pm = pM.next()
                for kc in range(KC):
                    cx.op("pe", lambda e: e.matmul(out=pm[:], lhsT=win[:, kc, j * P:(j + 1) * P], rhs=hnT[:, kc, :],
                                                   start=(kc == 0), stop=(kc == KC - 1)),
                          reads=[win.b, hnT.b], writes=[pm.b], inc=(kc == KC - 1))
                u = ub.next()
                cx.op("act", lambda e: e.copy(out=u[:].rearrange("p (l k) -> p l k", l=8),
                                              in_=pm[:].rearrange("p (k l) -> p l k", l=8)),
                      reads=[pm.b], writes=[u.b])
                cx.dma("sp", self.uT_d.ap()[j * P:(j + 1) * P, :, s * (T // 8):(s + 1) * (T // 8)],
                       u[:].rearrange("p (l k) -> p l k", l=8), u.b, reads=[u.b])
                if pump is not None and s >= PUMP_FROM:
                    pump()

    def ssm_b(self, li, with_a=False):
        nc, cx, S = self.nc, self.cx, self.S
        L = 8
        NK = S // L
        J = int(round(math.log2(NK)))
        assert 2 ** J == NK and NK <= 512
        W = self.w
        TWO_PI = 2.0 * math.pi
        with ExitStack() as st:
            sb, ps = self.mk(st)
            WB = [sb(f"WB{e}", [P, 8, 8, P], BF16) for e in range(2)]
            WC = sb("WC", [P, 64, 8, 32], BF16)
            Kb = sb("Kb", [P, 8, 8, P], BF16)
            a1 = sb("a1", [P, J, 64])
            a2 = sb("a2", [P, J, 64])
            I2 = sb("I2", [P, P])
            pmask = sb("pmask", [P, 2])
            bmask = sb("bmask", [P, P])
            dcol = sb("dcol", [P, 8])
            cx.dma("sp", I2[:], self.i2_d.ap(), I2.b, writes=[I2.b])
            cx.dma("sp", pmask[:], self.pmask_d.ap(), pmask.b, writes=[pmask.b])
            cx.dma("sp", bmask[:], self.bmask_d.ap(), bmask.b, writes=[bmask.b])
            cx.dma("sp", dcol[:], W["d_skip"].ap()[li].rearrange("(j q) -> q j", q=P), dcol.b, writes=[dcol.b],
                   allow_slow_non_contiguous=True)

            with ExitStack() as s2:
                sb2, ps2 = self.mk(s2)
                if with_a:
                    cx.rec = []
                cx.op("pool", lambda e: e.memset(WC[:], 0.0), writes=[WC.b])
                sm = lambda n: sb2(n, [P, 64])
                lr, lim, lsb, dt, mag, ang = sm("lr"), sm("lim"), sm("lsb"), sm("dt"), sm("mag"), sm("ang")
                yv, nf, fr, gt, sn, cs_, den, nr, t1, t2 = (sm("yv"), sm("nf"), sm("fr"), sm("gt"), sm("sn"), sm("cs"),
                                                           sm("den"), sm("nr"), sm("t1"), sm("t2"))
                ni = sb2("ni", [P, 64], I32)
                zr, zi = sm("zr"), sm("zi")
                PWr = sb2("PWr", [P, 9, 64])
                PWi = sb2("PWi", [P, 9, 64])
                DBr = sb2("DBr", [P, J, 64])
                DBi = sb2("DBi", [P, J, 64])
                big = lambda n, dt_=F32: sb2(n, [P, 1024], dt_)
                Bre, Bim = big("Bre"), big("Bim")
                Xr = [big("Xr0"), big("Xr1")]
                Xi = [big("Xi0"), big("Xi1")]
                Yr, Yi = Xr, Xi
                m1, m2 = big("m1"), big("m2")
                stk = big("stk")
                Bs = big("Bs", BF16)
                CLs = big("CLs", BF16)
                Cn = [sb2(f"Cn{i}", [P, 8, P]) for i in range(2)]
                ktmp = sb2("ktmp", [P, P])
                pw = Ring([ps2(f"pw{i}", [P, 4, P]) for i in range(3)])

                for hf in range(2):
                    hs = slice(hf * 64, hf * 64 + 64)
                    cx.dma("sp", lr[hs, :], W["lambda_re"].ap()[li].rearrange("g p -> p g"), lr.b, writes=[lr.b],
                           chain=True, allow_slow_non_contiguous=True)
                    cx.dma("sp", lim[hs, :], W["lambda_im"].ap()[li].rearrange("g p -> p g"), lim.b, writes=[lim.b],
                           chain=True, allow_slow_non_contiguous=True)
                    cx.dma("sp", Bre[hs, :].rearrange("p (g c) -> p g c", c=16),
                           W["b_re"].ap()[li].rearrange("g p c -> p g c"), Bre.b, writes=[Bre.b], chain=True)
                    cx.dma("sp", Bim[hs, :].rearrange("p (g c) -> p g c", c=16),
                           W["b_im"].ap()[li].rearrange("g p c -> p g c"), Bim.b, writes=[Bim.b], chain=True)
                    cx.dma("sp", Cn[0][:, :, hs], W["c_re"].ap()[li].rearrange("(t g) c p -> (g c) t p", g=8), Cn[0].b,
                           writes=[Cn[0].b], chain=True)
                    cx.dma("sp", Cn[1][:, :, hs], W["c_im"].ap()[li].rearrange("(t g) c p -> (g c) t p", g=8), Cn[1].b,
                           writes=[Cn[1].b], chain=True)
                cx.dma("sp", lsb[:], W["log_step"].ap()[li:li + 1, :].partition_broadcast(P), lsb.b, writes=[lsb.b])

                def V(fn, reads, writes, eng="dve"):
                    cx.op(eng, fn, reads=[r.b for r in reads], writes=[w_.b for w_ in writes])

                V(lambda e: e.activation(out=dt[:], in_=lsb[:], func=AF.Exp), [lsb], [dt], "act")
                V(lambda e: e.tensor_tensor(out=t1[:], in0=lr[:], in1=dt[:], op=ALU.mult), [lr, dt], [t1])
                V(lambda e: e.activation(out=mag[:], in_=t1[:], func=AF.Exp), [t1], [mag], "act")
                V(lambda e: e.tensor_tensor(out=ang[:], in0=lim[:], in1=dt[:], op=ALU.mult), [lim, dt], [ang])

                def sin_of(dst, shift):
                    V(lambda e: e.tensor_scalar(out=yv[:], in0=ang[:], scalar1=1.0 / TWO_PI, scalar2=shift,
                                                op0=ALU.mult, op1=ALU.add), [ang], [yv])
                    V(lambda e: e.tensor_copy(out=ni[:], in_=yv[:]), [yv], [ni])
                    V(lambda e: e.tensor_copy(out=nf[:], in_=ni[:]), [ni], [nf])
                    V(lambda e: e.tensor_tensor(out=fr[:], in0=yv[:], in1=nf[:], op=ALU.subtract), [yv, nf], [fr])
                    V(lambda e: e.tensor_scalar(out=gt[:], in0=fr[:], scalar1=0.5, scalar2=None, op0=ALU.is_gt),
                      [fr], [gt])
                    V(lambda e: e.tensor_tensor(out=fr[:], in0=fr[:], in1=gt[:], op=ALU.subtract), [fr, gt], [fr])
                    V(lambda e: e.tensor_scalar(out=gt[:], in0=fr[:], scalar1=-0.5, scalar2=None, op0=ALU.is_lt),
                      [fr], [gt])
                    V(lambda e: e.tensor_tensor(out=fr[:], in0=fr[:], in1=gt[:], op=ALU.add), [fr, gt], [fr])
                    V(lambda e: e.activation(out=dst[:], in_=fr[:], func=AF.Sin, scale=TWO_PI * (1.0 - 1e-6)),
                      [fr], [dst], "act")

                sin_of(sn, 0.0)
                sin_of(cs_, 0.25)
                ar, ai = PWr[:, 1, :], PWi[:, 1, :]
                V(lambda e: e.memset(PWr[:, 0, :], 1.0), [], [PWr])
                V(lambda e: e.memset(PWi[:, 0, :], 0.0), [], [PWi])
                V(lambda e: e.tensor_tensor(out=ar, in0=mag[:], in1=cs_[:], op=ALU.mult), [mag, cs_], [PWr])
                V(lambda e: e.tensor_tensor(out=ai, in0=mag[:], in1=sn[:], op=ALU.mult), [mag, sn], [PWi])
                V(lambda e: e.tensor_scalar(out=nr[:], in0=ar, scalar1=-1.0, scalar2=None, op0=ALU.add), [PWr], [nr])
                V(lambda e: e.tensor_tensor(out=t1[:], in0=lr[:], in1=lr[:], op=ALU.mult), [lr], [t1])
                V(lambda e: e.tensor_tensor(out=t2[:], in0=lim[:], in1=lim[:], op=ALU.mult), [lim], [t2])
                V(lambda e: e.tensor_tensor(out=den[:], in0=t1[:], in1=t2[:], op=ALU.add), [t1, t2], [den])
                V(lambda e: e.reciprocal(out=den[:], in_=den[:]), [den], [den])
                V(lambda e: e.tensor_tensor(out=t1[:], in0=nr[:], in1=lr[:], op=ALU.mult), [nr, lr], [t1])
                V(lambda e: e.tensor_tensor(out=t2[:], in0=ai, in1=lim[:], op=ALU.mult), [PWi, lim], [t2])
                V(lambda e: e.tensor_tensor(out=t1[:], in0=t1[:], in1=t2[:], op=ALU.add), [t1, t2], [t1])
                V(lambda e: e.tensor_tensor(out=zr[:], in0=t1[:], in1=den[:], op=ALU.mult), [t1, den], [zr])
                V(lambda e: e.tensor_tensor(out=t1[:], in0=ai, in1=lr[:], op=ALU.mult), [PWi, lr], [t1])
                V(lambda e: e.tensor_tensor(out=t2[:], in0=nr[:], in1=lim[:], op=ALU.mult), [nr, lim], [t2])
                V(lambda e: e.tensor_tensor(out=t1[:], in0=t1[:], in1=t2[:], op=ALU.subtract), [t1, t2], [t1])
                V(lambda e: e.tensor_tensor(out=zi[:], in0=t1[:], in1=den[:], op=ALU.mult), [t1, den], [zi])
                for d in range(2, 9):
                    pr_, pi_ = PWr[:, d - 1, :], PWi[:, d - 1, :]
                    V(lambda e: e.tensor_tensor(out=t1[:], in0=pr_, in1=ar, op=ALU.mult), [PWr], [t1])
                    V(lambda e: e.tensor_tensor(out=t2[:], in0=pi_, in1=ai, op=ALU.mult), [PWi], [t2])
                    V(lambda e: e.tensor_tensor(out=PWr[:, d, :], in0=t1[:], in1=t2[:], op=ALU.subtract), [t1, t2], [PWr])
                    V(lambda e: e.tensor_tensor(out=t1[:], in0=pr_, in1=ai, op=ALU.mult), [PWr, PWi], [t1])
                    V(lambda e: e.tensor_tensor(out=t2[:], in0=pi_, in1=ar, op=ALU.mult), [PWr, PWi], [t2])
                    V(lambda e: e.tensor_tensor(out=PWi[:, d, :], in0=t1[:], in1=t2[:], op=ALU.add), [t1, t2], [PWi])
                V(lambda e: e.tensor_copy(out=DBr[:, 0, :], in_=PWr[:, 8, :]), [PWr], [DBr])
                V(lambda e: e.tensor_copy(out=DBi[:, 0, :], in_=PWi[:, 8, :]), [PWi], [DBi])
                for j in range(1, J):
                    r_, i_ = DBr[:, j - 1, :], DBi[:, j - 1, :]
                    V(lambda e: e.tensor_tensor(out=t1[:], in0=r_, in1=r_, op=ALU.mult), [DBr], [t1])
                    V(lambda e: e.tensor_tensor(out=t2[:], in0=i_, in1=i_, op=ALU.mult), [DBi], [t2])
                    V(lambda e: e.tensor_tensor(out=DBr[:, j, :], in0=t1[:], in1=t2[:], op=ALU.subtract), [t1, t2], [DBr])
                    V(lambda e: e.tensor_tensor(out=t1[:], in0=r_, in1=i_, op=ALU.mult), [DBr, DBi], [t1])
                    V(lambda e: e.tensor_scalar(out=DBi[:, j, :], in0=t1[:], scalar1=2.0, scalar2=None, op0=ALU.mult),
                      [t1], [DBi])
                V(lambda e: e.tensor_copy(out=a1[0:64], in_=DBr[0:64]), [DBr], [a1])
                V(lambda e: e.tensor_scalar(out=a1[64:128], in0=DBi[64:128], scalar1=-1.0, scalar2=None, op0=ALU.mult),
                  [DBi], [a1])
                V(lambda e: e.tensor_copy(out=a2[0:64], in_=DBi[0:64]), [DBi], [a2])
                V(lambda e: e.tensor_copy(out=a2[64:128], in_=DBr[64:128]), [DBr], [a2])

                def cmul(orr, oii, xr, xi, br_ap, bi_ap, rb):
                    v3 = lambda t: t[:].rearrange("p (g c) -> p g c", c=16)
                    bb = lambda a: a.unsqueeze(2).to_broadcast([P, 64, 16])
                    V(lambda e: e.tensor_tensor(out=v3(m1), in0=v3(xr), in1=bb(br_ap), op=ALU.mult), [xr] + rb, [m1])
                    V(lambda e: e.tensor_tensor(out=v3(m2), in0=v3(xi), in1=bb(bi_ap), op=ALU.mult), [xi] + rb, [m2], "pool")
                    V(lambda e: e.tensor_tensor(out=orr[:], in0=m1[:], in1=m2[:], op=ALU.subtract), [m1, m2], [orr])
                    V(lambda e: e.tensor_tensor(out=v3(m1), in0=v3(xr), in1=bb(bi_ap), op=ALU.mult), [xr] + rb, [m1])
                    V(lambda e: e.tensor_tensor(out=v3(m2), in0=v3(xi), in1=bb(br_ap), op=ALU.mult), [xi] + rb, [m2], "pool")
                    V(lambda e: e.tensor_tensor(out=oii[:], in0=m1[:], in1=m2[:], op=ALU.add), [m1, m2], [oii], "pool")

                cmul(Xr[0], Xi[0], Bre, Bim, zr[:], zi[:], [zr, zi])
                for e_ in range(8):
                    cr, ci_ = Xr[e_ % 2], Xi[e_ % 2]
                    V(lambda e: e.tensor_copy(out=stk[0:64, :], in_=cr[0:64, :]), [cr], [stk])
                    V(lambda e: e.tensor_copy(out=stk[64:128, :], in_=ci_[64:128, :]), [ci_], [stk], "pool")
                    if e_ == 0:
                        V(lambda e: e.tensor_copy(out=Bs[:], in_=stk[:]), [stk], [Bs])
                    tau = 7 - e_
                    for j0 in (0, 4):
                        p_ = pw.next()
                        for jj in range(4):
                            j = j0 + jj
                            cx.op("pe", lambda e: e.transpose(out=p_[:, jj, :], in_=stk[:, j * P:(j + 1) * P],
                                                              identity=self.identf[:]),
                                  reads=[stk.b, self.b_const], writes=[p_.b], inc=(jj == 3))
                        for eo in range(2):
                            V(lambda e: e.tensor_scalar(out=WB[eo][:, j0:j0 + 4, tau, :], in0=p_[:], scalar1=pmask[:, eo:eo + 1],
                                                        scalar2=None, op0=ALU.mult), [p_, pmask], [WB[eo]])
                    if e_ < 7:
                        cmul(Xr[(e_ + 1) % 2], Xi[(e_ + 1) % 2], cr, ci_, ar, ai, [PWr, PWi])
                for ri in range(2):
                    dst = (Yr[0], Yi[0])[ri]
                    for j0 in (0, 4):
                        p_ = pw.next()
                        for jj in range(4):
                            j = j0 + jj
                            cx.op("pe", lambda e: e.transpose(out=p_[:, jj, :], in_=Cn[ri][:, j, :], identity=self.identf[:]),
                                  reads=[Cn[ri].b, self.b_const], writes=[p_.b], inc=(jj == 3))
                        V(lambda e: e.copy(out=dst[:, j0 * P:(j0 + 4) * P], in_=p_[:].rearrange("p a b -> p (a b)")),
                          [p_], [dst], "act")
                wc6 = WC[:].rearrange("p (pr e) t (e2 c) -> p pr e t e2 c", e=2, e2=2)
                for d in range(9):
                    cr, ci_ = Yr[d % 2], Yi[d % 2]
                    if d <= 7:
                        V(lambda e: e.tensor_copy(out=CLs[0:64, :], in_=cr[0:64, :]), [cr], [CLs])
                        V(lambda e: e.tensor_scalar(out=CLs[64:128, :], in0=ci_[64:128, :], scalar1=-1.0, scalar2=None,
                                                    op0=ALU.mult), [ci_], [CLs], "pool")
                        for j0 in (0, 4):
                            p_ = pw.next()
                            for jj in range(4):
                                j = j0 + jj
                                cx.op("pe", lambda e: e.matmul(out=p_[:, jj, :], lhsT=Bs[:, j * P:(j + 1) * P],
                                                               rhs=CLs[:, j * P:(j + 1) * P], start=True, stop=True),
                                      reads=[Bs.b, CLs.b], writes=[p_.b], inc=(jj == 3))
                            for jj in range(4):
                                j = j0 + jj
                                if d == 0:
                                    V(lambda e: e.tensor_tensor(out=ktmp[:], in0=p_[:, jj, :], in1=bmask[:], op=ALU.mult),
                                      [p_, bmask], [ktmp])
                                    V(lambda e: e.scalar_tensor_tensor(out=Kb[:, j, 0, :], in0=self.identf[:],
                                                                       scalar=dcol[:, j:j + 1], in1=ktmp[:],
                                                                       op0=ALU.mult, op1=ALU.add),
                                      [ktmp, dcol], [Kb])
                                else:
                                    V(lambda e: e.tensor_tensor(out=Kb[:, j, d, :], in0=p_[:, jj, :], in1=bmask[:],
                                                                op=ALU.mult), [p_, bmask], [Kb])
                    if d >= 1:
                        t = d - 1
                        c4r = cr[:].rearrange("p (pr e c) -> p pr e c", e=2, c=16)
                        c4i = ci_[:].rearrange("p (pr e c) -> p pr e c", e=2, c=16)
                        for e2 in range(2):
                            V(lambda e: e.tensor_copy(out=wc6[0:64, :, e2, t, e2, :], in_=c4r[0:64, :, e2, :]), [cr], [WC])
                            V(lambda e: e.tensor_scalar(out=wc6[64:128, :, e2, t, e2, :], in0=c4i[64:128, :, e2, :],
                                                        scalar1=-1.0, scalar2=None, op0=ALU.mult), [ci_], [WC], "pool")
                    if d < 8:
                        cmul(Yr[(d + 1) % 2], Yi[(d + 1) % 2], cr, ci_, ar, ai, [PWr, PWi])
                if with_a:
                    items, cx.rec = cx.rec, None
                    per = len(items) // 56 + 1
                    if DEBUG_NO_INTERLEAVE:
                        cx.pump(items, len(items))
                    self._ssm_a_body(li, sb2, ps2, lambda: cx.pump(items, per))
                    cx.pump(items, len(items))
            cx.barrier()

            with ExitStack() as s3:
                sb3, ps3 = self.mk(s3)
                uT = [sb3(f"uT{i}", [P, S], BF16) for i in range(3)]
                yT = [sb3(f"yT{i}", [P, S], BF16) for i in range(2)]
                Sb = [[sb3(f"Sb{k}{i}", [P, NK + 1], BF16) for i in range(8)] for k in range(3)]
                Rr = Ring([sb3(f"R{i}", [P, P], BF16) for i in range(12)])
                Rb1 = {r_: Buf("Rl") for r_ in Rr.items}
                pV = Ring([ps3(f"pV{i}", [P, NK]) for i in range(2)])
                pDb = Ring([ps3(f"pDb{i}", [P, NK]) for i in range(3)])
                pY = Ring([ps3(f"pY{i}", [P, NK]) for i in range(2)])
                for k_ in range(3):
                    for g_ in range(8):
                        cx.op("dve", lambda e: e.memset(Sb[k_][g_][:, 0:1], 0.0), writes=[Sb[k_][g_].b])

                def stage_b(j):
                    u = uT[j % 3]
                    u3 = u[:].rearrange("p (l k) -> p k l", l=L)
                    out = []

                    def load():
                        cx.dma("sp", u[:].rearrange("p (l k) -> p l k", l=L), self.uT_d.ap()[j * P:(j + 1) * P],
                               u.b, writes=[u.b])
                    out.append(load)
                    for g_ in range(8):
                        def grp(g_=g_):
                            q, eo = g_ // 2, g_ % 2
                            rows = slice(32 * q, 32 * q + 32)
                            pv = pV.next()
                            sg = Sb[j % 3][g_]
                            for tau in range(L):
                                cx.op("pe", lambda e: e.matmul(out=pv[:], lhsT=WB[eo][rows, j, tau, :], rhs=u3[rows, :, tau],
                                                               start=(tau == 0), stop=(tau == L - 1),
                                                               tile_position=(32 * q, 0)),
                                      reads=[WB[eo].b, u.b], writes=[pv.b], inc=(tau == L - 1))
                            cx.op("act", lambda e: e.copy(out=sg[:, 1:NK + 1], in_=pv[:]), reads=[pv.b], writes=[sg.b])
                        out.append(grp)
                    return out

                def stage_c(j):
                    out = []
                    for jj in range(J):
                        def step(jj=jj):
                            dsh = 2 ** jj
                            for g_ in range(8):
                                g = 8 * j + g_
                                R = Rr.next()
                                cx.op("pool", lambda e: e.tensor_scalar(out=R[:, 0:64], in0=I2[:, 0:64],
                                                                        scalar1=a1[:, jj, g:g + 1], scalar2=1.0,
                                                                        op0=ALU.mult, op1=ALU.mult),
                                      reads=[I2.b, a1.b], writes=[Rb1[R]])
                                cx.op("act", lambda e: e.activation(out=R[:, 64:128], in_=I2[:, 64:128], func=AF.Copy,
                                                                    scale=a2[:, jj, g:g + 1]),
                                      reads=[I2.b, a2.b], writes=[R.b])
                                pd = pDb.next()
                                sg = Sb[j % 3][g_]
                                cx.op("pe", lambda e: e.matmul(out=pd[:, 0:NK - dsh], lhsT=R[:], rhs=sg[:, 1:NK + 1 - dsh],
                                                               start=True, stop=True),
                                      reads=[R.b, Rb1[R], sg.b], writes=[pd.b])
                                cx.op("dve", lambda e: e.tensor_tensor(out=sg[:, 1 + dsh:NK + 1], in0=pd[:, 0:NK - dsh],
                                                                       in1=sg[:, 1 + dsh:NK + 1], op=ALU.add),
                                      reads=[pd.b, sg.b], writes=[sg.b])
                        out.append(step)
                    return out

                def stage_ad(j):
                    u = uT[j % 3]
                    y = yT[j % 2]
                    u3 = u[:].rearrange("p (l k) -> p k l", l=L)
                    y3 = y[:].rearrange("p (k l) -> p k l", l=L)
                    out = []
                    for t in range(L):
                        def pos(t=t):
                            py = pY.next()
                            for tau in range(t + 1):
                                cx.op("pe", lambda e: e.matmul(out=py[:], lhsT=Kb[:, j, t - tau, :], rhs=u3[:, :, tau],
                                                               start=(tau == 0), stop=False),
                                      reads=[Kb.b, u.b], writes=[py.b], inc=False)
                            for g_ in range(8):
                                g = 8 * j + g_
                                q = g_ // 2
                                sg = Sb[j % 3][g_]
                                cx.op("pe", lambda e: e.matmul(out=py[32 * q:32 * q + 32, :], lhsT=WC[:, g, t, :],
                                                               rhs=sg[:, 0:NK], start=False, stop=(g_ == 7),
                                                               tile_position=(0, 32 * q)),
                                      reads=[WC.b, sg.b], writes=[py.b], inc=(g_ == 7))
                            cx.op("act", lambda e: e.activation(out=y3[:, :, t], in_=py[:], func=AF.Gelu_apprx_tanh),
                                  reads=[py.b], writes=[y.b])
                        out.append(pos)

                    def store():
                        cx.dma("sp", self.gT_d.ap()[j * P:(j + 1) * P, :], y[:], y.b, reads=[y.b])
                    out.append(store)
                    return out

                for j in range(-1, 9):
                    lb = stage_b(j + 1) if 0 <= j + 1 < 8 else []
                    lc = stage_c(j) if 0 <= j < 8 else []
                    la = stage_ad(j - 1) if 0 <= j - 1 < 8 else []
                    n = max(len(lb), len(lc), len(la))
                    for i in range(n):
                        if i < len(lc):
                            lc[i]()
                        if i < len(lb):
                            lb[i]()
                        if i < len(la):
                            la[i]()

    def ssm_c(self, li):
        nc, cx, S = self.nc, self.cx, self.S
        T = 512
        NS = S // T
        with ExitStack() as st:
            sb, ps = self.mk(st)
            wg = sb("wglu", [P, 8, 2 * D], BF16)
            gt = [sb(f"gt{i}", [P, 8, T], BF16) for i in range(2)]
            xs = [sb(f"xs{i}", [P, D]) for i in range(4)]
            sg = Ring([sb(f"sg{i}", [P, 512]) for i in range(2)])
            tm = Ring([sb(f"tm{i}", [P, 512]) for i in range(2)])
            pA = Ring([ps(f"pA{i}", [P, 512]) for i in range(3)])
            pB = Ring([ps(f"pB{i}", [P, 512]) for i in range(3)])
            self.load_cast(wg, self.w["w_glu"].ap()[li].rearrange("(k p) n -> p k n", p=P), 8)
            for s in range(NS):
                tsl = slice(s * T, (s + 1) * T)
                g = gt[s % 2]
                cx.dma("sp", g[:], self.gT_d.ap()[:, tsl].rearrange("(k p) t -> p k t", p=P), g.b, writes=[g.b])
                for i in range(4):
                    t = s * 4 + i
                    src, sbuf = self.src_x(t)
                    x = xs[i]
                    cx.dma("sp", x[:], src, x.b, reads=[sbuf], writes=[x.b])
                    for hh in range(2):
                        pa, pb = pA.next(), pB.next()
                        for (pp, c0) in ((pa, hh * 512), (pb, D + hh * 512)):
                            for kc in range(8):
                                cx.op("pe", lambda e: e.matmul(out=pp[:], lhsT=g[:, kc, i * P:(i + 1) * P],
                                                               rhs=wg[:, kc, c0:c0 + 512], start=(kc == 0), stop=(kc == 7)),
                                      reads=[g.b, wg.b], writes=[pp.b], inc=(kc == 7))
                        s1 = sg.next()
                        cx.op("act", lambda e: e.activation(out=s1[:], in_=pb[:], func=AF.Sigmoid), reads=[pb.b], writes=[s1.b])
                        m = tm.next()
                        cx.op("dve", lambda e: e.tensor_tensor(out=m[:], in0=pa[:], in1=s1[:], op=ALU.mult),
                              reads=[pa.b, s1.b], writes=[m.b])
                        cx.op("pool", lambda e: e.tensor_tensor(out=x[:, hh * 512:(hh + 1) * 512],
                                                                in0=x[:, hh * 512:(hh + 1) * 512], in1=m[:], op=ALU.add),
                              reads=[m.b, x.b], writes=[x.b])
                    cx.dma("sp", self.out.ap()[t * P:(t + 1) * P, :], x[:], x.b, reads=[x.b], writes=[self.xt[t]])
        self.first = False


WEIGHT_SHAPES = {
    "attn_norm": (2, 1024), "mix_w_in": (2, 1024, 2080), "cq_norm": (2, 256), "ckv_norm": (2, 256),
    "w_uq": (2, 256, 768), "w_ukv": (2, 256, 1024), "q_gain": (2, 96), "k_gain": (2, 96),
    "sconv_w": (2, 3, 512), "mix_w_out": (2, 1024, 1024), "ssm_norm": (2, 1024), "ssm_w_in": (2, 1024, 1024),
    "lambda_re": (2, 64, 64), "lambda_im": (2, 64, 64), "log_step": (2, 64),
    "b_re": (2, 64, 64, 16), "b_im": (2, 64, 64, 16), "c_re": (2, 64, 16, 64), "c_im": (2, 64, 16, 64),
    "d_skip": (2, 1024), "w_glu": (2, 1024, 2048), "ffn_norm": (4, 1024), "ffn_w_up": (4, 1024, 5632),
    "ffn_conv_w": (4, 3, 5632), "ffn_w_down": (4, 2816, 1024),
}


def host_consts(S):
    rot = np.zeros((96, 96), np.float32)
    for i in range(16):
        rot[80 + i, 64 + i] = -1.0
        rot[64 + i, 80 + i] = 1.0
    kk = np.arange(P)[:, None]
    qq = np.arange(P)[None, :]
    tri = np.where(kk <= qq, 0.0, -30000.0).astype(np.float32)
    inv_freq = (1.0 / (10000.0 ** (np.arange(0, 32, 2, dtype=np.float32) / np.float32(32)))).astype(np.float32)
    ang = (np.arange(S, dtype=np.float32)[None, :] * inv_freq[:, None]).astype(np.float32)
    cos_t = np.ones((96, S), np.float32)
    sin_t = np.zeros((96, S), np.float32)
    cos_t[64:80] = np.cos(ang)
    cos_t[80:96] = np.cos(ang)
    sin_t[64:80] = np.sin(ang)
    sin_t[80:96] = np.sin(ang)
    i2 = np.tile(np.eye(64, dtype=np.float32), (2, 2))
    pidx = np.arange(P)
    pmask = np.stack([((pidx // 16) % 2 == 0), ((pidx // 16) % 2 == 1)], axis=1).astype(np.float32)
    bmask = ((pidx[:, None] // 16) == (pidx[None, :] // 16)).astype(np.float32)
    return {"ident": np.eye(P, dtype=np.float32), "rot": rot, "tri": tri, "cos_t": cos_t, "sin_t": sin_t,
            "i2": i2, "pmask": pmask, "bmask": bmask}


def all_phases():
    ph = []
    for l in range(DEPTH):
        if l % 2 == 0:
            ph += [("mixa", l), ("mixb", l), ("mixc", l)]
        else:
            ph += [("ssmab", l), ("ssmc", l)]
        ph += [("ffn", l)]
    return ph


def kernel(**inputs):
    x = np.ascontiguousarray(inputs["x"], dtype=np.float32)
    B, S, _ = x.shape
    prog = Prog(S, all_phases())
    nc = prog.build()
    consts = host_consts(S)
    shared = {k: np.ascontiguousarray(inputs[k], dtype=np.float32) for k in WEIGHT_SHAPES}
    in_maps = []
    for c in range(B):
        m = {"x": x[c]}
        m.update(shared)
        m.update(consts)
        in_maps.append(m)
    res = run_bass_kernel_spmd(nc, in_maps, core_ids=list(range(B)))
    return np.stack([r["out"] for r in res.results], axis=0)
```

```python
import math
from contextlib import ExitStack

import numpy as np
import concourse.bass as bass
import concourse.mybir as mybir
from concourse.bass_utils import run_bass_kernel_spmd

F32 = mybir.dt.float32
BF16 = mybir.dt.bfloat16
I32 = mybir.dt.int32
AF = mybir.ActivationFunctionType
ALU = mybir.AluOpType
AX = mybir.AxisListType

D = 1024
SEQ = 4096
NCORES = 8
DEPTH = 4
FH = 2816
EPS = 1e-6
P = 128


class Buf:
    __slots__ = ("name", "w", "r", "sem", "dcnt")

    def __init__(self, name):
        self.name = name
        self.w = None
        self.r = {}
        self.sem = None
        self.dcnt = 0


class _Dummy:
    def then_inc(self, *a, **k):
        return self


class _Rec:
    def __init__(self):
        self.call = None

    def __getattr__(self, name):
        def f(*a, **kw):
            self.call = (name, a, kw)
            return _Dummy()
        return f


class Ctx:
    def __init__(self, nc):
        self.nc = nc
        self.E = {"pe": nc.tensor, "act": nc.scalar, "dve": nc.vector, "pool": nc.gpsimd, "sp": nc.sync}
        self.sem = {}
        self.cnt = {}
        self.pending = {}
        for e in self.E:
            self.sem[e] = nc.alloc_semaphore("prog_" + e)
            self.cnt[e] = 0
            self.pending[e] = False
        self.waited = {}
        self.dma_sems = []
        self.sem_pool = []
        self.n_dsem = 0
        self.n_ins = 0
        self.rec = None

    def _wait(self, eng, toks):
        need = {}
        for t in toks:
            if t is None:
                continue
            sem, val, src = t
            if src == eng and eng in ("pe", "sp"):
                continue
            k = id(sem)
            if k not in need or need[k][1] < val:
                need[k] = (sem, val)
        for k, (sem, val) in need.items():
            if self.waited.get((eng, k), 0) >= val:
                continue
            self.E[eng].wait_ge(sem, val)
            self.n_ins += 1
            self.waited[(eng, k)] = val

    def _deps(self, reads, writes):
        deps = []
        for b in reads:
            deps.append(b.w)
        for b in writes:
            deps.append(b.w)
            deps.extend(b.r.values())
        return deps

    def op(self, eng, fn, reads=(), writes=(), inc=True):
        if self.rec is not None:
            r = _Rec()
            fn(r)
            self.rec.append(("op", eng, r.call, list(reads), list(writes), inc))
            return None
        self._wait(eng, self._deps(reads, writes))
        ins = fn(self.E[eng])
        self.n_ins += 1
        if inc:
            self.cnt[eng] += 1
            ins.then_inc(self.sem[eng], 1)
            self.pending[eng] = False
            tok = (self.sem[eng], self.cnt[eng], eng)
        else:
            self.pending[eng] = True
            tok = (self.sem[eng], self.cnt[eng] + 1, eng)
        for b in reads:
            b.r[eng] = tok
        for b in writes:
            b.w = tok
            b.r = {}
        return ins

    def dma(self, q, out, in_, slot, reads=(), writes=(), chain=False, **kw):
        if self.rec is not None:
            self.rec.append(("dma", q, out, in_, slot, list(reads), list(writes), chain, kw))
            return None
        if slot.sem is None:
            if self.sem_pool:
                slot.sem, slot.dcnt = self.sem_pool.pop()
            else:
                self.n_dsem += 1
                slot.sem = self.nc.alloc_semaphore(f"dsem{self.n_dsem}")
                slot.dcnt = 0
            self.dma_sems.append(slot)
        deps = self._deps(reads, writes)
        if chain:
            deps = [d for d in deps if d is None or d[2] != "dma" or d[0] is not slot.sem]
        self._wait(q, deps)
        ins = self.E[q].dma_start(out=out, in_=in_, **kw)
        self.n_ins += 1
        slot.dcnt += 16
        ins.then_inc(slot.sem, 16)
        tok = (slot.sem, slot.dcnt, "dma")
        for b in reads:
            b.r["dma" + str(id(slot))] = tok
        for b in writes:
            b.w = tok
            b.r = {}
        return tok

    def pump(self, items, n):
        rec, self.rec = self.rec, None
        for _ in range(min(n, len(items))):
            it = items.pop(0)
            if it[0] == "op":
                _, eng, (name, a, kw), reads, writes, inc = it
                self.op(eng, lambda e: getattr(e, name)(*a, **kw), reads=reads, writes=writes, inc=inc)
            else:
                _, q, out, in_, slot, reads, writes, chain, kw = it
                self.dma(q, out, in_, slot, reads=reads, writes=writes, chain=chain, **kw)
        self.rec = rec

    def barrier(self, exclude=()):
        toks = []
        for e in self.E:
            assert not self.pending[e], e
            if self.cnt[e]:
                toks.append((self.sem[e], self.cnt[e], "x"))
        keep = [s for s in self.dma_sems if s in exclude]
        for s in self.dma_sems:
            if s.dcnt and s not in exclude:
                toks.append((s.sem, s.dcnt, "dma"))
        for e in self.E:
            self._wait(e, [t for t in toks if t[0] is not self.sem[e]])
        for s in self.dma_sems:
            if s not in exclude:
                self.sem_pool.append((s.sem, s.dcnt))
                s.sem = None
        self.dma_sems = keep


def rms_scale(cx, x_ap, ss_ap, rstd_ap, sq_ap, xb, ssb, sqb, n):
    cx.op("act", lambda e: e.activation(out=sq_ap, in_=x_ap, func=AF.Square, accum_out=ss_ap),
          reads=[xb], writes=[sqb, ssb])
    cx.op("act", lambda e: e.activation(out=rstd_ap, in_=ss_ap, func=AF.Ln, bias=EPS_AP[0], scale=1.0 / n),
          reads=[ssb], writes=[ssb])
    cx.op("act", lambda e: e.activation(out=rstd_ap, in_=rstd_ap, func=AF.Exp, scale=-0.5), reads=[ssb], writes=[ssb])


EPS_AP = [None]
DEBUG_NO_INTERLEAVE = False
PUMP_FROM = 0


class TB:
    def __init__(self, t, name):
        self.t = t
        self.b = Buf(name)

    def __getitem__(self, k):
        return self.t[k]


class Ring:
    def __init__(self, items):
        self.items = items
        self.i = 0

    def next(self):
        x = self.items[self.i % len(self.items)]
        self.i += 1
        return x


class Prog:
    def __init__(self, S, layers, first_reads_x=True):
        self.S = S
        self.layers = layers
        nc = bass.Bass("TRN2", target_bir_lowering=False)
        self.nc = nc
        self.cx = Ctx(nc)
        self.din = {}
        self.NT = S // P

    def inp(self, name, shape, dt=F32):
        t = self.nc.dram_tensor(name, list(shape), dt, kind="ExternalInput")
        self.din[name] = t
        return t

    def build(self):
        nc, cx, S = self.nc, self.cx, self.S
        self.x_in = self.inp("x", [S, D])
        self.ident_d = self.inp("ident", [P, P])
        self.w = {}
        for name, shape in WEIGHT_SHAPES.items():
            self.w[name] = self.inp(name, shape)
        self.out = nc.dram_tensor("out", [S, D], F32, kind="ExternalOutput")
        self.rot_d = self.inp("rot", [96, 96])
        self.tri_d = self.inp("tri", [P, P])
        self.cos_d = self.inp("cos_t", [96, S])
        self.sin_d = self.inp("sin_t", [96, S])
        self.qT_d = nc.dram_tensor("qT_s", [8, 96, S], BF16, kind="Internal")
        self.kT_d = nc.dram_tensor("kT_s", [8, 96, S], BF16, kind="Internal")
        self.v_d = nc.dram_tensor("v_s", [S, 512], BF16, kind="Internal")
        self.convT_d = nc.dram_tensor("convT_s", [512, S], BF16, kind="Internal")
        self.attnT_d = nc.dram_tensor("attnT_s", [512, S], BF16, kind="Internal")
        self.uT_d = nc.dram_tensor("uT_s", [D, 8, S // 8], BF16, kind="Internal")
        self.gT_d = nc.dram_tensor("gT_s", [D, S], BF16, kind="Internal")
        self.i2_d = self.inp("i2", [P, P])
        self.pmask_d = self.inp("pmask", [P, 2])
        self.bmask_d = self.inp("bmask", [P, P])
        self.xt_in = [Buf(f"xin{t}") for t in range(self.NT)]
        self.xt = [Buf(f"xo{t}") for t in range(self.NT)]
        self.first = True

        with ExitStack() as gs:
            self.ident = gs.enter_context(nc.sbuf_tensor("identb", [P, P], BF16))
            self.identf = gs.enter_context(nc.sbuf_tensor("identf", [P, P], F32))
            self.epsc = gs.enter_context(nc.sbuf_tensor("epsc", [P, 1], F32))
            self.b_const = Buf("const")
            cx.dma("sp", self.identf[:], self.ident_d.ap(), self.b_const, writes=[self.b_const])
            cx.op("dve", lambda e: e.tensor_copy(out=self.ident[:], in_=self.identf[:]),
                  reads=[self.b_const], writes=[self.b_const])
            cx.op("dve", lambda e: e.memset(self.epsc[:], EPS), writes=[self.b_const])
            EPS_AP[0] = self.epsc[:]
            phs = list(self.layers)
            k_ = 0
            while k_ < len(phs) - 1:
                if phs[k_][0] in ("mixc", "ssmc") and phs[k_ + 1][0] == "ffn":
                    phs[k_:k_ + 2] = [("ffn+", phs[k_ + 1][1], phs[k_])]
                k_ += 1
            for ph in phs:
                kind, l = ph[0], ph[1]
                if kind == "ffn":
                    self.ffn_phase(l)
                elif kind == "ffn+":
                    pk, pl = ph[2]
                    self.ffn_phase(l, pre=lambda: getattr(self, {"mixc": "mixer_c", "ssmc": "ssm_c"}[pk])(pl // 2))
                elif kind == "mixa":
                    self.mixer_a(l // 2)
                elif kind == "mixb":
                    self.mixer_b(l // 2)
                elif kind == "mixc":
                    self.mixer_c(l // 2)
                elif kind == "ssm":
                    self.ssm_phase(l // 2)
                elif kind in ("ssma", "ssmb", "ssmc"):
                    getattr(self, "ssm_" + kind[-1])(l // 2)
                elif kind == "ssmab":
                    self.ssm_b(l // 2, with_a=True)
                cx.barrier()
            cx.barrier()
        return nc

    def src_x(self, t):
        if self.first:
            return self.x_in.ap()[t * P:(t + 1) * P, :], self.xt_in[t]
        return self.out.ap()[t * P:(t + 1) * P, :], self.xt[t]

    def ffn_phase(self, l, pre=None):
        nc, cx, S = self.nc, self.cx, self.S
        T = 512
        NS = S // T
        KC = D // P
        CT = FH // P
        with ExitStack() as st:
            sb, ps = self.mk(st)
            wup = sb("wup", [P, KC, 2 * FH], BF16)
            wdn = sb("wdn", [P, CT, D], BF16)
            self.load_cast(wup, self.w["ffn_w_up"].ap()[l].rearrange("(kc p) n -> p kc n", p=P), KC)
            self.load_cast(wdn, self.w["ffn_w_down"].ap()[l].rearrange("(kt p) n -> p kt n", p=P), CT)
            if pre is not None:
                pre()
                cx.barrier(exclude=(wup.b, wdn.b))
            gbc = sb("gbc", [P, D])
            cw = sb("cw", [P, 3, 2 * CT])
            xn = [sb(f"xn{i}", [P, D]) for i in range(1)]
            xs = [[xn[0], xn[0], xn[0], xn[0]]] * 2
            xr = Ring([sb(f"xr{i}", [P, D]) for i in range(2)])
            hn = [sb(f"hn{i}", [P, D], BF16) for i in range(2)]
            sq = sb("sq", [P, D], BF16)
            ss = [sb(f"ss{i}", [P, 1]) for i in range(2)]
            hnTs = [sb(f"hnT{i}", [P, KC, T], BF16) for i in range(2)]
            act = sb("act", [P, CT, T], BF16)
            U = [[sb(f"U{i}{j}", [P, T + 2], BF16) for j in range(2)] for i in range(2)]
            A = [[sb(f"A{i}{j}", [P, T]) for j in range(2)] for i in range(2)]
            halo = sb("halo", [P, 2 * CT, 2])
            pT = [ps(f"pT{i}", [P, KC, P], BF16) for i in range(2)]
            pU = [[ps(f"pU{i}{j}", [P, T]) for j in range(2)] for i in range(2)]
            pD = Ring([ps(f"pD{i}", [P, 512]) for i in range(2)])

            cx.dma("sp", gbc[:], self.w["ffn_norm"].ap()[l:l + 1, :].partition_broadcast(P), gbc.b, writes=[gbc.b])
            for k in range(3):
                cx.dma("sp", cw[:, k, :], self.w["ffn_conv_w"].ap()[l, k].rearrange("(c p) -> p c", p=P), cw.b,
                       writes=[cw.b], chain=True, allow_slow_non_contiguous=True)
            cx.op("dve", lambda e: e.memset(halo[:], 0.0), writes=[halo.b])

            self.load_norm_T(0, xs[0], hn, sq, ss, pT, hnTs[0], gbc)
            for s in range(NS):
                xcur = xs[s % 2]
                hnT = hnTs[s % 2]
                for ct in range(CT + 1):
                    if s + 1 < NS and ct in (3, 9, 15):
                        nx = (s + 1, xs[(s + 1) % 2], hn, sq, ss, pT, hnTs[(s + 1) % 2], gbc)
                        if ct == 3:
                            self.load_norm_T(*nx, part="A", subs=(0, 1))
                        elif ct == 9:
                            self.load_norm_T(*nx, part="B", subs=(0, 1))
                            self.load_norm_T(*nx, part="A", subs=(2, 3))
                        else:
                            self.load_norm_T(*nx, part="B", subs=(2, 3))
                    if ct < CT:
                        bi = ct % 2
                        cs_ = [ct, ct + CT]
                        for hv in range(2):
                            c, pu = cs_[hv], pU[bi][hv]
                            for kc in range(KC):
                                cx.op("pe", lambda e: e.matmul(out=pu[:], lhsT=wup[:, kc, c * P:(c + 1) * P],
                                                               rhs=hnT[:, kc, :], start=(kc == 0), stop=(kc == KC - 1)),
                                      reads=[wup.b, hnT.b], writes=[pu.b], inc=(kc == KC - 1))
                        for hv in range(2):
                            c, u = cs_[hv], U[bi][hv]
                            cx.op("pool", lambda e: e.tensor_copy(out=u[:, 0:2], in_=halo[:, c, :]),
                                  reads=[halo.b], writes=[u.b])
                        for hv in range(2):
                            c, pu, u, a = cs_[hv], pU[bi][hv], U[bi][hv], A[bi][hv]
                            cx.op("act", lambda e: e.copy(out=u[:, 2:T + 2], in_=pu[:]), reads=[pu.b], writes=[u.b])
                            cx.op("act", lambda e: e.activation(out=a[:], in_=pu[:], func=AF.Copy, scale=cw[:, 2, c:c + 1]),
                                  reads=[pu.b, cw.b], writes=[a.b])
                        for hv in range(2):
                            c, u = cs_[hv], U[bi][hv]
                            cx.op("pool", lambda e: e.tensor_copy(out=halo[:, c, :], in_=u[:, T:T + 2]),
                                  reads=[u.b], writes=[halo.b])
                        for (k, off) in ((1, 1), (0, 0)):
                            for hv in range(2):
                                c, u, a = cs_[hv], U[bi][hv], A[bi][hv]
                                cx.op("dve", lambda e: e.scalar_tensor_tensor(out=a[:], in0=u[:, off:T + off],
                                                                              scalar=cw[:, k, c:c + 1], in1=a[:],
                                                                              op0=ALU.mult, op1=ALU.add),
                                      reads=[u.b, a.b, cw.b], writes=[a.b])
                    if ct >= 1:
                        pc = ct - 1
                        ag, av = A[pc % 2][0], A[pc % 2][1]
                        cx.op("act", lambda e: e.activation(out=ag[:], in_=ag[:], func=AF.Silu), reads=[ag.b], writes=[ag.b])
                        cx.op("pool", lambda e: e.tensor_tensor(out=act[:, pc, :], in0=ag[:], in1=av[:], op=ALU.mult),
                              reads=[ag.b, av.b], writes=[act.b])
                for i in range(4):
                    t = s * 4 + i
                    x = xr.next()
                    src, sbuf = self.src_x(t)
                    cx.dma("sp", x[:], src, x.b, reads=[sbuf], writes=[x.b])
                    for h in range(2):
                        pd = pD.next()
                        for kt in range(CT):
                            cx.op("pe", lambda e: e.matmul(out=pd[:], lhsT=act[:, kt, i * P:(i + 1) * P],
                                                           rhs=wdn[:, kt, h * 512:(h + 1) * 512],
                                                           start=(kt == 0), stop=(kt == CT - 1)),
                                  reads=[act.b, wdn.b], writes=[pd.b], inc=(kt == CT - 1))
                        cx.op("dve", lambda e: e.tensor_tensor(out=x[:, h * 512:(h + 1) * 512],
                                                               in0=x[:, h * 512:(h + 1) * 512], in1=pd[:], op=ALU.add),
                              reads=[pd.b, x.b], writes=[x.b])
                    cx.dma("sp", self.out.ap()[t * P:(t + 1) * P, :], x[:], x.b, reads=[x.b], writes=[self.xt[t]])
        self.first = False

    def mk(self, st):
        nc = self.nc
        self.uid = getattr(self, "uid", 0) + 1
        pre = f"u{self.uid}_"
        sb = lambda name, shape, dt=F32: TB(st.enter_context(nc.sbuf_tensor(pre + name, list(shape), dt)), pre + name)
        ps = lambda name, shape, dt=F32: TB(st.enter_context(nc.psum_tensor(pre + name, list(shape), dt)), pre + name)
        return sb, ps

    def load_cast(self, dst, src_ap, nk):
        for k in range(nk):
            self.cx.dma("pool", dst[:, k, :], src_ap[:, k, :], dst.b, writes=[dst.b], chain=True,
                        max_dma_last_dim=4096)

    def load_norm_T(self, s, xs, hn, sq, ss, pT, hnT, gbc, T=512, KC=8, part="AB", subs=None):
        cx = self.cx
        nsub = T // P
        for i in (range(nsub) if subs is None else subs):
            t = s * nsub + i
            x = xs[i]
            j = i % 2
            hj = hn[i % len(hn)]
            if "A" in part:
                src, sbuf = self.src_x(t)
                cx.dma("sp", x[:], src, x.b, reads=[sbuf], writes=[x.b])
                rms_scale(cx, x[:], ss[j][:], ss[j][:], sq[:], x.b, ss[j].b, sq.b, D)
                cx.op("dve", lambda e: e.scalar_tensor_tensor(out=hj[:], in0=x[:], scalar=ss[j][:],
                                                              in1=gbc[:], op0=ALU.mult, op1=ALU.mult),
                      reads=[x.b, ss[j].b, gbc.b], writes=[hj.b])
            if "B" in part:
                for kc in range(KC):
                    cx.op("pe", lambda e: e.transpose(out=pT[j][:, kc, :], in_=hj[:, kc * P:(kc + 1) * P],
                                                      identity=self.ident[:]),
                          reads=[hj.b, self.b_const], writes=[pT[j].b], inc=(kc == KC - 1))
                cx.op("act", lambda e: e.copy(out=hnT[:, :, i * P:(i + 1) * P], in_=pT[j][:]),
                      reads=[pT[j].b], writes=[hnT.b])

    def mixer_a(self, li):
        nc, cx, S = self.nc, self.cx, self.S
        T = 512
        NS = S // T
        KC = 8
        W = self.w
        with ExitStack() as st:
            sb, ps = self.mk(st)
            win = sb("win", [P, KC, 2080], BF16)
            wuq = sb("wuq", [P, 2, 768], BF16)
            wukv = sb("wukv", [P, 2, 1024], BF16)
            wkr = sb("wkr", [P, KC, 96], BF16)
            wkn = sb("wkn", [P, 2, 8, 96], BF16)
            gbc = sb("gbc", [P, D])
            cqg = sb("cqg", [P, 2])
            ckvg = sb("ckvg", [P, 2])
            qg = sb("qg", [96, 1])
            kg = sb("kg", [96, 1])
            scw = sb("scw", [P, 3, 4])
            ones = sb("ones", [P, P], BF16)
            rotf = sb("rotf", [96, 96])
            rot = sb("rot", [96, 96], BF16)
            cosb = [sb(f"cos{i}", [96, T]) for i in range(2)]
            sinb = [sb(f"sin{i}", [96, T]) for i in range(2)]
            xs = [sb(f"xs{i}", [P, D]) for i in range(4)]
            hn = [sb(f"hn{i}", [P, D], BF16) for i in range(2)]
            sq = sb("sq", [P, D], BF16)
            ss = [sb(f"ss{i}", [P, 1]) for i in range(2)]
            hnT = sb("hnT", [P, KC, T], BF16)
            cn = [sb(f"cn{i}", [P, 2, T], BF16) for i in range(2)]
            sqb = Ring([sb(f"sqb{i}", [P, T], BF16) for i in range(3)])
            rs = Ring([sb(f"rs{i}", [P, T]) for i in range(3)])
            qn = Ring([sb(f"qn{i}", [96, T], BF16) for i in range(3)])
            t1 = Ring([sb(f"t1{i}", [96, T]) for i in range(2)])
            t2 = Ring([sb(f"t2{i}", [96, T]) for i in range(2)])
            qf = Ring([sb(f"qf{i}", [96, T], BF16) for i in range(3)])
            vt = Ring([sb(f"vt{i}", [P, 512], BF16) for i in range(2)])
            ci = Ring([sb(f"ci{i}", [P, T]) for i in range(2)])
            GC = [sb(f"GC{i}", [P, T + 2]) for i in range(4)]
            ca = Ring([sb(f"ca{i}", [P, T]) for i in range(2)])
            cv = Ring([sb(f"cv{i}", [P, T], BF16) for i in range(2)])
            pT = [ps(f"pT{i}", [P, KC, P], BF16) for i in range(2)]
            pM = Ring([ps(f"pM{i}", [P, T]) for i in range(3)])
            pS = Ring([ps(f"pS{i}", [P, T]) for i in range(2)])
            pR = Ring([ps(f"pR{i}", [P, T]) for i in range(1)])

            cx.dma("sp", gbc[:], W["attn_norm"].ap()[li:li + 1, :].partition_broadcast(P), gbc.b, writes=[gbc.b])
            cx.dma("sp", cqg[:], W["cq_norm"].ap()[li].rearrange("(c p) -> p c", p=P), cqg.b, writes=[cqg.b],
                   allow_slow_non_contiguous=True)
            cx.dma("sp", ckvg[:], W["ckv_norm"].ap()[li].rearrange("(c p) -> p c", p=P), ckvg.b, writes=[ckvg.b],
                   allow_slow_non_contiguous=True)
            cx.dma("sp", qg[:], W["q_gain"].ap()[li].rearrange("(p o) -> p o", o=1), qg.b, writes=[qg.b])
            cx.dma("sp", kg[:], W["k_gain"].ap()[li].rearrange("(p o) -> p o", o=1), kg.b, writes=[kg.b])
            for k in range(3):
                cx.dma("sp", scw[:, k, :], W["sconv_w"].ap()[li, k].rearrange("(c p) -> p c", p=P), scw.b,
                       writes=[scw.b], chain=True, allow_slow_non_contiguous=True)
            cx.dma("sp", rotf[:], self.rot_d.ap(), rotf.b, writes=[rotf.b])
            cx.op("dve", lambda e: e.tensor_copy(out=rot[:], in_=rotf[:]), reads=[rotf.b], writes=[rot.b])
            cx.op("dve", lambda e: e.memset(ones[:], 1.0), writes=[ones.b])
            self.load_cast(win, W["mix_w_in"].ap()[li].rearrange("(k p) n -> p k n", p=P), KC)
            self.load_cast(wuq, W["w_uq"].ap()[li].rearrange("(k p) n -> p k n", p=P), 2)
            self.load_cast(wukv, W["w_ukv"].ap()[li].rearrange("(k p) n -> p k n", p=P), 2)
            cx.op("pool", lambda e: e.memset(wkr[:], 0.0), writes=[wkr.b])
            cx.op("pool", lambda e: e.memset(wkn[:], 0.0), writes=[wkn.b])
            cx.op("pool", lambda e: e.tensor_copy(out=wkr[:, :, 64:96], in_=win[:, :, 512:544]),
                  reads=[win.b], writes=[wkr.b])
            for kc in range(2):
                cx.op("pool", lambda e: e.tensor_copy(
                    out=wkn[:, kc, :, 0:64],
                    in_=wukv[:, kc, :].rearrange("p (h c) -> p h c", c=128)[:, :, 0:64]),
                      reads=[wukv.b], writes=[wkn.b])
            for c in range(4):
                cx.op("dve", lambda e: e.memset(GC[c][:, 0:2], 0.0), writes=[GC[c].b])

            def lat_norm(col0, gcol, dst):
                pp = []
                sqs = []
                for c2 in range(2):
                    pm = pM.next()
                    for kc in range(KC):
                        cx.op("pe", lambda e: e.matmul(out=pm[:], lhsT=win[:, kc, col0 + c2 * P: col0 + (c2 + 1) * P],
                                                       rhs=hnT[:, kc, :], start=(kc == 0), stop=(kc == KC - 1)),
                              reads=[win.b, hnT.b], writes=[pm.b], inc=(kc == KC - 1))
                    q2 = sqb.next()
                    cx.op("act", lambda e: e.activation(out=q2[:], in_=pm[:], func=AF.Square),
                          reads=[pm.b], writes=[q2.b])
                    pp.append(pm)
                    sqs.append(q2)
                p1 = pS.next()
                for c2 in range(2):
                    cx.op("pe", lambda e: e.matmul(out=p1[:], lhsT=ones[:], rhs=sqs[c2][:], start=(c2 == 0),
                                                   stop=(c2 == 1)),
                          reads=[ones.b, sqs[c2].b], writes=[p1.b], inc=(c2 == 1))
                r = rs.next()
                cx.op("act", lambda e: e.activation(out=r[:], in_=p1[:], func=AF.Ln, bias=self.epsc[:],
                                                    scale=1.0 / 256),
                      reads=[p1.b, self.b_const], writes=[r.b])
                cx.op("act", lambda e: e.activation(out=r[:], in_=r[:], func=AF.Exp, scale=-0.5), reads=[r.b], writes=[r.b])
                for c2 in range(2):
                    cx.op("dve", lambda e: e.scalar_tensor_tensor(out=dst[:, c2, :], in0=pp[c2][:],
                                                                  scalar=gcol[:, c2:c2 + 1], in1=r[:],
                                                                  op0=ALU.mult, op1=ALU.mult),
                          reads=[pp[c2].b, gcol.b, r.b], writes=[dst.b])

            def head_stages(proj, gain, dst_ap, cs, sn):
                st_ = {}

                def A():
                    pm = proj()
                    q2 = sqb.next()
                    cx.op("act", lambda e: e.activation(out=q2[0:96, :], in_=pm[0:96, :], func=AF.Square),
                          reads=[pm.b], writes=[q2.b])
                    st_["pm"], st_["q2"] = pm, q2

                def B():
                    pm, q2 = st_["pm"], st_["q2"]
                    p1 = pS.next()
                    cx.op("pe", lambda e: e.matmul(out=p1[0:96, :], lhsT=ones[0:96, 0:96], rhs=q2[0:96, :],
                                                   start=True, stop=True),
                          reads=[ones.b, q2.b], writes=[p1.b])
                    r = rs.next()
                    cx.op("act", lambda e: e.activation(out=r[0:96, :], in_=p1[0:96, :], func=AF.Ln,
                                                        bias=self.epsc[0:96, :], scale=1.0 / 96),
                          reads=[p1.b, self.b_const], writes=[r.b])
                    cx.op("act", lambda e: e.activation(out=r[0:96, :], in_=r[0:96, :], func=AF.Exp, scale=-0.5),
                          reads=[r.b], writes=[r.b])
                    n = qn.next()
                    cx.op("dve", lambda e: e.scalar_tensor_tensor(out=n[:], in0=pm[0:96, :], scalar=gain[:],
                                                                  in1=r[0:96, :], op0=ALU.mult, op1=ALU.mult),
                          reads=[pm.b, gain.b, r.b], writes=[n.b])
                    st_["n"] = n

                def C():
                    n = st_["n"]
                    pr = pR.next()
                    cx.op("pe", lambda e: e.matmul(out=pr[0:96, :], lhsT=rot[:], rhs=n[:], start=True, stop=True),
                          reads=[rot.b, n.b], writes=[pr.b])
                    a1 = t1.next()
                    cx.op("pool", lambda e: e.tensor_tensor(out=a1[:], in0=n[:], in1=cs[:], op=ALU.mult),
                          reads=[n.b, cs.b], writes=[a1.b])
                    a2 = t2.next()
                    cx.op("dve", lambda e: e.tensor_tensor(out=a2[:], in0=pr[0:96, :], in1=sn[:], op=ALU.mult),
                          reads=[pr.b, sn.b], writes=[a2.b])
                    f = qf.next()
                    cx.op("pool", lambda e: e.tensor_tensor(out=f[:], in0=a1[:], in1=a2[:], op=ALU.add),
                          reads=[a1.b, a2.b], writes=[f.b])
                    cx.dma("sp", dst_ap, f[:], f.b, reads=[f.b])
                return A, B, C

            for s in range(NS):
                tsl = slice(s * T, (s + 1) * T)
                cs, sn = cosb[s % 2], sinb[s % 2]
                cx.dma("sp", cs[:], self.cos_d.ap()[:, tsl], cs.b, writes=[cs.b])
                cx.dma("sp", sn[:], self.sin_d.ap()[:, tsl], sn.b, writes=[sn.b])
                self.load_norm_T(s, xs, hn, sq, ss, pT, hnT, gbc)
                lat_norm(0, cqg, cn[0])
                lat_norm(256, ckvg, cn[1])
                heads = []
                for h in range(8):
                    def projq(h=h):
                        pm = pM.next()
                        for kc in range(2):
                            cx.op("pe", lambda e: e.matmul(out=pm[0:96, :], lhsT=wuq[:, kc, h * 96:(h + 1) * 96],
                                                           rhs=cn[0][:, kc, :], start=(kc == 0), stop=(kc == 1)),
                                  reads=[wuq.b, cn[0].b], writes=[pm.b], inc=(kc == 1))
                        return pm
                    heads.append(head_stages(projq, qg, self.qT_d.ap()[h, :, tsl], cs, sn))
                for h in range(8):
                    def projk(h=h):
                        pm = pM.next()
                        for kc in range(2):
                            cx.op("pe", lambda e: e.matmul(out=pm[0:96, :], lhsT=wkn[:, kc, h, :],
                                                           rhs=cn[1][:, kc, :], start=(kc == 0), stop=False),
                                  reads=[wkn.b, cn[1].b], writes=[pm.b], inc=False)
                        for kc in range(KC):
                            cx.op("pe", lambda e: e.matmul(out=pm[0:96, :], lhsT=wkr[:, kc, :], rhs=hnT[:, kc, :],
                                                           start=False, stop=(kc == KC - 1)),
                                  reads=[wkr.b, hnT.b], writes=[pm.b], inc=(kc == KC - 1))
                        return pm
                    heads.append(head_stages(projk, kg, self.kT_d.ap()[h, :, tsl], cs, sn))
                nh = len(heads)
                for i in range(nh + 2):
                    if i < nh:
                        heads[i][0]()
                    if 0 <= i - 1 < nh:
                        heads[i - 1][1]()
                    if 0 <= i - 2 < nh:
                        heads[i - 2][2]()
                for i in range(4):
                    pm = pM.next()
                    for kc in range(2):
                        cx.op("pe", lambda e: e.matmul(
                            out=pm[:], lhsT=cn[1][:, kc, i * P:(i + 1) * P],
                            rhs=wukv[:, kc, :].rearrange("p (h c) -> p h c", c=128)[:, :, 64:128],
                            start=(kc == 0), stop=(kc == 1)),
                              reads=[wukv.b, cn[1].b], writes=[pm.b], inc=(kc == 1))
                    v1 = vt.next()
                    cx.op("act", lambda e: e.copy(out=v1[:], in_=pm[:]), reads=[pm.b], writes=[v1.b])
                    t = s * 4 + i
                    cx.dma("sp", self.v_d.ap()[t * P:(t + 1) * P, :], v1[:], v1.b, reads=[v1.b])
                for c in range(4):
                    def proj(col0):
                        pm = pM.next()
                        for kc in range(KC):
                            cx.op("pe", lambda e: e.matmul(out=pm[:], lhsT=win[:, kc, col0 + c * P: col0 + (c + 1) * P],
                                                           rhs=hnT[:, kc, :], start=(kc == 0), stop=(kc == KC - 1)),
                                  reads=[win.b, hnT.b], writes=[pm.b], inc=(kc == KC - 1))
                        return pm
                    pci = proj(1568)
                    c1 = ci.next()
                    cx.op("act", lambda e: e.copy(out=c1[:], in_=pci[:]), reads=[pci.b], writes=[c1.b])
                    pgc = proj(1056)
                    g = GC[c]
                    cx.op("dve", lambda e: e.tensor_tensor(out=g[:, 2:T + 2], in0=pgc[:], in1=c1[:], op=ALU.mult),
                          reads=[pgc.b, c1.b], writes=[g.b])
                    a = ca.next()
                    cx.op("act", lambda e: e.activation(out=a[:], in_=g[:, 2:T + 2], func=AF.Copy,
                                                        scale=scw[:, 2, c:c + 1]),
                          reads=[g.b, scw.b], writes=[a.b])
                    cx.op("dve", lambda e: e.scalar_tensor_tensor(out=a[:], in0=g[:, 1:T + 1], scalar=scw[:, 1, c:c + 1],
                                                                  in1=a[:], op0=ALU.mult, op1=ALU.add),
                          reads=[g.b, a.b, scw.b], writes=[a.b])
                    cx.op("dve", lambda e: e.scalar_tensor_tensor(out=a[:], in0=g[:, 0:T], scalar=scw[:, 0, c:c + 1],
                                                                  in1=a[:], op0=ALU.mult, op1=ALU.add),
                          reads=[g.b, a.b, scw.b], writes=[a.b])
                    cx.op("pool", lambda e: e.tensor_copy(out=g[:, 0:2], in_=g[:, T:T + 2]), reads=[g.b], writes=[g.b])
                    pgb = proj(544)
                    o = cv.next()
                    cx.op("dve", lambda e: e.tensor_tensor(out=o[:], in0=pgb[:], in1=a[:], op=ALU.mult),
                          reads=[pgb.b, a.b], writes=[o.b])
                    cx.dma("sp", self.convT_d.ap()[c * P:(c + 1) * P, tsl], o[:], o.b, reads=[o.b])

    def mixer_b(self, li):
        nc, cx, S = self.nc, self.cx, self.S
        NT = S // P
        QB = 512
        NQ = S // QB
        scale = 96.0 ** -0.5
        with ExitStack() as st:
            sb, ps = self.mk(st)
            qT = [sb(f"qT{i}", [96, S], BF16) for i in range(2)]
            kT = [sb(f"kT{i}", [96, S], BF16) for i in range(2)]
            vv = [sb(f"vv{i}", [P, NT, 65], BF16) for i in range(2)]
            trif = sb("trif", [P, P])
            tri = sb("tri", [P, P], BF16)
            onesf = sb("onesf", [P, 64])
            pTb = Ring([sb(f"pTb{i}", [P, QB], BF16) for i in range(6)])
            rec = Ring([sb(f"rec{i}", [P, QB]) for i in range(2)])
            osb = Ring([sb(f"osb{i}", [64, QB]) for i in range(2)])
            ao = Ring([sb(f"ao{i}", [64, QB], BF16) for i in range(3)])
            pSc = Ring([ps(f"pSc{i}", [P, QB]) for i in range(5)])
            pO = Ring([ps(f"pO{i}", [P, QB]) for i in range(2)])
            pB = Ring([ps(f"pB{i}", [P, QB]) for i in range(1)])

            cx.dma("sp", trif[:], self.tri_d.ap(), trif.b, writes=[trif.b])
            cx.op("dve", lambda e: e.tensor_copy(out=tri[:], in_=trif[:]), reads=[trif.b], writes=[tri.b])
            cx.op("dve", lambda e: e.memset(onesf[:], 1.0), writes=[onesf.b])
            for i in range(2):
                cx.op("pool", lambda e: e.memset(vv[i][:, :, 64:65], 1.0), writes=[vv[i].b])

            def load_head(hh):
                q_, k_, v_ = qT[hh % 2], kT[hh % 2], vv[hh % 2]
                cx.dma("sp", q_[:], self.qT_d.ap()[hh], q_.b, writes=[q_.b])
                cx.dma("sp", k_[:], self.kT_d.ap()[hh], k_.b, writes=[k_.b])
                cx.dma("sp", v_[:, :, 0:64],
                       self.v_d.ap().rearrange("(t p) (h c) -> p t h c", p=P, c=64)[:, :, hh, :], v_.b, writes=[v_.b])

            load_head(0)
            for h in range(8):
                q, k, v = qT[h % 2], kT[h % 2], vv[h % 2]
                if h + 1 < 8:
                    load_head(h + 1)
                items = [(qb, j) for qb in range(NQ) for j in range(4 * qb + 4)]
                LOOK = 3
                scs = {}
                pos = {}
                deferred = []

                def stage_qk(idx):
                    qb, j = items[idx]
                    c0 = P * max(j - 4 * qb, 0)
                    psc = pSc.next()
                    diag = (j - 4 * qb) >= 0
                    cx.op("pe", lambda e: e.matmul(out=psc[:, c0:QB], lhsT=k[:, j * P:(j + 1) * P],
                                                   rhs=q[:, qb * QB + c0:(qb + 1) * QB], start=True, stop=not diag),
                          reads=[k.b, q.b], writes=[psc.b], inc=not diag)
                    if diag:
                        cx.op("pe", lambda e: e.matmul(out=psc[:, c0:c0 + P], lhsT=self.ident[:], rhs=tri[:],
                                                       start=False, stop=True),
                              reads=[tri.b, self.b_const], writes=[psc.b])
                    scs[idx] = psc

                for idx in range(min(LOOK, len(items))):
                    stage_qk(idx)
                for idx, (qb, j) in enumerate(items):
                    if idx + LOOK < len(items):
                        stage_qk(idx + LOOK)
                    nj = 4 * qb + 4
                    r = j - 4 * qb
                    c0 = P * max(r, 0)
                    if j == 0:
                        pos[qb] = pO.next()
                    po = pos[qb]
                    psc = scs.pop(idx)
                    pt = pTb.next()
                    cx.op("act", lambda e: e.activation(out=pt[:, c0:QB], in_=psc[:, c0:QB], func=AF.Exp, scale=scale),
                          reads=[psc.b], writes=[pt.b])
                    cx.op("pe", lambda e: e.matmul(out=po[0:65, c0:QB], lhsT=v[:, j, :], rhs=pt[:, c0:QB],
                                                   start=(j == 0), stop=(j == nj - 1)),
                          reads=[v.b, pt.b], writes=[po.b], inc=(j == nj - 1))
                    if j == nj - 1:
                        def norm(po=po, qb=qb):
                            rc = rec.next()
                            cx.op("act", lambda e: e.activation(out=rc[64:65, :], in_=po[64:65, :], func=AF.Ln),
                                  reads=[po.b], writes=[rc.b])
                            cx.op("act", lambda e: e.activation(out=rc[64:65, :], in_=rc[64:65, :], func=AF.Exp, scale=-1.0),
                                  reads=[rc.b], writes=[rc.b])
                            o1 = osb.next()
                            cx.op("act", lambda e: e.copy(out=o1[:], in_=po[0:64, :]), reads=[po.b], writes=[o1.b])

                            def fin():
                                pb = pB.next()
                                cx.op("pe", lambda e: e.matmul(out=pb[0:64, :], lhsT=onesf[64:65, :], rhs=rc[64:65, :],
                                                               start=True, stop=True),
                                      reads=[onesf.b, rc.b], writes=[pb.b])
                                a = ao.next()
                                cx.op("dve", lambda e: e.tensor_tensor(out=a[:], in0=pb[0:64, :], in1=o1[:], op=ALU.mult),
                                      reads=[pb.b, o1.b], writes=[a.b])
                                cx.dma("sp", self.attnT_d.ap()[h * 64:(h + 1) * 64, qb * QB:(qb + 1) * QB], a[:], a.b,
                                       reads=[a.b])
                            return fin
                        deferred.append((idx + 3, norm()))
                    while deferred and deferred[0][0] <= idx:
                        deferred.pop(0)[1]()
                while deferred:
                    deferred.pop(0)[1]()

    def mixer_c(self, li):
        nc, cx, S = self.nc, self.cx, self.S
        T = 512
        NS = S // T
        with ExitStack() as st:
            sb, ps = self.mk(st)
            wout = sb("wout", [P, 8, D], BF16)
            cat = [sb(f"cat{i}", [P, 8, T], BF16) for i in range(2)]
            xs = [sb(f"xs{i}", [P, D]) for i in range(4)]
            pD = Ring([ps(f"pD{i}", [P, 512]) for i in range(4)])
            self.load_cast(wout, self.w["mix_w_out"].ap()[li].rearrange("(k p) n -> p k n", p=P), 8)
            for s in range(NS):
                tsl = slice(s * T, (s + 1) * T)
                c = cat[s % 2]
                cx.dma("sp", c[:, 0:4, :], self.attnT_d.ap()[:, tsl].rearrange("(k p) t -> p k t", p=P), c.b,
                       writes=[c.b])
                cx.dma("sp", c[:, 4:8, :], self.convT_d.ap()[:, tsl].rearrange("(k p) t -> p k t", p=P), c.b,
                       writes=[c.b], chain=True)
                for i in range(4):
                    t = s * 4 + i
                    src, sbuf = self.src_x(t)
                    x = xs[i]
                    cx.dma("sp", x[:], src, x.b, reads=[sbuf], writes=[x.b])
                    for hh in range(2):
                        pd = pD.next()
                        for kc in range(8):
                            cx.op("pe", lambda e: e.matmul(out=pd[:], lhsT=c[:, kc, i * P:(i + 1) * P],
                                                           rhs=wout[:, kc, hh * 512:(hh + 1) * 512],
                                                           start=(kc == 0), stop=(kc == 7)),
                                  reads=[c.b, wout.b], writes=[pd.b], inc=(kc == 7))
                        cx.op("dve", lambda e: e.tensor_tensor(out=x[:, hh * 512:(hh + 1) * 512],
                                                               in0=x[:, hh * 512:(hh + 1) * 512], in1=pd[:],
                                                               op=ALU.add),
                              reads=[pd.b, x.b], writes=[x.b])
                    cx.dma("sp", self.out.ap()[t * P:(t + 1) * P, :], x[:], x.b, reads=[x.b], writes=[self.xt[t]])
        self.first = False

    def ssm_phase(self, li):
        self.ssm_a(li)
        self.cx.barrier()
        self.ssm_b(li)
        self.cx.barrier()
        self.ssm_c(li)

    def ssm_a(self, li):
        with ExitStack() as st:
            sb, ps = self.mk(st)
            self._ssm_a_body(li, sb, ps, None)

    def _ssm_a_body(self, li, sb, ps, pump):
        nc, cx, S = self.nc, self.cx, self.S
        T = 512
        NS = S // T
        KC = 8
        win = sb("win", [P, KC, D], BF16)
        gbc = sb("gbc", [P, D])
        xs = [sb(f"xs{i}", [P, D]) for i in range(4)]
        hn = [sb(f"hn{i}", [P, D], BF16) for i in range(2)]
        sq = sb("sq", [P, D], BF16)
        ss = [sb(f"ss{i}", [P, 1]) for i in range(2)]
        hnT = sb("hnT", [P, KC, T], BF16)
        ub = Ring([sb(f"ub{i}", [P, T], BF16) for i in range(3)])
        pT = [ps(f"pT{i}", [P, KC, P], BF16) for i in range(2)]
        pM = Ring([ps(f"pM{i}", [P, T]) for i in range(3)])
        cx.dma("sp", gbc[:], self.w["ssm_norm"].ap()[li:li + 1, :].partition_broadcast(P), gbc.b, writes=[gbc.b])
        self.load_cast(win, self.w["ssm_w_in"].ap()[li].rearrange("(k p) n -> p k n", p=P), KC)
        for s in range(NS):
            self.load_norm_T(s, xs, hn, sq, ss, pT, hnT, gbc)
            for j in range(8):
                pm = pM.next()
                for kc in range(KC):
                    cx.op("pe", lambda e: e.matmul(out=pm[:], lhsT=win[:, kc, j * P:(j + 1) * P], rhs=hnT[:, kc, :],
                                                   start=(kc == 0), stop=(kc == KC - 1)),
                          reads=[win.b, hnT.b], writes=[pm.b], inc=(kc == KC - 1))
                u = ub.next()
                cx.op("act", lambda e: e.copy(out=u[:].rearrange("p (l k) -> p l k", l=8),
                                              in_=pm[:].rearrange("p (k l) -> p l k", l=8)),
                      reads=[pm.b], writes=[u.b])
                cx.dma("sp", self.uT_d.ap()[j * P:(j + 1) * P, :, s * (T // 8):(s + 1) * (T // 8)],
                       u[:].rearrange("p (l k) -> p l k", l=8), u.b, reads=[u.b])
                if pump is not None and s >= PUMP_FROM:
                    pump()

    def ssm_b(self, li, with_a=False):
        nc, cx, S = self.nc, self.cx, self.S
        L = 8
        NK = S // L
        J = int(round(math.log2(NK)))
        assert 2 ** J == NK and NK <= 512
        W = self.w
        TWO_PI = 2.0 * math.pi
        with ExitStack() as st:
            sb, ps = self.mk(st)
            WB = [sb(f"WB{e}", [P, 8, 8, P], BF16) for e in range(2)]
            WC = sb("WC", [P, 64, 8, 32], BF16)
            Kb = sb("Kb", [P, 8, 8, P], BF16)
            a1 = sb("a1", [P, J, 64])
            a2 = sb("a2", [P, J, 64])
            I2 = sb("I2", [P, P])
            pmask = sb("pmask", [P, 2])
            bmask = sb("bmask", [P, P])
            dcol = sb("dcol", [P, 8])
            cx.dma("sp", I2[:], self.i2_d.ap(), I2.b, writes=[I2.b])
            cx.dma("sp", pmask[:], self.pmask_d.ap(), pmask.b, writes=[pmask.b])
            cx.dma("sp", bmask[:], self.bmask_d.ap(), bmask.b, writes=[bmask.b])
            cx.dma("sp", dcol[:], W["d_skip"].ap()[li].rearrange("(j q) -> q j", q=P), dcol.b, writes=[dcol.b],
                   allow_slow_non_contiguous=True)

            with ExitStack() as s2:
                sb2, ps2 = self.mk(s2)
                if with_a:
                    cx.rec = []
                cx.op("pool", lambda e: e.memset(WC[:], 0.0), writes=[WC.b])
                sm = lambda n: sb2(n, [P, 64])
                lr, lim, lsb, dt, mag, ang = sm("lr"), sm("lim"), sm("lsb"), sm("dt"), sm("mag"), sm("ang")
                yv, nf, fr, gt, sn, cs_, den, nr, t1, t2 = (sm("yv"), sm("nf"), sm("fr"), sm("gt"), sm("sn"), sm("cs"),
                                                           sm("den"), sm("nr"), sm("t1"), sm("t2"))
                ni = sb2("ni", [P, 64], I32)
                zr, zi = sm("zr"), sm("zi")
                PWr = sb2("PWr", [P, 9, 64])
                PWi = sb2("PWi", [P, 9, 64])
                DBr = sb2("DBr", [P, J, 64])
                DBi = sb2("DBi", [P, J, 64])
                big = lambda n, dt_=F32: sb2(n, [P, 1024], dt_)
                Bre, Bim = big("Bre"), big("Bim")
                Xr = [big("Xr0"), big("Xr1")]
                Xi = [big("Xi0"), big("Xi1")]
                Yr, Yi = Xr, Xi
                m1, m2 = big("m1"), big("m2")
                stk = big("stk")
                Bs = big("Bs", BF16)
                CLs = big("CLs", BF16)
                Cn = [sb2(f"Cn{i}", [P, 8, P]) for i in range(2)]
                ktmp = sb2("ktmp", [P, P])
                pw = Ring([ps2(f"pw{i}", [P, 4, P]) for i in range(3)])

                for hf in range(2):
                    hs = slice(hf * 64, hf * 64 + 64)
                    cx.dma("sp", lr[hs, :], W["lambda_re"].ap()[li].rearrange("g p -> p g"), lr.b, writes=[lr.b],
                           chain=True, allow_slow_non_contiguous=True)
                    cx.dma("sp", lim[hs, :], W["lambda_im"].ap()[li].rearrange("g p -> p g"), lim.b, writes=[lim.b],
                           chain=True, allow_slow_non_contiguous=True)
                    cx.dma("sp", Bre[hs, :].rearrange("p (g c) -> p g c", c=16),
                           W["b_re"].ap()[li].rearrange("g p c -> p g c"), Bre.b, writes=[Bre.b], chain=True)
                    cx.dma("sp", Bim[hs, :].rearrange("p (g c) -> p g c", c=16),
                           W["b_im"].ap()[li].rearrange("g p c -> p g c"), Bim.b, writes=[Bim.b], chain=True)
                    cx.dma("sp", Cn[0][:, :, hs], W["c_re"].ap()[li].rearrange("(t g) c p -> (g c) t p", g=8), Cn[0].b,
                           writes=[Cn[0].b], chain=True)
                    cx.dma("sp", Cn[1][:, :, hs], W["c_im"].ap()[li].rearrange("(t g) c p -> (g c) t p", g=8), Cn[1].b,
                           writes=[Cn[1].b], chain=True)
                cx.dma("sp", lsb[:], W["log_step"].ap()[li:li + 1, :].partition_broadcast(P), lsb.b, writes=[lsb.b])

                def V(fn, reads, writes, eng="dve"):
                    cx.op(eng, fn, reads=[r.b for r in reads], writes=[w_.b for w_ in writes])

                V(lambda e: e.activation(out=dt[:], in_=lsb[:], func=AF.Exp), [lsb], [dt], "act")
                V(lambda e: e.tensor_tensor(out=t1[:], in0=lr[:], in1=dt[:], op=ALU.mult), [lr, dt], [t1])
                V(lambda e: e.activation(out=mag[:], in_=t1[:], func=AF.Exp), [t1], [mag], "act")
                V(lambda e: e.tensor_tensor(out=ang[:], in0=lim[:], in1=dt[:], op=ALU.mult), [lim, dt], [ang])

                def sin_of(dst, shift):
                    V(lambda e: e.tensor_scalar(out=yv[:], in0=ang[:], scalar1=1.0 / TWO_PI, scalar2=shift,
                                                op0=ALU.mult, op1=ALU.add), [ang], [yv])
                    V(lambda e: e.tensor_copy(out=ni[:], in_=yv[:]), [yv], [ni])
                    V(lambda e: e.tensor_copy(out=nf[:], in_=ni[:]), [ni], [nf])
                    V(lambda e: e.tensor_tensor(out=fr[:], in0=yv[:], in1=nf[:], op=ALU.subtract), [yv, nf], [fr])
                    V(lambda e: e.tensor_scalar(out=gt[:], in0=fr[:], scalar1=0.5, scalar2=None, op0=ALU.is_gt),
                      [fr], [gt])
                    V(lambda e: e.tensor_tensor(out=fr[:], in0=fr[:], in1=gt[:], op=ALU.subtract), [fr, gt], [fr])
                    V(lambda e: e.tensor_scalar(out=gt[:], in0=fr[:], scalar1=-0.5, scalar2=None, op0=ALU.is_lt),
                      [fr], [gt])
                    V(lambda e: e.tensor_tensor(out=fr[:], in0=fr[:], in1=gt[:], op=ALU.add), [fr, gt], [fr])
                    V(lambda e: e.activation(out=dst[:], in_=fr[:], func=AF.Sin, scale=TWO_PI * (1.0 - 1e-6)),
                      [fr], [dst], "act")

                sin_of(sn, 0.0)
                sin_of(cs_, 0.25)
                ar, ai = PWr[:, 1, :], PWi[:, 1, :]
                V(lambda e: e.memset(PWr[:, 0, :], 1.0), [], [PWr])
                V(lambda e: e.memset(PWi[:, 0, :], 0.0), [], [PWi])
                V(lambda e: e.tensor_tensor(out=ar, in0=mag[:], in1=cs_[:], op=ALU.mult), [mag, cs_], [PWr])
                V(lambda e: e.tensor_tensor(out=ai, in0=mag[:], in1=sn[:], op=ALU.mult), [mag, sn], [PWi])
                V(lambda e: e.tensor_scalar(out=nr[:], in0=ar, scalar1=-1.0, scalar2=None, op0=ALU.add), [PWr], [nr])
                V(lambda e: e.tensor_tensor(out=t1[:], in0=lr[:], in1=lr[:], op=ALU.mult), [lr], [t1])
                V(lambda e: e.tensor_tensor(out=t2[:], in0=lim[:], in1=lim[:], op=ALU.mult), [lim], [t2])
                V(lambda e: e.tensor_tensor(out=den[:], in0=t1[:], in1=t2[:], op=ALU.add), [t1, t2], [den])
                V(lambda e: e.reciprocal(out=den[:], in_=den[:]), [den], [den])
                V(lambda e: e.tensor_tensor(out=t1[:], in0=nr[:], in1=lr[:], op=ALU.mult), [nr, lr], [t1])
                V(lambda e: e.tensor_tensor(out=t2[:], in0=ai, in1=lim[:], op=ALU.mult), [PWi, lim], [t2])
                V(lambda e: e.tensor_tensor(out=t1[:], in0=t1[:], in1=t2[:], op=ALU.add), [t1, t2], [t1])
                V(lambda e: e.tensor_tensor(out=zr[:], in0=t1[:], in1=den[:], op=ALU.mult), [t1, den], [zr])
                V(lambda e: e.tensor_tensor(out=t1[:], in0=ai, in1=lr[:], op=ALU.mult), [PWi, lr], [t1])
                V(lambda e: e.tensor_tensor(out=t2[:], in0=nr[:], in1=lim[:], op=ALU.mult), [nr, lim], [t2])
                V(lambda e: e.tensor_tensor(out=t1[:], in0=t1[:], in1=t2[:], op=ALU.subtract), [t1, t2], [t1])
                V(lambda e: e.tensor_tensor(out=zi[:], in0=t1[:], in1=den[:], op=ALU.mult), [t1, den], [zi])
                for d in range(2, 9):
                    pr_, pi_ = PWr[:, d - 1, :], PWi[:, d - 1, :]
                    V(lambda e: e.tensor_tensor(out=t1[:], in0=pr_, in1=ar, op=ALU.mult), [PWr], [t1])
                    V(lambda e: e.tensor_tensor(out=t2[:], in0=pi_, in1=ai, op=ALU.mult), [PWi], [t2])
                    V(lambda e: e.tensor_tensor(out=PWr[:, d, :], in0=t1[:], in1=t2[:], op=ALU.subtract), [t1, t2], [PWr])
                    V(lambda e: e.tensor_tensor(out=t1[:], in0=pr_, in1=ai, op=ALU.mult), [PWr, PWi], [t1])
                    V(lambda e: e.tensor_tensor(out=t2[:], in0=pi_, in1=ar, op=ALU.mult), [PWr, PWi], [t2])
                    V(lambda e: e.tensor_tensor(out=PWi[:, d, :], in0=t1[:], in1=t2[:], op=ALU.add), [t1, t2], [PWi])
                V(lambda e: e.tensor_copy(out=DBr[:, 0, :], in_=PWr[:, 8, :]), [PWr], [DBr])
                V(lambda e: e.tensor_copy(out=DBi[:, 0, :], in_=PWi[:, 8, :]), [PWi], [DBi])
                for j in range(1, J):
                    r_, i_ = DBr[:, j - 1, :], DBi[:, j - 1, :]
                    V(lambda e: e.tensor_tensor(out=t1[:], in0=r_, in1=r_, op=ALU.mult), [DBr], [t1])
                    V(lambda e: e.tensor_tensor(out=t2[:], in0=i_, in1=i_, op=ALU.mult), [DBi], [t2])
                    V(lambda e: e.tensor_tensor(out=DBr[:, j, :], in0=t1[:], in1=t2[:], op=ALU.subtract), [t1, t2], [DBr])
                    V(lambda e: e.tensor_tensor(out=t1[:], in0=r_, in1=i_, op=ALU.mult), [DBr, DBi], [t1])
                    V(lambda e: e.tensor_scalar(out=DBi[:, j, :], in0=t1[:], scalar1=2.0, scalar2=None, op0=ALU.mult),
                      [t1], [DBi])
                V(lambda e: e.tensor_copy(out=a1[0:64], in_=DBr[0:64]), [DBr], [a1])
                V(lambda e: e.tensor_scalar(out=a1[64:128], in0=DBi[64:128], scalar1=-1.0, scalar2=None, op0=ALU.mult),
                  [DBi], [a1])
                V(lambda e: e.tensor_copy(out=a2[0:64], in_=DBi[0:64]), [DBi], [a2])
                V(lambda e: e.tensor_copy(out=a2[64:128], in_=DBr[64:128]), [DBr], [a2])

                def cmul(orr, oii, xr, xi, br_ap, bi_ap, rb):
                    v3 = lambda t: t[:].rearrange("p (g c) -> p g c", c=16)
                    bb = lambda a: a.unsqueeze(2).to_broadcast([P, 64, 16])
                    V(lambda e: e.tensor_tensor(out=v3(m1), in0=v3(xr), in1=bb(br_ap), op=ALU.mult), [xr] + rb, [m1])
                    V(lambda e: e.tensor_tensor(out=v3(m2), in0=v3(xi), in1=bb(bi_ap), op=ALU.mult), [xi] + rb, [m2], "pool")
                    V(lambda e: e.tensor_tensor(out=orr[:], in0=m1[:], in1=m2[:], op=ALU.subtract), [m1, m2], [orr])
                    V(lambda e: e.tensor_tensor(out=v3(m1), in0=v3(xr), in1=bb(bi_ap), op=ALU.mult), [xr] + rb, [m1])
                    V(lambda e: e.tensor_tensor(out=v3(m2), in0=v3(xi), in1=bb(br_ap), op=ALU.mult), [xi] + rb, [m2], "pool")
                    V(lambda e: e.tensor_tensor(out=oii[:], in0=m1[:], in1=m2[:], op=ALU.add), [m1, m2], [oii], "pool")

                cmul(Xr[0], Xi[0], Bre, Bim, zr[:], zi[:], [zr, zi])
                for e_ in range(8):
                    cr, ci_ = Xr[e_ % 2], Xi[e_ % 2]
                    V(lambda e: e.tensor_copy(out=stk[0:64, :], in_=cr[0:64, :]), [cr], [stk])
                    V(lambda e: e.tensor_copy(out=stk[64:128, :], in_=ci_[64:128, :]), [ci_], [stk], "pool")
                    if e_ == 0:
                        V(lambda e: e.tensor_copy(out=Bs[:], in_=stk[:]), [stk], [Bs])
                    tau = 7 - e_
                    for j0 in (0, 4):
                        p_ = pw.next()
                        for jj in range(4):
                            j = j0 + jj
                            cx.op("pe", lambda e: e.transpose(out=p_[:, jj, :], in_=stk[:, j * P:(j + 1) * P],
                                                              identity=self.identf[:]),
                                  reads=[stk.b, self.b_const], writes=[p_.b], inc=(jj == 3))
                        for eo in range(2):
                            V(lambda e: e.tensor_scalar(out=WB[eo][:, j0:j0 + 4, tau, :], in0=p_[:], scalar1=pmask[:, eo:eo + 1],
                                                        scalar2=None, op0=ALU.mult), [p_, pmask], [WB[eo]])
                    if e_ < 7:
                        cmul(Xr[(e_ + 1) % 2], Xi[(e_ + 1) % 2], cr, ci_, ar, ai, [PWr, PWi])
                for ri in range(2):
                    dst = (Yr[0], Yi[0])[ri]
                    for j0 in (0, 4):
                        p_ = pw.next()
                        for jj in range(4):
                            j = j0 + jj
                            cx.op("pe", lambda e: e.transpose(out=p_[:, jj, :], in_=Cn[ri][:, j, :], identity=self.identf[:]),
                                  reads=[Cn[ri].b, self.b_const], writes=[p_.b], inc=(jj == 3))
                        V(lambda e: e.copy(out=dst[:, j0 * P:(j0 + 4) * P], in_=p_[:].rearrange("p a b -> p (a b)")),
                          [p_], [dst], "act")
                wc6 = WC[:].rearrange("p (pr e) t (e2 c) -> p pr e t e2 c", e=2, e2=2)
                for d in range(9):
                    cr, ci_ = Yr[d % 2], Yi[d % 2]
                    if d <= 7:
                        V(lambda e: e.tensor_copy(out=CLs[0:64, :], in_=cr[0:64, :]), [cr], [CLs])
                        V(lambda e: e.tensor_scalar(out=CLs[64:128, :], in0=ci_[64:128, :], scalar1=-1.0, scalar2=None,
                                                    op0=ALU.mult), [ci_], [CLs], "pool")
                        for j0 in (0, 4):
                            p_ = pw.next()
                            for jj in range(4):
                                j = j0 + jj
                                cx.op("pe", lambda e: e.matmul(out=p_[:, jj, :], lhsT=Bs[:, j * P:(j + 1) * P],
                                                               rhs=CLs[:, j * P:(j + 1) * P], start=True, stop=True),
                                      reads=[Bs.b, CLs.b], writes=[p_.b], inc=(jj == 3))
                            for jj in range(4):
                                j = j0 + jj
                                if d == 0:
                                    V(lambda e: e.tensor_tensor(out=ktmp[:], in0=p_[:, jj, :], in1=bmask[:], op=ALU.mult),
                                      [p_, bmask], [ktmp])
                                    V(lambda e: e.scalar_tensor_tensor(out=Kb[:, j, 0, :], in0=self.identf[:],
                                                                       scalar=dcol[:, j:j + 1], in1=ktmp[:],
                                                                       op0=ALU.mult, op1=ALU.add),
                                      [ktmp, dcol], [Kb])
                                else:
                                    V(lambda e: e.tensor_tensor(out=Kb[:, j, d, :], in0=p_[:, jj, :], in1=bmask[:],
                                                                op=ALU.mult), [p_, bmask], [Kb])
                    if d >= 1:
                        t = d - 1
                        c4r = cr[:].rearrange("p (pr e c) -> p pr e c", e=2, c=16)
                        c4i = ci_[:].rearrange("p (pr e c) -> p pr e c", e=2, c=16)
                        for e2 in range(2):
                            V(lambda e: e.tensor_copy(out=wc6[0:64, :, e2, t, e2, :], in_=c4r[0:64, :, e2, :]), [cr], [WC])
                            V(lambda e: e.tensor_scalar(out=wc6[64:128, :, e2, t, e2, :], in0=c4i[64:128, :, e2, :],
                                                        scalar1=-1.0, scalar2=None, op0=ALU.mult), [ci_], [WC], "pool")
                    if d < 8:
                        cmul(Yr[(d + 1) % 2], Yi[(d + 1) % 2], cr, ci_, ar, ai, [PWr, PWi])
                if with_a:
                    items, cx.rec = cx.rec, None
                    per = len(items) // 56 + 1
                    if DEBUG_NO_INTERLEAVE:
                        cx.pump(items, len(items))
                    self._ssm_a_body(li, sb2, ps2, lambda: cx.pump(items, per))
                    cx.pump(items, len(items))
            cx.barrier()

            with ExitStack() as s3:
                sb3, ps3 = self.mk(s3)
                uT = [sb3(f"uT{i}", [P, S], BF16) for i in range(3)]
                yT = [sb3(f"yT{i}", [P, S], BF16) for i in range(2)]
                Sb = [[sb3(f"Sb{k}{i}", [P, NK + 1], BF16) for i in range(8)] for k in range(3)]
                Rr = Ring([sb3(f"R{i}", [P, P], BF16) for i in range(12)])
                Rb1 = {r_: Buf("Rl") for r_ in Rr.items}
                pV = Ring([ps3(f"pV{i}", [P, NK]) for i in range(2)])
                pDb = Ring([ps3(f"pDb{i}", [P, NK]) for i in range(3)])
                pY = Ring([ps3(f"pY{i}", [P, NK]) for i in range(2)])
                for k_ in range(3):
                    for g_ in range(8):
                        cx.op("dve", lambda e: e.memset(Sb[k_][g_][:, 0:1], 0.0), writes=[Sb[k_][g_].b])

                def stage_b(j):
                    u = uT[j % 3]
                    u3 = u[:].rearrange("p (l k) -> p k l", l=L)
                    out = []

                    def load():
                        cx.dma("sp", u[:].rearrange("p (l k) -> p l k", l=L), self.uT_d.ap()[j * P:(j + 1) * P],
                               u.b, writes=[u.b])
                    out.append(load)
                    for g_ in range(8):
                        def grp(g_=g_):
                            q, eo = g_ // 2, g_ % 2
                            rows = slice(32 * q, 32 * q + 32)
                            pv = pV.next()
                            sg = Sb[j % 3][g_]
                            for tau in range(L):
                                cx.op("pe", lambda e: e.matmul(out=pv[:], lhsT=WB[eo][rows, j, tau, :], rhs=u3[rows, :, tau],
                                                               start=(tau == 0), stop=(tau == L - 1),
                                                               tile_position=(32 * q, 0)),
                                      reads=[WB[eo].b, u.b], writes=[pv.b], inc=(tau == L - 1))
                            cx.op("act", lambda e: e.copy(out=sg[:, 1:NK + 1], in_=pv[:]), reads=[pv.b], writes=[sg.b])
                        out.append(grp)
                    return out

                def stage_c(j):
                    out = []
                    for jj in range(J):
                        def step(jj=jj):
                            dsh = 2 ** jj
                            for g_ in range(8):
                                g = 8 * j + g_
                                R = Rr.next()
                                cx.op("pool", lambda e: e.tensor_scalar(out=R[:, 0:64], in0=I2[:, 0:64],
                                                                        scalar1=a1[:, jj, g:g + 1], scalar2=1.0,
                                                                        op0=ALU.mult, op1=ALU.mult),
                                      reads=[I2.b, a1.b], writes=[Rb1[R]])
                                cx.op("act", lambda e: e.activation(out=R[:, 64:128], in_=I2[:, 64:128], func=AF.Copy,
                                                                    scale=a2[:, jj, g:g + 1]),
                                      reads=[I2.b, a2.b], writes=[R.b])
                                pd = pDb.next()
                                sg = Sb[j % 3][g_]
                                cx.op("pe", lambda e: e.matmul(out=pd[:, 0:NK - dsh], lhsT=R[:], rhs=sg[:, 1:NK + 1 - dsh],
                                                               start=True, stop=True),
                                      reads=[R.b, Rb1[R], sg.b], writes=[pd.b])
                                cx.op("dve", lambda e: e.tensor_tensor(out=sg[:, 1 + dsh:NK + 1], in0=pd[:, 0:NK - dsh],
                                                                       in1=sg[:, 1 + dsh:NK + 1], op=ALU.add),
                                      reads=[pd.b, sg.b], writes=[sg.b])
                        out.append(step)
                    return out

                def stage_ad(j):
                    u = uT[j % 3]
                    y = yT[j % 2]
                    u3 = u[:].rearrange("p (l k) -> p k l", l=L)
                    y3 = y[:].rearrange("p (k l) -> p k l", l=L)
                    out = []
                    for t in range(L):
                        def pos(t=t):
                            py = pY.next()
                            for tau in range(t + 1):
                                cx.op("pe", lambda e: e.matmul(out=py[:], lhsT=Kb[:, j, t - tau, :], rhs=u3[:, :, tau],
                                                               start=(tau == 0), stop=False),
                                      reads=[Kb.b, u.b], writes=[py.b], inc=False)
                            for g_ in range(8):
                                g = 8 * j + g_
                                q = g_ // 2
                                sg = Sb[j % 3][g_]
                                cx.op("pe", lambda e: e.matmul(out=py[32 * q:32 * q + 32, :], lhsT=WC[:, g, t, :],
                                                               rhs=sg[:, 0:NK], start=False, stop=(g_ == 7),
                                                               tile_position=(0, 32 * q)),
                                      reads=[WC.b, sg.b], writes=[py.b], inc=(g_ == 7))
                            cx.op("act", lambda e: e.activation(out=y3[:, :, t], in_=py[:], func=AF.Gelu_apprx_tanh),
                                  reads=[py.b], writes=[y.b])
                        out.append(pos)

                    def store():
                        cx.dma("sp", self.gT_d.ap()[j * P:(j + 1) * P, :], y[:], y.b, reads=[y.b])
                    out.append(store)
                    return out

                for j in range(-1, 9):
                    lb = stage_b(j + 1) if 0 <= j + 1 < 8 else []
                    lc = stage_c(j) if 0 <= j < 8 else []
                    la = stage_ad(j - 1) if 0 <= j - 1 < 8 else []
                    n = max(len(lb), len(lc), len(la))
                    for i in range(n):
                        if i < len(lc):
                            lc[i]()
                        if i < len(lb):
                            lb[i]()
                        if i < len(la):
                            la[i]()

    def ssm_c(self, li):
        nc, cx, S = self.nc, self.cx, self.S
        T = 512
        NS = S // T
        with ExitStack() as st:
            sb, ps = self.mk(st)
            wg = sb("wglu", [P, 8, 2 * D], BF16)
            gt = [sb(f"gt{i}", [P, 8, T], BF16) for i in range(2)]
            xs = [sb(f"xs{i}", [P, D]) for i in range(4)]
            sg = Ring([sb(f"sg{i}", [P, 512]) for i in range(2)])
            tm = Ring([sb(f"tm{i}", [P, 512]) for i in range(2)])
            pA = Ring([ps(f"pA{i}", [P, 512]) for i in range(3)])
            pB = Ring([ps(f"pB{i}", [P, 512]) for i in range(3)])
            self.load_cast(wg, self.w["w_glu"].ap()[li].rearrange("(k p) n -> p k n", p=P), 8)
            for s in range(NS):
                tsl = slice(s * T, (s + 1) * T)
                g = gt[s % 2]
                cx.dma("sp", g[:], self.gT_d.ap()[:, tsl].rearrange("(k p) t -> p k t", p=P), g.b, writes=[g.b])
                for i in range(4):
                    t = s * 4 + i
                    src, sbuf = self.src_x(t)
                    x = xs[i]
                    cx.dma("sp", x[:], src, x.b, reads=[sbuf], writes=[x.b])
                    for hh in range(2):
                        pa, pb = pA.next(), pB.next()
                        for (pp, c0) in ((pa, hh * 512), (pb, D + hh * 512)):
                            for kc in range(8):
                                cx.op("pe", lambda e: e.matmul(out=pp[:], lhsT=g[:, kc, i * P:(i + 1) * P],
                                                               rhs=wg[:, kc, c0:c0 + 512], start=(kc == 0), stop=(kc == 7)),
                                      reads=[g.b, wg.b], writes=[pp.b], inc=(kc == 7))
                        s1 = sg.next()
                        cx.op("act", lambda e: e.activation(out=s1[:], in_=pb[:], func=AF.Sigmoid), reads=[pb.b], writes=[s1.b])
                        m = tm.next()
                        cx.op("dve", lambda e: e.tensor_tensor(out=m[:], in0=pa[:], in1=s1[:], op=ALU.mult),
                              reads=[pa.b, s1.b], writes=[m.b])
                        cx.op("pool", lambda e: e.tensor_tensor(out=x[:, hh * 512:(hh + 1) * 512],
                                                                in0=x[:, hh * 512:(hh + 1) * 512], in1=m[:], op=ALU.add),
                              reads=[m.b, x.b], writes=[x.b])
                    cx.dma("sp", self.out.ap()[t * P:(t + 1) * P, :], x[:], x.b, reads=[x.b], writes=[self.xt[t]])
        self.first = False


WEIGHT_SHAPES = {
    "attn_norm": (2, 1024), "mix_w_in": (2, 1024, 2080), "cq_norm": (2, 256), "ckv_norm": (2, 256),
    "w_uq": (2, 256, 768), "w_ukv": (2, 256, 1024), "q_gain": (2, 96), "k_gain": (2, 96),
    "sconv_w": (2, 3, 512), "mix_w_out": (2, 1024, 1024), "ssm_norm": (2, 1024), "ssm_w_in": (2, 1024, 1024),
    "lambda_re": (2, 64, 64), "lambda_im": (2, 64, 64), "log_step": (2, 64),
    "b_re": (2, 64, 64, 16), "b_im": (2, 64, 64, 16), "c_re": (2, 64, 16, 64), "c_im": (2, 64, 16, 64),
    "d_skip": (2, 1024), "w_glu": (2, 1024, 2048), "ffn_norm": (4, 1024), "ffn_w_up": (4, 1024, 5632),
    "ffn_conv_w": (4, 3, 5632), "ffn_w_down": (4, 2816, 1024),
}


def host_consts(S):
    rot = np.zeros((96, 96), np.float32)
    for i in range(16):
        rot[80 + i, 64 + i] = -1.0
        rot[64 + i, 80 + i] = 1.0
    kk = np.arange(P)[:, None]
    qq = np.arange(P)[None, :]
    tri = np.where(kk <= qq, 0.0, -30000.0).astype(np.float32)
    inv_freq = (1.0 / (10000.0 ** (np.arange(0, 32, 2, dtype=np.float32) / np.float32(32)))).astype(np.float32)
    ang = (np.arange(S, dtype=np.float32)[None, :] * inv_freq[:, None]).astype(np.float32)
    cos_t = np.ones((96, S), np.float32)
    sin_t = np.zeros((96, S), np.float32)
    cos_t[64:80] = np.cos(ang)
    cos_t[80:96] = np.cos(ang)
    sin_t[64:80] = np.sin(ang)
    sin_t[80:96] = np.sin(ang)
    i2 = np.tile(np.eye(64, dtype=np.float32), (2, 2))
    pidx = np.arange(P)
    pmask = np.stack([((pidx // 16) % 2 == 0), ((pidx // 16) % 2 == 1)], axis=1).astype(np.float32)
    bmask = ((pidx[:, None] // 16) == (pidx[None, :] // 16)).astype(np.float32)
    return {"ident": np.eye(P, dtype=np.float32), "rot": rot, "tri": tri, "cos_t": cos_t, "sin_t": sin_t,
            "i2": i2, "pmask": pmask, "bmask": bmask}


def all_phases():
    ph = []
    for l in range(DEPTH):
        if l % 2 == 0:
            ph += [("mixa", l), ("mixb", l), ("mixc", l)]
        else:
            ph += [("ssmab", l), ("ssmc", l)]
        ph += [("ffn", l)]
    return ph


def kernel(**inputs):
    x = np.ascontiguousarray(inputs["x"], dtype=np.float32)
    B, S, _ = x.shape
    prog = Prog(S, all_phases())
    nc = prog.build()
    consts = host_consts(S)
    shared = {k: np.ascontiguousarray(inputs[k], dtype=np.float32) for k in WEIGHT_SHAPES}
    in_maps = []
    for c in range(B):
        m = {"x": x[c]}
        m.update(shared)
        m.update(consts)
        in_maps.append(m)
    res = run_bass_kernel_spmd(nc, in_maps, core_ids=list(range(B)))
    return np.stack([r["out"] for r in res.results], axis=0)
```

```python
import math
from contextlib import ExitStack

import numpy as np
import concourse.bass as bass
import concourse.mybir as mybir
from concourse.bass_utils import run_bass_kernel_spmd

F32 = mybir.dt.float32
BF16 = mybir.dt.bfloat16
I32 = mybir.dt.int32
AF = mybir.ActivationFunctionType
ALU = mybir.AluOpType
AX = mybir.AxisListType

D = 1024
SEQ = 4096
NCORES = 8
DEPTH = 4
FH = 2816
EPS = 1e-6
P = 128


class Buf:
    __slots__ = ("name", "w", "r", "sem", "dcnt")

    def __init__(self, name):
        self.name = name
        self.w = None
        self.r = {}
        self.sem = None
        self.dcnt = 0


class _Dummy:
    def then_inc(self, *a, **k):
        return self


class _Rec:
    def __init__(self):
        self.call = None

    def __getattr__(self, name):
        def f(*a, **kw):
            self.call = (name, a, kw)
            return _Dummy()
        return f


class Ctx:
    def __init__(self, nc):
        self.nc = nc
        self.E = {"pe": nc.tensor, "act": nc.scalar, "dve": nc.vector, "pool": nc.gpsimd, "sp": nc.sync}
        self.sem = {}
        self.cnt = {}
        self.pending = {}
        for e in self.E:
            self.sem[e] = nc.alloc_semaphore("prog_" + e)
            self.cnt[e] = 0
            self.pending[e] = False
        self.waited = {}
        self.dma_sems = []
        self.sem_pool = []
        self.n_dsem = 0
        self.n_ins = 0
        self.rec = None

    def _wait(self, eng, toks):
        need = {}
        for t in toks:
            if t is None:
                continue
            sem, val, src = t
            if src == eng and eng in ("pe", "sp"):
                continue
            k = id(sem)
            if k not in need or need[k][1] < val:
                need[k] = (sem, val)
        for k, (sem, val) in need.items():
            if self.waited.get((eng, k), 0) >= val:
                continue
            self.E[eng].wait_ge(sem, val)
            self.n_ins += 1
            self.waited[(eng, k)] = val

    def _deps(self, reads, writes):
        deps = []
        for b in reads:
            deps.append(b.w)
        for b in writes:
            deps.append(b.w)
            deps.extend(b.r.values())
        return deps

    def op(self, eng, fn, reads=(), writes=(), inc=True):
        if self.rec is not None:
            r = _Rec()
            fn(r)
            self.rec.append(("op", eng, r.call, list(reads), list(writes), inc))
            return None
        self._wait(eng, self._deps(reads, writes))
        ins = fn(self.E[eng])
        self.n_ins += 1
        if inc:
            self.cnt[eng] += 1
            ins.then_inc(self.sem[eng], 1)
            self.pending[eng] = False
            tok = (self.sem[eng], self.cnt[eng], eng)
        else:
            self.pending[eng] = True
            tok = (self.sem[eng], self.cnt[eng] + 1, eng)
        for b in reads:
            b.r[eng] = tok
        for b in writes:
            b.w = tok
            b.r = {}
        return ins

    def dma(self, q, out, in_, slot, reads=(), writes=(), chain=False, **kw):
        if self.rec is not None:
            self.rec.append(("dma", q, out, in_, slot, list(reads), list(writes), chain, kw))
            return None
        if slot.sem is None:
            if self.sem_pool:
                slot.sem, slot.dcnt = self.sem_pool.pop()
            else:
                self.n_dsem += 1
                slot.sem = self.nc.alloc_semaphore(f"dsem{self.n_dsem}")
                slot.dcnt = 0
            self.dma_sems.append(slot)
        deps = self._deps(reads, writes)
        if chain:
            deps = [d for d in deps if d is None or d[2] != "dma" or d[0] is not slot.sem]
        self._wait(q, deps)
        ins = self.E[q].dma_start(out=out, in_=in_, **kw)
        self.n_ins += 1
        slot.dcnt += 16
        ins.then_inc(slot.sem, 16)
        tok = (slot.sem, slot.dcnt, "dma")
        for b in reads:
            b.r["dma" + str(id(slot))] = tok
        for b in writes:
            b.w = tok
            b.r = {}
        return tok

    def pump(self, items, n):
        rec, self.rec = self.rec, None
        for _ in range(min(n, len(items))):
            it = items.pop(0)
            if it[0] == "op":
                _, eng, (name, a, kw), reads, writes, inc = it
                self.op(eng, lambda e: getattr(e, name)(*a, **kw), reads=reads, writes=writes, inc=inc)
            else:
                _, q, out, in_, slot, reads, writes, chain, kw = it
                self.dma(q, out, in_, slot, reads=reads, writes=writes, chain=chain, **kw)
        self.rec = rec

    def barrier(self, exclude=()):
        toks = []
        for e in self.E:
            assert not self.pending[e], e
            if self.cnt[e]:
                toks.append((self.sem[e], self.cnt[e], "x"))
        keep = [s for s in self.dma_sems if s in exclude]
        for s in self.dma_sems:
            if s.dcnt and s not in exclude:
                toks.append((s.sem, s.dcnt, "dma"))
        for e in self.E:
            self._wait(e, [t for t in toks if t[0] is not self.sem[e]])
        for s in self.dma_sems:
            if s not in exclude:
                self.sem_pool.append((s.sem, s.dcnt))
                s.sem = None
        self.dma_sems = keep


def rms_scale(cx, x_ap, ss_ap, rstd_ap, sq_ap, xb, ssb, sqb, n):
    cx.op("act", lambda e: e.activation(out=sq_ap, in_=x_ap, func=AF.Square, accum_out=ss_ap),
          reads=[xb], writes=[sqb, ssb])
    cx.op("act", lambda e: e.activation(out=rstd_ap, in_=ss_ap, func=AF.Ln, bias=EPS_AP[0], scale=1.0 / n),
          reads=[ssb], writes=[ssb])
    cx.op("act", lambda e: e.activation(out=rstd_ap, in_=rstd_ap, func=AF.Exp, scale=-0.5), reads=[ssb], writes=[ssb])


EPS_AP = [None]
DEBUG_NO_INTERLEAVE = False
PUMP_FROM = 0


class TB:
    def __init__(self, t, name):
        self.t = t
        self.b = Buf(name)

    def __getitem__(self, k):
        return self.t[k]


class Ring:
    def __init__(self, items):
        self.items = items
        self.i = 0

    def next(self):
        x = self.items[self.i % len(self.items)]
        self.i += 1
        return x


class Prog:
    def __init__(self, S, layers, first_reads_x=True):
        self.S = S
        self.layers = layers
        nc = bass.Bass("TRN2", target_bir_lowering=False)
        self.nc = nc
        self.cx = Ctx(nc)
        self.din = {}
        self.NT = S // P

    def inp(self, name, shape, dt=F32):
        t = self.nc.dram_tensor(name, list(shape), dt, kind="ExternalInput")
        self.din[name] = t
        return t

    def build(self):
        nc, cx, S = self.nc, self.cx, self.S
        self.x_in = self.inp("x", [S, D])
        self.ident_d = self.inp("ident", [P, P])
        self.w = {}
        for name, shape in WEIGHT_SHAPES.items():
            self.w[name] = self.inp(name, shape)
        self.out = nc.dram_tensor("out", [S, D], F32, kind="ExternalOutput")
        self.rot_d = self.inp("rot", [96, 96])
        self.tri_d = self.inp("tri", [P, P])
        self.cos_d = self.inp("cos_t", [96, S])
        self.sin_d = self.inp("sin_t", [96, S])
        self.qT_d = nc.dram_tensor("qT_s", [8, 96, S], BF16, kind="Internal")
        self.kT_d = nc.dram_tensor("kT_s", [8, 96, S], BF16, kind="Internal")
        self.v_d = nc.dram_tensor("v_s", [S, 512], BF16, kind="Internal")
        self.convT_d = nc.dram_tensor("convT_s", [512, S], BF16, kind="Internal")
        self.attnT_d = nc.dram_tensor("attnT_s", [512, S], BF16, kind="Internal")
        self.uT_d = nc.dram_tensor("uT_s", [D, 8, S // 8], BF16, kind="Internal")
        self.gT_d = nc.dram_tensor("gT_s", [D, S], BF16, kind="Internal")
        self.i2_d = self.inp("i2", [P, P])
        self.pmask_d = self.inp("pmask", [P, 2])
        self.bmask_d = self.inp("bmask", [P, P])
        self.xt_in = [Buf(f"xin{t}") for t in range(self.NT)]
        self.xt = [Buf(f"xo{t}") for t in range(self.NT)]
        self.first = True

        with ExitStack() as gs:
            self.ident = gs.enter_context(nc.sbuf_tensor("identb", [P, P], BF16))
            self.identf = gs.enter_context(nc.sbuf_tensor("identf", [P, P], F32))
            self.epsc = gs.enter_context(nc.sbuf_tensor("epsc", [P, 1], F32))
            self.b_const = Buf("const")
            cx.dma("sp", self.identf[:], self.ident_d.ap(), self.b_const, writes=[self.b_const])
            cx.op("dve", lambda e: e.tensor_copy(out=self.ident[:], in_=self.identf[:]),
                  reads=[self.b_const], writes=[self.b_const])
            cx.op("dve", lambda e: e.memset(self.epsc[:], EPS), writes=[self.b_const])
            EPS_AP[0] = self.epsc[:]
            phs = list(self.layers)
            k_ = 0
            while k_ < len(phs) - 1:
                if phs[k_][0] in ("mixc", "ssmc") and phs[k_ + 1][0] == "ffn":
                    phs[k_:k_ + 2] = [("ffn+", phs[k_ + 1][1], phs[k_])]
                k_ += 1
            for ph in phs:
                kind, l = ph[0], ph[1]
                if kind == "ffn":
                    self.ffn_phase(l)
                elif kind == "ffn+":
                    pk, pl = ph[2]
                    self.ffn_phase(l, pre=lambda: getattr(self, {"mixc": "mixer_c", "ssmc": "ssm_c"}[pk])(pl // 2))
                elif kind == "mixa":
                    self.mixer_a(l // 2)
                elif kind == "mixb":
                    self.mixer_b(l // 2)
                elif kind == "mixc":
                    self.mixer_c(l // 2)
                elif kind == "ssm":
                    self.ssm_phase(l // 2)
                elif kind in ("ssma", "ssmb", "ssmc"):
                    getattr(self, "ssm_" + kind[-1])(l // 2)
                elif kind == "ssmab":
                    self.ssm_b(l // 2, with_a=True)
                cx.barrier()
            cx.barrier()
        return nc

    def src_x(self, t):
        if self.first:
            return self.x_in.ap()[t * P:(t + 1) * P, :], self.xt_in[t]
        return self.out.ap()[t * P:(t + 1) * P, :], self.xt[t]

    def ffn_phase(self, l, pre=None):
        nc, cx, S = self.nc, self.cx, self.S
        T = 512
        NS = S // T
        KC = D // P
        CT = FH // P
        with ExitStack() as st:
            sb, ps = self.mk(st)
            wup = sb("wup", [P, KC, 2 * FH], BF16)
            wdn = sb("wdn", [P, CT, D], BF16)
            self.load_cast(wup, self.w["ffn_w_up"].ap()[l].rearrange("(kc p) n -> p kc n", p=P), KC)
            self.load_cast(wdn, self.w["ffn_w_down"].ap()[l].rearrange("(kt p) n -> p kt n", p=P), CT)
            if pre is not None:
                pre()
                cx.barrier(exclude=(wup.b, wdn.b))
            gbc = sb("gbc", [P, D])
            cw = sb("cw", [P, 3, 2 * CT])
            xn = [sb(f"xn{i}", [P, D]) for i in range(1)]
            xs = [[xn[0], xn[0], xn[0], xn[0]]] * 2
            xr = Ring([sb(f"xr{i}", [P, D]) for i in range(2)])
            hn = [sb(f"hn{i}", [P, D], BF16) for i in range(2)]
            sq = sb("sq", [P, D], BF16)
            ss = [sb(f"ss{i}", [P, 1]) for i in range(2)]
            hnTs = [sb(f"hnT{i}", [P, KC, T], BF16) for i in range(2)]
            act = sb("act", [P, CT, T], BF16)
            U = [[sb(f"U{i}{j}", [P, T + 2], BF16) for j in range(2)] for i in range(2)]
            A = [[sb(f"A{i}{j}", [P, T]) for j in range(2)] for i in range(2)]
            halo = sb("halo", [P, 2 * CT, 2])
            pT = [ps(f"pT{i}", [P, KC, P], BF16) for i in range(2)]
            pU = [[ps(f"pU{i}{j}", [P, T]) for j in range(2)] for i in range(2)]
            pD = Ring([ps(f"pD{i}", [P, 512]) for i in range(2)])

            cx.dma("sp", gbc[:], self.w["ffn_norm"].ap()[l:l + 1, :].partition_broadcast(P), gbc.b, writes=[gbc.b])
            for k in range(3):
                cx.dma("sp", cw[:, k, :], self.w["ffn_conv_w"].ap()[l, k].rearrange("(c p) -> p c", p=P), cw.b,
                       writes=[cw.b], chain=True, allow_slow_non_contiguous=True)
            cx.op("dve", lambda e: e.memset(halo[:], 0.0), writes=[halo.b])

            self.load_norm_T(0, xs[0], hn, sq, ss, pT, hnTs[0], gbc)
            for s in range(NS):
                xcur = xs[s % 2]
                hnT = hnTs[s % 2]
                for ct in range(CT + 1):
                    if s + 1 < NS and ct in (3, 9, 15):
                        nx = (s + 1, xs[(s + 1) % 2], hn, sq, ss, pT, hnTs[(s + 1) % 2], gbc)
                        if ct == 3:
                            self.load_norm_T(*nx, part="A", subs=(0, 1))
                        elif ct == 9:
                            self.load_norm_T(*nx, part="B", subs=(0, 1))
                            self.load_norm_T(*nx, part="A", subs=(2, 3))
                        else:
                            self.load_norm_T(*nx, part="B", subs=(2, 3))
                    if ct < CT:
                        bi = ct % 2
                        cs_ = [ct, ct + CT]
                        for hv in range(2):
                            c, pu = cs_[hv], pU[bi][hv]
                            for kc in range(KC):
                                cx.op("pe", lambda e: e.matmul(out=pu[:], lhsT=wup[:, kc, c * P:(c + 1) * P],
                                                               rhs=hnT[:, kc, :], start=(kc == 0), stop=(kc == KC - 1)),
                                      reads=[wup.b, hnT.b], writes=[pu.b], inc=(kc == KC - 1))
                        for hv in range(2):
                            c, u = cs_[hv], U[bi][hv]
                            cx.op("pool", lambda e: e.tensor_copy(out=u[:, 0:2], in_=halo[:, c, :]),
                                  reads=[halo.b], writes=[u.b])
                        for hv in range(2):
                            c, pu, u, a = cs_[hv], pU[bi][hv], U[bi][hv], A[bi][hv]
                            cx.op("act", lambda e: e.copy(out=u[:, 2:T + 2], in_=pu[:]), reads=[pu.b], writes=[u.b])
                            cx.op("act", lambda e: e.activation(out=a[:], in_=pu[:], func=AF.Copy, scale=cw[:, 2, c:c + 1]),
                                  reads=[pu.b, cw.b], writes=[a.b])
                        for hv in range(2):
                            c, u = cs_[hv], U[bi][hv]
                            cx.op("pool", lambda e: e.tensor_copy(out=halo[:, c, :], in_=u[:, T:T + 2]),
                                  reads=[u.b], writes=[halo.b])
                        for (k, off) in ((1, 1), (0, 0)):
                            for hv in range(2):
                                c, u, a = cs_[hv], U[bi][hv], A[bi][hv]
                                cx.op("dve", lambda e: e.scalar_tensor_tensor(out=a[:], in0=u[:, off:T + off],
                                                                              scalar=cw[:, k, c:c + 1], in1=a[:],
                                                                              op0=ALU.mult, op1=ALU.add),
                                      reads=[u.b, a.b, cw.b], writes=[a.b])
                    if ct >= 1:
                        pc = ct - 1
                        ag, av = A[pc % 2][0], A[pc % 2][1]
                        cx.op("act", lambda e: e.activation(out=ag[:], in_=ag[:], func=AF.Silu), reads=[ag.b], writes=[ag.b])
                        cx.op("pool", lambda e: e.tensor_tensor(out=act[:, pc, :], in0=ag[:], in1=av[:], op=ALU.mult),
                              reads=[ag.b, av.b], writes=[act.b])
                for i in range(4):
                    t = s * 4 + i
                    x = xr.next()
                    src, sbuf = self.src_x(t)
                    cx.dma("sp", x[:], src, x.b, reads=[sbuf], writes=[x.b])
                    for h in range(2):
                        pd = pD.next()
                        for kt in range(CT):
                            cx.op("pe", lambda e: e.matmul(out=pd[:], lhsT=act[:, kt, i * P:(i + 1) * P],
                                                           rhs=wdn[:, kt, h * 512:(h + 1) * 512],
                                                           start=(kt == 0), stop=(kt == CT - 1)),
                                  reads=[act.b, wdn.b], writes=[pd.b], inc=(kt == CT - 1))
                        cx.op("dve", lambda e: e.tensor_tensor(out=x[:, h * 512:(h + 1) * 512],
                                                               in0=x[:, h * 512:(h + 1) * 512], in1=pd[:], op=ALU.add),
                              reads=[pd.b, x.b], writes=[x.b])
                    cx.dma("sp", self.out.ap()[t * P:(t + 1) * P, :], x[:], x.b, reads=[x.b], writes=[self.xt[t]])
        self.first = False

    def mk(self, st):
        nc = self.nc
        self.uid = getattr(self, "uid", 0) + 1
        pre = f"u{self.uid}_"
        sb = lambda name, shape, dt=F32: TB(st.enter_context(nc.sbuf_tensor(pre + name, list(shape), dt)), pre + name)
        ps = lambda name, shape, dt=F32: TB(st.enter_context(nc.psum_tensor(pre + name, list(shape), dt)), pre + name)
        return sb, ps

    def load_cast(self, dst, src_ap, nk):
        for k in range(nk):
            self.cx.dma("pool", dst[:, k, :], src_ap[:, k, :], dst.b, writes=[dst.b], chain=True,
                        max_dma_last_dim=4096)

    def load_norm_T(self, s, xs, hn, sq, ss, pT, hnT, gbc, T=512, KC=8, part="AB", subs=None):
        cx = self.cx
        nsub = T // P
        for i in (range(nsub) if subs is None else subs):
            t = s * nsub + i
            x = xs[i]
            j = i % 2
            hj = hn[i % len(hn)]
            if "A" in part:
                src, sbuf = self.src_x(t)
                cx.dma("sp", x[:], src, x.b, reads=[sbuf], writes=[x.b])
                rms_scale(cx, x[:], ss[j][:], ss[j][:], sq[:], x.b, ss[j].b, sq.b, D)
                cx.op("dve", lambda e: e.scalar_tensor_tensor(out=hj[:], in0=x[:], scalar=ss[j][:],
                                                              in1=gbc[:], op0=ALU.mult, op1=ALU.mult),
                      reads=[x.b, ss[j].b, gbc.b], writes=[hj.b])
            if "B" in part:
                for kc in range(KC):
                    cx.op("pe", lambda e: e.transpose(out=pT[j][:, kc, :], in_=hj[:, kc * P:(kc + 1) * P],
                                                      identity=self.ident[:]),
                          reads=[hj.b, self.b_const], writes=[pT[j].b], inc=(kc == KC - 1))
                cx.op("act", lambda e: e.copy(out=hnT[:, :, i * P:(i + 1) * P], in_=pT[j][:]),
                      reads=[pT[j].b], writes=[hnT.b])

    def mixer_a(self, li):
        nc, cx, S = self.nc, self.cx, self.S
        T = 512
        NS = S // T
        KC = 8
        W = self.w
        with ExitStack() as st:
            sb, ps = self.mk(st)
            win = sb("win", [P, KC, 2080], BF16)
            wuq = sb("wuq", [P, 2, 768], BF16)
            wukv = sb("wukv", [P, 2, 1024], BF16)
            wkr = sb("wkr", [P, KC, 96], BF16)
            wkn = sb("wkn", [P, 2, 8, 96], BF16)
            gbc = sb("gbc", [P, D])
            cqg = sb("cqg", [P, 2])
            ckvg = sb("ckvg", [P, 2])
            qg = sb("qg", [96, 1])
            kg = sb("kg", [96, 1])
            scw = sb("scw", [P, 3, 4])
            ones = sb("ones", [P, P], BF16)
            rotf = sb("rotf", [96, 96])
            rot = sb("rot", [96, 96], BF16)
            cosb = [sb(f"cos{i}", [96, T]) for i in range(2)]
            sinb = [sb(f"sin{i}", [96, T]) for i in range(2)]
            xs = [sb(f"xs{i}", [P, D]) for i in range(4)]
            hn = [sb(f"hn{i}", [P, D], BF16) for i in range(2)]
            sq = sb("sq", [P, D], BF16)
            ss = [sb(f"ss{i}", [P, 1]) for i in range(2)]
            hnT = sb("hnT", [P, KC, T], BF16)
            cn = [sb(f"cn{i}", [P, 2, T], BF16) for i in range(2)]
            sqb = Ring([sb(f"sqb{i}", [P, T], BF16) for i in range(3)])
            rs = Ring([sb(f"rs{i}", [P, T]) for i in range(3)])
            qn = Ring([sb(f"qn{i}", [96, T], BF16) for i in range(3)])
            t1 = Ring([sb(f"t1{i}", [96, T]) for i in range(2)])
            t2 = Ring([sb(f"t2{i}", [96, T]) for i in range(2)])
            qf = Ring([sb(f"qf{i}", [96, T], BF16) for i in range(3)])
            vt = Ring([sb(f"vt{i}", [P, 512], BF16) for i in range(2)])
            ci = Ring([sb(f"ci{i}", [P, T]) for i in range(2)])
            GC = [sb(f"GC{i}", [P, T + 2]) for i in range(4)]
            ca = Ring([sb(f"ca{i}", [P, T]) for i in range(2)])
            cv = Ring([sb(f"cv{i}", [P, T], BF16) for i in range(2)])
            pT = [ps(f"pT{i}", [P, KC, P], BF16) for i in range(2)]
            pM = Ring([ps(f"pM{i}", [P, T]) for i in range(3)])
            pS = Ring([ps(f"pS{i}", [P, T]) for i in range(2)])
            pR = Ring([ps(f"pR{i}", [P, T]) for i in range(1)])

            cx.dma("sp", gbc[:], W["attn_norm"].ap()[li:li + 1, :].partition_broadcast(P), gbc.b, writes=[gbc.b])
            cx.dma("sp", cqg[:], W["cq_norm"].ap()[li].rearrange("(c p) -> p c", p=P), cqg.b, writes=[cqg.b],
                   allow_slow_non_contiguous=True)
            cx.dma("sp", ckvg[:], W["ckv_norm"].ap()[li].rearrange("(c p) -> p c", p=P), ckvg.b, writes=[ckvg.b],
                   allow_slow_non_contiguous=True)
            cx.dma("sp", qg[:], W["q_gain"].ap()[li].rearrange("(p o) -> p o", o=1), qg.b, writes=[qg.b])
            cx.dma("sp", kg[:], W["k_gain"].ap()[li].rearrange("(p o) -> p o", o=1), kg.b, writes=[kg.b])
            for k in range(3):
                cx.dma("sp", scw[:, k, :], W["sconv_w"].ap()[li, k].rearrange("(c p) -> p c", p=P), scw.b,
                       writes=[scw.b], chain=True, allow_slow_non_contiguous=True)
            cx.dma("sp", rotf[:], self.rot_d.ap(), rotf.b, writes=[rotf.b])
            cx.op("dve", lambda e: e.tensor_copy(out=rot[:], in_=rotf[:]), reads=[rotf.b], writes=[rot.b])
            cx.op("dve", lambda e: e.memset(ones[:], 1.0), writes=[ones.b])
            self.load_cast(win, W["mix_w_in"].ap()[li].rearrange("(k p) n -> p k n", p=P), KC)
            self.load_cast(wuq, W["w_uq"].ap()[li].rearrange("(k p) n -> p k n", p=P), 2)
            self.load_cast(wukv, W["w_ukv"].ap()[li].rearrange("(k p) n -> p k n", p=P), 2)
            cx.op("pool", lambda e: e.memset(wkr[:], 0.0), writes=[wkr.b])
            cx.op("pool", lambda e: e.memset(wkn[:], 0.0), writes=[wkn.b])
            cx.op("pool", lambda e: e.tensor_copy(out=wkr[:, :, 64:96], in_=win[:, :, 512:544]),
                  reads=[win.b], writes=[wkr.b])
            for kc in range(2):
                cx.op("pool", lambda e: e.tensor_copy(
                    out=wkn[:, kc, :, 0:64],
                    in_=wukv[:, kc, :].rearrange("p (h c) -> p h c", c=128)[:, :, 0:64]),
                      reads=[wukv.b], writes=[wkn.b])
            for c in range(4):
                cx.op("dve", lambda e: e.memset(GC[c][:, 0:2], 0.0), writes=[GC[c].b])

            def lat_norm(col0, gcol, dst):
                pp = []
                sqs = []
                for c2 in range(2):
                    pm = pM.next()
                    for kc in range(KC):
                        cx.op("pe", lambda e: e.matmul(out=pm[:], lhsT=win[:, kc, col0 + c2 * P: col0 + (c2 + 1) * P],
                                                       rhs=hnT[:, kc, :], start=(kc == 0), stop=(kc == KC - 1)),
                              reads=[win.b, hnT.b], writes=[pm.b], inc=(kc == KC - 1))
                    q2 = sqb.next()
                    cx.op("act", lambda e: e.activation(out=q2[:], in_=pm[:], func=AF.Square),
                          reads=[pm.b], writes=[q2.b])
                    pp.append(pm)
                    sqs.append(q2)
                p1 = pS.next()
                for c2 in range(2):
                    cx.op("pe", lambda e: e.matmul(out=p1[:], lhsT=ones[:], rhs=sqs[c2][:], start=(c2 == 0),
                                                   stop=(c2 == 1)),
                          reads=[ones.b, sqs[c2].b], writes=[p1.b], inc=(c2 == 1))
                r = rs.next()
                cx.op("act", lambda e: e.activation(out=r[:], in_=p1[:], func=AF.Ln, bias=self.epsc[:],
                                                    scale=1.0 / 256),
                      reads=[p1.b, self.b_const], writes=[r.b])
                cx.op("act", lambda e: e.activation(out=r[:], in_=r[:], func=AF.Exp, scale=-0.5), reads=[r.b], writes=[r.b])
                for c2 in range(2):
                    cx.op("dve", lambda e: e.scalar_tensor_tensor(out=dst[:, c2, :], in0=pp[c2][:],
                                                                  scalar=gcol[:, c2:c2 + 1], in1=r[:],
                                                                  op0=ALU.mult, op1=ALU.mult),
                          reads=[pp[c2].b, gcol.b, r.b], writes=[dst.b])

            def head_stages(proj, gain, dst_ap, cs, sn):
                st_ = {}

                def A():
                    pm = proj()
                    q2 = sqb.next()
                    cx.op("act", lambda e: e.activation(out=q2[0:96, :], in_=pm[0:96, :], func=AF.Square),
                          reads=[pm.b], writes=[q2.b])
                    st_["pm"], st_["q2"] = pm, q2

                def B():
                    pm, q2 = st_["pm"], st_["q2"]
                    p1 = pS.next()
                    cx.op("pe", lambda e: e.matmul(out=p1[0:96, :], lhsT=ones[0:96, 0:96], rhs=q2[0:96, :],
                                                   start=True, stop=True),
                          reads=[ones.b, q2.b], writes=[p1.b])
                    r = rs.next()
                    cx.op("act", lambda e: e.activation(out=r[0:96, :], in_=p1[0:96, :], func=AF.Ln,
                                                        bias=self.epsc[0:96, :], scale=1.0 / 96),
                          reads=[p1.b, self.b_const], writes=[r.b])
                    cx.op("act", lambda e: e.activation(out=r[0:96, :], in_=r[0:96, :], func=AF.Exp, scale=-0.5),
                          reads=[r.b], writes=[r.b])
                    n = qn.next()
                    cx.op("dve", lambda e: e.scalar_tensor_tensor(out=n[:], in0=pm[0:96, :], scalar=gain[:],
                                                                  in1=r[0:96, :], op0=ALU.mult, op1=ALU.mult),
                          reads=[pm.b, gain.b, r.b], writes=[n.b])
                    st_["n"] = n

                def C():
                    n = st_["n"]
                    pr = pR.next()
                    cx.op("pe", lambda e: e.matmul(out=pr[0:96, :], lhsT=rot[:], rhs=n[:], start=True, stop=True),
                          reads=[rot.b, n.b], writes=[pr.b])
                    a1 = t1.next()
                    cx.op("pool", lambda e: e.tensor_tensor(out=a1[:], in0=n[:], in1=cs[:], op=ALU.mult),
                          reads=[n.b, cs.b], writes=[a1.b])
                    a2 = t2.next()
                    cx.op("dve", lambda e: e.tensor_tensor(out=a2[:], in0=pr[0:96, :], in1=sn[:], op=ALU.mult),
                          reads=[pr.b, sn.b], writes=[a2.b])
                    f = qf.next()
                    cx.op("pool", lambda e: e.tensor_tensor(out=f[:], in0=a1[:], in1=a2[:], op=ALU.add),
                          reads=[a1.b, a2.b], writes=[f.b])
                    cx.dma("sp", dst_ap, f[:], f.b, reads=[f.b])
                return A, B, C

            for s in range(NS):
                tsl = slice(s * T, (s + 1) * T)
                cs, sn = cosb[s % 2], sinb[s % 2]
                cx.dma("sp", cs[:], self.cos_d.ap()[:, tsl], cs.b, writes=[cs.b])
                cx.dma("sp", sn[:], self.sin_d.ap()[:, tsl], sn.b, writes=[sn.b])
                self.load_norm_T(s, xs, hn, sq, ss, pT, hnT, gbc)
                lat_norm(0, cqg, cn[0])
                lat_norm(256, ckvg, cn[1])
                heads = []
                for h in range(8):
                    def projq(h=h):
                        pm = pM.next()
                        for kc in range(2):
                            cx.op("pe", lambda e: e.matmul(out=pm[0:96, :], lhsT=wuq[:, kc, h * 96:(h + 1) * 96],
                                                           rhs=cn[0][:, kc, :], start=(kc == 0), stop=(kc == 1)),
                                  reads=[wuq.b, cn[0].b], writes=[pm.b], inc=(kc == 1))
                        return pm
                    heads.append(head_stages(projq, qg, self.qT_d.ap()[h, :, tsl], cs, sn))
                for h in range(8):
                    def projk(h=h):
                        pm = pM.next()
                        for kc in range(2):
                            cx.op("pe", lambda e: e.matmul(out=pm[0:96, :], lhsT=wkn[:, kc, h, :],
                                                           rhs=cn[1][:, kc, :], start=(kc == 0), stop=False),
                                  reads=[wkn.b, cn[1].b], writes=[pm.b], inc=False)
                        for kc in range(KC):
                            cx.op("pe", lambda e: e.matmul(out=pm[0:96, :], lhsT=wkr[:, kc, :], rhs=hnT[:, kc, :],
                                                           start=False, stop=(kc == KC - 1)),
                                  reads=[wkr.b, hnT.b], writes=[pm.b], inc=(kc == KC - 1))
                        return pm
                    heads.append(head_stages(projk, kg, self.kT_d.ap()[h, :, tsl], cs, sn))
                nh = len(heads)
                for i in range(nh + 2):
                    if i < nh:
                        heads[i][0]()
                    if 0 <= i - 1 < nh:
                        heads[i - 1][1]()
                    if 0 <= i - 2 < nh:
                        heads[i - 2][2]()
                for i in range(4):
                    pm = pM.next()
                    for kc in range(2):
                        cx.op("pe", lambda e: e.matmul(
                            out=pm[:], lhsT=cn[1][:, kc, i * P:(i + 1) * P],
                            rhs=wukv[:, kc, :].rearrange("p (h c) -> p h c", c=128)[:, :, 64:128],
                            start=(kc == 0), stop=(kc == 1)),
                              reads=[wukv.b, cn[1].b], writes=[pm.b], inc=(kc == 1))
                    v1 = vt.next()
                    cx.op("act", lambda e: e.copy(out=v1[:], in_=pm[:]), reads=[pm.b], writes=[v1.b])
                    t = s * 4 + i
                    cx.dma("sp", self.v_d.ap()[t * P:(t + 1) * P, :], v1[:], v1.b, reads=[v1.b])
                for c in range(4):
                    def proj(col0):
                        pm = pM.next()
                        for kc in range(KC):
                            cx.op("pe", lambda e: e.matmul(out=pm[:], lhsT=win[:, kc, col0 + c * P: col0 + (c + 1) * P],
                                                           rhs=hnT[:, kc, :], start=(kc == 0), stop=(kc == KC - 1)),
                                  reads=[win.b, hnT.b], writes=[pm.b], inc=(kc == KC - 1))
                        return pm
                    pci = proj(1568)
                    c1 = ci.next()
                    cx.op("act", lambda e: e.copy(out=c1[:], in_=pci[:]), reads=[pci.b], writes=[c1.b])
                    pgc = proj(1056)
                    g = GC[c]
                    cx.op("dve", lambda e: e.tensor_tensor(out=g[:, 2:T + 2], in0=pgc[:], in1=c1[:], op=ALU.mult),
                          reads=[pgc.b, c1.b], writes=[g.b])
                    a = ca.next()
                    cx.op("act", lambda e: e.activation(out=a[:], in_=g[:, 2:T + 2], func=AF.Copy,
                                                        scale=scw[:, 2, c:c + 1]),
                          reads=[g.b, scw.b], writes=[a.b])
                    cx.op("dve", lambda e: e.scalar_tensor_tensor(out=a[:], in0=g[:, 1:T + 1], scalar=scw[:, 1, c:c + 1],
                                                                  in1=a[:], op0=ALU.mult, op1=ALU.add),
                          reads=[g.b, a.b, scw.b], writes=[a.b])
                    cx.op("dve", lambda e: e.scalar_tensor_tensor(out=a[:], in0=g[:, 0:T], scalar=scw[:, 0, c:c + 1],
                                                                  in1=a[:], op0=ALU.mult, op1=ALU.add),
                          reads=[g.b, a.b, scw.b], writes=[a.b])
                    cx.op("pool", lambda e: e.tensor_copy(out=g[:, 0:2], in_=g[:, T:T + 2]), reads=[g.b], writes=[g.b])
                    pgb = proj(544)
                    o = cv.next()
                    cx.op("dve", lambda e: e.tensor_tensor(out=o[:], in0=pgb[:], in1=a[:], op=ALU.mult),
                          reads=[pgb.b, a.b], writes=[o.b])
                    cx.dma("sp", self.convT_d.ap()[c * P:(c + 1) * P, tsl], o[:], o.b, reads=[o.b])

    def mixer_b(self, li):
        nc, cx, S = self.nc, self.cx, self.S
        NT = S // P
        QB = 512
        NQ = S // QB
        scale = 96.0 ** -0.5
        with ExitStack() as st:
            sb, ps = self.mk(st)
            qT = [sb(f"qT{i}", [96, S], BF16) for i in range(2)]
            kT = [sb(f"kT{i}", [96, S], BF16) for i in range(2)]
            vv = [sb(f"vv{i}", [P, NT, 65], BF16) for i in range(2)]
            trif = sb("trif", [P, P])
            tri = sb("tri", [P, P], BF16)
            onesf = sb("onesf", [P, 64])
            pTb = Ring([sb(f"pTb{i}", [P, QB], BF16) for i in range(6)])
            rec = Ring([sb(f"rec{i}", [P, QB]) for i in range(2)])
            osb = Ring([sb(f"osb{i}", [64, QB]) for i in range(2)])
            ao = Ring([sb(f"ao{i}", [64, QB], BF16) for i in range(3)])
            pSc = Ring([ps(f"pSc{i}", [P, QB]) for i in range(5)])
            pO = Ring([ps(f"pO{i}", [P, QB]) for i in range(2)])
            pB = Ring([ps(f"pB{i}", [P, QB]) for i in range(1)])

            cx.dma("sp", trif[:], self.tri_d.ap(), trif.b, writes=[trif.b])
            cx.op("dve", lambda e: e.tensor_copy(out=tri[:], in_=trif[:]), reads=[trif.b], writes=[tri.b])
            cx.op("dve", lambda e: e.memset(onesf[:], 1.0), writes=[onesf.b])
            for i in range(2):
                cx.op("pool", lambda e: e.memset(vv[i][:, :, 64:65], 1.0), writes=[vv[i].b])

            def load_head(hh):
                q_, k_, v_ = qT[hh % 2], kT[hh % 2], vv[hh % 2]
                cx.dma("sp", q_[:], self.qT_d.ap()[hh], q_.b, writes=[q_.b])
                cx.dma("sp", k_[:], self.kT_d.ap()[hh], k_.b, writes=[k_.b])
                cx.dma("sp", v_[:, :, 0:64],
                       self.v_d.ap().rearrange("(t p) (h c) -> p t h c", p=P, c=64)[:, :, hh, :], v_.b, writes=[v_.b])

            load_head(0)
            for h in range(8):
                q, k, v = qT[h % 2], kT[h % 2], vv[h % 2]
                if h + 1 < 8:
                    load_head(h + 1)
                items = [(qb, j) for qb in range(NQ) for j in range(4 * qb + 4)]
                LOOK = 3
                scs = {}
                pos = {}
                deferred = []

                def stage_qk(idx):
                    qb, j = items[idx]
                    c0 = P * max(j - 4 * qb, 0)
                    psc = pSc.next()
                    diag = (j - 4 * qb) >= 0
                    cx.op("pe", lambda e: e.matmul(out=psc[:, c0:QB], lhsT=k[:, j * P:(j + 1) * P],
                                                   rhs=q[:, qb * QB + c0:(qb + 1) * QB], start=True, stop=not diag),
                          reads=[k.b, q.b], writes=[psc.b], inc=not diag)
                    if diag:
                        cx.op("pe", lambda e: e.matmul(out=psc[:, c0:c0 + P], lhsT=self.ident[:], rhs=tri[:],
                                                       start=False, stop=True),
                              reads=[tri.b, self.b_const], writes=[psc.b])
                    scs[idx] = psc

                for idx in range(min(LOOK, len(items))):
                    stage_qk(idx)
                for idx, (qb, j) in enumerate(items):
                    if idx + LOOK < len(items):
                        stage_qk(idx + LOOK)
                    nj = 4 * qb + 4
                    r = j - 4 * qb
                    c0 = P * max(r, 0)
                    if j == 0:
                        pos[qb] = pO.next()
                    po = pos[qb]
                    psc = scs.pop(idx)
                    pt = pTb.next()
                    cx.op("act", lambda e: e.activation(out=pt[:, c0:QB], in_=psc[:, c0:QB], func=AF.Exp, scale=scale),
                          reads=[psc.b], writes=[pt.b])
                    cx.op("pe", lambda e: e.matmul(out=po[0:65, c0:QB], lhsT=v[:, j, :], rhs=pt[:, c0:QB],
                                                   start=(j == 0), stop=(j == nj - 1)),
                          reads=[v.b, pt.b], writes=[po.b], inc=(j == nj - 1))
                    if j == nj - 1:
                        def norm(po=po, qb=qb):
                            rc = rec.next()
                            cx.op("act", lambda e: e.activation(out=rc[64:65, :], in_=po[64:65, :], func=AF.Ln),
                                  reads=[po.b], writes=[rc.b])
                            cx.op("act", lambda e: e.activation(out=rc[64:65, :], in_=rc[64:65, :], func=AF.Exp, scale=-1.0),
                                  reads=[rc.b], writes=[rc.b])
                            o1 = osb.next()
                            cx.op("act", lambda e: e.copy(out=o1[:], in_=po[0:64, :]), reads=[po.b], writes=[o1.b])

                            def fin():
                                pb = pB.next()
                                cx.op("pe", lambda e: e.matmul(out=pb[0:64, :], lhsT=onesf[64:65, :], rhs=rc[64:65, :],
                                                               start=True, stop=True),
                                      reads=[onesf.b, rc.b], writes=[pb.b])
                                a = ao.next()
                                cx.op("dve", lambda e: e.tensor_tensor(out=a[:], in0=pb[0:64, :], in1=o1[:], op=ALU.mult),
                                      reads=[pb.b, o1.b], writes=[a.b])
                                cx.dma("sp", self.attnT_d.ap()[h * 64:(h + 1) * 64, qb * QB:(qb + 1) * QB], a[:], a.b,
                                       reads=[a.b])
                            return fin
                        deferred.append((idx + 3, norm()))
                    while deferred and deferred[0][0] <= idx:
                        deferred.pop(0)[1]()
                while deferred:
                    deferred.pop(0)[1]()

    def mixer_c(self, li):
        nc, cx, S = self.nc, self.cx, self.S
        T = 512
        NS = S // T
        with ExitStack() as st:
            sb, ps = self.mk(st)
            wout = sb("wout", [P, 8, D], BF16)
            cat = [sb(f"cat{i}", [P, 8, T], BF16) for i in range(2)]
            xs = [sb(f"xs{i}", [P, D]) for i in range(4)]
            pD = Ring([ps(f"pD{i}", [P, 512]) for i in range(4)])
            self.load_cast(wout, self.w["mix_w_out"].ap()[li].rearrange("(k p) n -> p k n", p=P), 8)
            for s in range(NS):
                tsl = slice(s * T, (s + 1) * T)
                c = cat[s % 2]
                cx.dma("sp", c[:, 0:4, :], self.attnT_d.ap()[:, tsl].rearrange("(k p) t -> p k t", p=P), c.b,
                       writes=[c.b])
                cx.dma("sp", c[:, 4:8, :], self.convT_d.ap()[:, tsl].rearrange("(k p) t -> p k t", p=P), c.b,
                       writes=[c.b], chain=True)
                for i in range(4):
                    src, sbuf = self.src_x(s * 4 + i)
                    cx.dma("sp", xs[i][:], src, xs[i].b, reads=[sbuf], writes=[xs[i].b])
                for i in range(4):
                    t = s * 4 + i
                    x = xs[i]
                    for hh in range(2):
                        pd = pD.next()
                        for kc in range(8):
                            cx.op("pe", lambda e: e.matmul(out=pd[:], lhsT=c[:, kc, i * P:(i + 1) * P],
                                                           rhs=wout[:, kc, hh * 512:(hh + 1) * 512],
                                                           start=(kc == 0), stop=(kc == 7)),
                                  reads=[c.b, wout.b], writes=[pd.b], inc=(kc == 7))
                        cx.op("dve", lambda e: e.tensor_tensor(out=x[:, hh * 512:(hh + 1) * 512],
                                                               in0=x[:, hh * 512:(hh + 1) * 512], in1=pd[:],
                                                               op=ALU.add),
                              reads=[pd.b, x.b], writes=[x.b])
                    cx.dma("sp", self.out.ap()[t * P:(t + 1) * P, :], x[:], x.b, reads=[x.b], writes=[self.xt[t]])
        self.first = False

    def ssm_phase(self, li):
        self.ssm_a(li)
        self.cx.barrier()
        self.ssm_b(li)
        self.cx.barrier()
        self.ssm_c(li)

    def ssm_a(self, li):
        with ExitStack() as st:
            sb, ps = self.mk(st)
            self._ssm_a_body(li, sb, ps, None)

    def _ssm_a_body(self, li, sb, ps, pump):
        nc, cx, S = self.nc, self.cx, self.S
        T = 512
        NS = S // T
        KC = 8
        win = sb("win", [P, KC, D], BF16)
        gbc = sb("gbc", [P, D])
        xs = [sb(f"xs{i}", [P, D]) for i in range(4)]
        hn = [sb(f"hn{i}", [P, D], BF16) for i in range(2)]
        sq = sb("sq", [P, D], BF16)
        ss = [sb(f"ss{i}", [P, 1]) for i in range(2)]
        hnT = sb("hnT", [P, KC, T], BF16)
        ub = Ring([sb(f"ub{i}", [P, T], BF16) for i in range(3)])
        pT = [ps(f"pT{i}", [P, KC, P], BF16) for i in range(2)]
        pM = Ring([ps(f"pM{i}", [P, T]) for i in range(3)])
        cx.dma("sp", gbc[:], self.w["ssm_norm"].ap()[li:li + 1, :].partition_broadcast(P), gbc.b, writes=[gbc.b])
        self.load_cast(win, self.w["ssm_w_in"].ap()[li].rearrange("(k p) n -> p k n", p=P), KC)
        for s in range(NS):
            self.load_norm_T(s, xs, hn, sq, ss, pT, hnT, gbc)
            for j in range(8):
                pm = pM.next()
                for kc in range(KC):
                    cx.op("pe", lambda e: e.matmul(out=pm[:], lhsT=win[:, kc, j * P:(j + 1) * P], rhs=hnT[:, kc, :],
                                                   start=(kc == 0), stop=(kc == KC - 1)),
                          reads=[win.b, hnT.b], writes=[pm.b], inc=(kc == KC - 1))
                u = ub.next()
                cx.op("act", lambda e: e.copy(out=u[:].rearrange("p (l k) -> p l k", l=8),
                                              in_=pm[:].rearrange("p (k l) -> p l k", l=8)),
                      reads=[pm.b], writes=[u.b])
                cx.dma("sp", self.uT_d.ap()[j * P:(j + 1) * P, :, s * (T // 8):(s + 1) * (T // 8)],
                       u[:].rearrange("p (l k) -> p l k", l=8), u.b, reads=[u.b])
                if pump is not None and s >= PUMP_FROM:
                    pump()

    def ssm_b(self, li, with_a=False):
        nc, cx, S = self.nc, self.cx, self.S
        L = 8
        NK = S // L
        J = int(round(math.log2(NK)))
        assert 2 ** J == NK and NK <= 512
        W = self.w
        TWO_PI = 2.0 * math.pi
        with ExitStack() as st:
            sb, ps = self.mk(st)
            WB = [sb(f"WB{e}", [P, 8, 8, P], BF16) for e in range(2)]
            WC = sb("WC", [P, 64, 8, 32], BF16)
            Kb = sb("Kb", [P, 8, 8, P], BF16)
            a1 = sb("a1", [P, J, 64])
            a2 = sb("a2", [P, J, 64])
            I2 = sb("I2", [P, P])
            pmask = sb("pmask", [P, 2])
            bmask = sb("bmask", [P, P])
            dcol = sb("dcol", [P, 8])
            cx.dma("sp", I2[:], self.i2_d.ap(), I2.b, writes=[I2.b])
            cx.dma("sp", pmask[:], self.pmask_d.ap(), pmask.b, writes=[pmask.b])
            cx.dma("sp", bmask[:], self.bmask_d.ap(), bmask.b, writes=[bmask.b])
            cx.dma("sp", dcol[:], W["d_skip"].ap()[li].rearrange("(j q) -> q j", q=P), dcol.b, writes=[dcol.b],
                   allow_slow_non_contiguous=True)

            with ExitStack() as s2:
                sb2, ps2 = self.mk(s2)
                if with_a:
                    cx.rec = []
                cx.op("pool", lambda e: e.memset(WC[:], 0.0), writes=[WC.b])
                sm = lambda n: sb2(n, [P, 64])
                lr, lim, lsb, dt, mag, ang = sm("lr"), sm("lim"), sm("lsb"), sm("dt"), sm("mag"), sm("ang")
                yv, nf, fr, gt, sn, cs_, den, nr, t1, t2 = (sm("yv"), sm("nf"), sm("fr"), sm("gt"), sm("sn"), sm("cs"),
                                                           sm("den"), sm("nr"), sm("t1"), sm("t2"))
                ni = sb2("ni", [P, 64], I32)
                zr, zi = sm("zr"), sm("zi")
                PWr = sb2("PWr", [P, 9, 64])
                PWi = sb2("PWi", [P, 9, 64])
                DBr = sb2("DBr", [P, J, 64])
                DBi = sb2("DBi", [P, J, 64])
                big = lambda n, dt_=F32: sb2(n, [P, 1024], dt_)
                Bre, Bim = big("Bre"), big("Bim")
                Xr = [big("Xr0"), big("Xr1")]
                Xi = [big("Xi0"), big("Xi1")]
                Yr, Yi = Xr, Xi
                m1, m2 = big("m1"), big("m2")
                stk = big("stk")
                Bs = big("Bs", BF16)
                CLs = big("CLs", BF16)
                Cn = [sb2(f"Cn{i}", [P, 8, P]) for i in range(2)]
                ktmp = sb2("ktmp", [P, P])
                pw = Ring([ps2(f"pw{i}", [P, 4, P]) for i in range(3)])

                for hf in range(2):
                    hs = slice(hf * 64, hf * 64 + 64)
                    cx.dma("sp", lr[hs, :], W["lambda_re"].ap()[li].rearrange("g p -> p g"), lr.b, writes=[lr.b],
                           chain=True, allow_slow_non_contiguous=True)
                    cx.dma("sp", lim[hs, :], W["lambda_im"].ap()[li].rearrange("g p -> p g"), lim.b, writes=[lim.b],
                           chain=True, allow_slow_non_contiguous=True)
                    cx.dma("sp", Bre[hs, :].rearrange("p (g c) -> p g c", c=16),
                           W["b_re"].ap()[li].rearrange("g p c -> p g c"), Bre.b, writes=[Bre.b], chain=True)
                    cx.dma("sp", Bim[hs, :].rearrange("p (g c) -> p g c", c=16),
                           W["b_im"].ap()[li].rearrange("g p c -> p g c"), Bim.b, writes=[Bim.b], chain=True)
                    cx.dma("sp", Cn[0][:, :, hs], W["c_re"].ap()[li].rearrange("(t g) c p -> (g c) t p", g=8), Cn[0].b,
                           writes=[Cn[0].b], chain=True)
                    cx.dma("sp", Cn[1][:, :, hs], W["c_im"].ap()[li].rearrange("(t g) c p -> (g c) t p", g=8), Cn[1].b,
                           writes=[Cn[1].b], chain=True)
                cx.dma("sp", lsb[:], W["log_step"].ap()[li:li + 1, :].partition_broadcast(P), lsb.b, writes=[lsb.b])

                def V(fn, reads, writes, eng="dve"):
                    cx.op(eng, fn, reads=[r.b for r in reads], writes=[w_.b for w_ in writes])

                V(lambda e: e.activation(out=dt[:], in_=lsb[:], func=AF.Exp), [lsb], [dt], "act")
                V(lambda e: e.tensor_tensor(out=t1[:], in0=lr[:], in1=dt[:], op=ALU.mult), [lr, dt], [t1])
                V(lambda e: e.activation(out=mag[:], in_=t1[:], func=AF.Exp), [t1], [mag], "act")
                V(lambda e: e.tensor_tensor(out=ang[:], in0=lim[:], in1=dt[:], op=ALU.mult), [lim, dt], [ang])

                def sin_of(dst, shift):
                    V(lambda e: e.tensor_scalar(out=yv[:], in0=ang[:], scalar1=1.0 / TWO_PI, scalar2=shift,
                                                op0=ALU.mult, op1=ALU.add), [ang], [yv])
                    V(lambda e: e.tensor_copy(out=ni[:], in_=yv[:]), [yv], [ni])
                    V(lambda e: e.tensor_copy(out=nf[:], in_=ni[:]), [ni], [nf])
                    V(lambda e: e.tensor_tensor(out=fr[:], in0=yv[:], in1=nf[:], op=ALU.subtract), [yv, nf], [fr])
                    V(lambda e: e.tensor_scalar(out=gt[:], in0=fr[:], scalar1=0.5, scalar2=None, op0=ALU.is_gt),
                      [fr], [gt])
                    V(lambda e: e.tensor_tensor(out=fr[:], in0=fr[:], in1=gt[:], op=ALU.subtract), [fr, gt], [fr])
                    V(lambda e: e.tensor_scalar(out=gt[:], in0=fr[:], scalar1=-0.5, scalar2=None, op0=ALU.is_lt),
                      [fr], [gt])
                    V(lambda e: e.tensor_tensor(out=fr[:], in0=fr[:], in1=gt[:], op=ALU.add), [fr, gt], [fr])
                    V(lambda e: e.activation(out=dst[:], in_=fr[:], func=AF.Sin, scale=TWO_PI * (1.0 - 1e-6)),
                      [fr], [dst], "act")

                sin_of(sn, 0.0)
                sin_of(cs_, 0.25)
                ar, ai = PWr[:, 1, :], PWi[:, 1, :]
                V(lambda e: e.memset(PWr[:, 0, :], 1.0), [], [PWr])
                V(lambda e: e.memset(PWi[:, 0, :], 0.0), [], [PWi])
                V(lambda e: e.tensor_tensor(out=ar, in0=mag[:], in1=cs_[:], op=ALU.mult), [mag, cs_], [PWr])
                V(lambda e: e.tensor_tensor(out=ai, in0=mag[:], in1=sn[:], op=ALU.mult), [mag, sn], [PWi])
                V(lambda e: e.tensor_scalar(out=nr[:], in0=ar, scalar1=-1.0, scalar2=None, op0=ALU.add), [PWr], [nr])
                V(lambda e: e.tensor_tensor(out=t1[:], in0=lr[:], in1=lr[:], op=ALU.mult), [lr], [t1])
                V(lambda e: e.tensor_tensor(out=t2[:], in0=lim[:], in1=lim[:], op=ALU.mult), [lim], [t2])
                V(lambda e: e.tensor_tensor(out=den[:], in0=t1[:], in1=t2[:], op=ALU.add), [t1, t2], [den])
                V(lambda e: e.reciprocal(out=den[:], in_=den[:]), [den], [den])
                V(lambda e: e.tensor_tensor(out=t1[:], in0=nr[:], in1=lr[:], op=ALU.mult), [nr, lr], [t1])
                V(lambda e: e.tensor_tensor(out=t2[:], in0=ai, in1=lim[:], op=ALU.mult), [PWi, lim], [t2])
                V(lambda e: e.tensor_tensor(out=t1[:], in0=t1[:], in1=t2[:], op=ALU.add), [t1, t2], [t1])
                V(lambda e: e.tensor_tensor(out=zr[:], in0=t1[:], in1=den[:], op=ALU.mult), [t1, den], [zr])
                V(lambda e: e.tensor_tensor(out=t1[:], in0=ai, in1=lr[:], op=ALU.mult), [PWi, lr], [t1])
                V(lambda e: e.tensor_tensor(out=t2[:], in0=nr[:], in1=lim[:], op=ALU.mult), [nr, lim], [t2])
                V(lambda e: e.tensor_tensor(out=t1[:], in0=t1[:], in1=t2[:], op=ALU.subtract), [t1, t2], [t1])
                V(lambda e: e.tensor_tensor(out=zi[:], in0=t1[:], in1=den[:], op=ALU.mult), [t1, den], [zi])
                for d in range(2, 9):
                    pr_, pi_ = PWr[:, d - 1, :], PWi[:, d - 1, :]
                    V(lambda e: e.tensor_tensor(out=t1[:], in0=pr_, in1=ar, op=ALU.mult), [PWr], [t1])
                    V(lambda e: e.tensor_tensor(out=t2[:], in0=pi_, in1=ai, op=ALU.mult), [PWi], [t2])
                    V(lambda e: e.tensor_tensor(out=PWr[:, d, :], in0=t1[:], in1=t2[:], op=ALU.subtract), [t1, t2], [PWr])
                    V(lambda e: e.tensor_tensor(out=t1[:], in0=pr_, in1=ai, op=ALU.mult), [PWr, PWi], [t1])
                    V(lambda e: e.tensor_tensor(out=t2[:], in0=pi_, in1=ar, op=ALU.mult), [PWr, PWi], [t2])
                    V(lambda e: e.tensor_tensor(out=PWi[:, d, :], in0=t1[:], in1=t2[:], op=ALU.add), [t1, t2], [PWi])
                V(lambda e: e.tensor_copy(out=DBr[:, 0, :], in_=PWr[:, 8, :]), [PWr], [DBr])
                V(lambda e: e.tensor_copy(out=DBi[:, 0, :], in_=PWi[:, 8, :]), [PWi], [DBi])
                for j in range(1, J):
                    r_, i_ = DBr[:, j - 1, :], DBi[:, j - 1, :]
                    V(lambda e: e.tensor_tensor(out=t1[:], in0=r_, in1=r_, op=ALU.mult), [DBr], [t1])
                    V(lambda e: e.tensor_tensor(out=t2[:], in0=i_, in1=i_, op=ALU.mult), [DBi], [t2])
                    V(lambda e: e.tensor_tensor(out=DBr[:, j, :], in0=t1[:], in1=t2[:], op=ALU.subtract), [t1, t2], [DBr])
                    V(lambda e: e.tensor_tensor(out=t1[:], in0=r_, in1=i_, op=ALU.mult), [DBr, DBi], [t1])
                    V(lambda e: e.tensor_scalar(out=DBi[:, j, :], in0=t1[:], scalar1=2.0, scalar2=None, op0=ALU.mult),
                      [t1], [DBi])
                V(lambda e: e.tensor_copy(out=a1[0:64], in_=DBr[0:64]), [DBr], [a1])
                V(lambda e: e.tensor_scalar(out=a1[64:128], in0=DBi[64:128], scalar1=-1.0, scalar2=None, op0=ALU.mult),
                  [DBi], [a1])
                V(lambda e: e.tensor_copy(out=a2[0:64], in_=DBi[0:64]), [DBi], [a2])
                V(lambda e: e.tensor_copy(out=a2[64:128], in_=DBr[64:128]), [DBr], [a2])

                def cmul(orr, oii, xr, xi, br_ap, bi_ap, rb):
                    v3 = lambda t: t[:].rearrange("p (g c) -> p g c", c=16)
                    bb = lambda a: a.unsqueeze(2).to_broadcast([P, 64, 16])
                    V(lambda e: e.tensor_tensor(out=v3(m1), in0=v3(xr), in1=bb(br_ap), op=ALU.mult), [xr] + rb, [m1])
                    V(lambda e: e.tensor_tensor(out=v3(m2), in0=v3(xi), in1=bb(bi_ap), op=ALU.mult), [xi] + rb, [m2], "pool")
                    V(lambda e: e.tensor_tensor(out=orr[:], in0=m1[:], in1=m2[:], op=ALU.subtract), [m1, m2], [orr])
                    V(lambda e: e.tensor_tensor(out=v3(m1), in0=v3(xr), in1=bb(bi_ap), op=ALU.mult), [xr] + rb, [m1])
                    V(lambda e: e.tensor_tensor(out=v3(m2), in0=v3(xi), in1=bb(br_ap), op=ALU.mult), [xi] + rb, [m2], "pool")
                    V(lambda e: e.tensor_tensor(out=oii[:], in0=m1[:], in1=m2[:], op=ALU.add), [m1, m2], [oii], "pool")

                cmul(Xr[0], Xi[0], Bre, Bim, zr[:], zi[:], [zr, zi])
                for e_ in range(8):
                    cr, ci_ = Xr[e_ % 2], Xi[e_ % 2]
                    V(lambda e: e.tensor_copy(out=stk[0:64, :], in_=cr[0:64, :]), [cr], [stk])
                    V(lambda e: e.tensor_copy(out=stk[64:128, :], in_=ci_[64:128, :]), [ci_], [stk], "pool")
                    if e_ == 0:
                        V(lambda e: e.tensor_copy(out=Bs[:], in_=stk[:]), [stk], [Bs])
                    tau = 7 - e_
                    for j0 in (0, 4):
                        p_ = pw.next()
                        for jj in range(4):
                            j = j0 + jj
                            cx.op("pe", lambda e: e.transpose(out=p_[:, jj, :], in_=stk[:, j * P:(j + 1) * P],
                                                              identity=self.identf[:]),
                                  reads=[stk.b, self.b_const], writes=[p_.b], inc=(jj == 3))
                        for eo in range(2):
                            V(lambda e: e.tensor_scalar(out=WB[eo][:, j0:j0 + 4, tau, :], in0=p_[:], scalar1=pmask[:, eo:eo + 1],
                                                        scalar2=None, op0=ALU.mult), [p_, pmask], [WB[eo]])
                    if e_ < 7:
                        cmul(Xr[(e_ + 1) % 2], Xi[(e_ + 1) % 2], cr, ci_, ar, ai, [PWr, PWi])
                for ri in range(2):
                    dst = (Yr[0], Yi[0])[ri]
                    for j0 in (0, 4):
                        p_ = pw.next()
                        for jj in range(4):
                            j = j0 + jj
                            cx.op("pe", lambda e: e.transpose(out=p_[:, jj, :], in_=Cn[ri][:, j, :], identity=self.identf[:]),
                                  reads=[Cn[ri].b, self.b_const], writes=[p_.b], inc=(jj == 3))
                        V(lambda e: e.copy(out=dst[:, j0 * P:(j0 + 4) * P], in_=p_[:].rearrange("p a b -> p (a b)")),
                          [p_], [dst], "act")
                wc6 = WC[:].rearrange("p (pr e) t (e2 c) -> p pr e t e2 c", e=2, e2=2)
                for d in range(9):
                    cr, ci_ = Yr[d % 2], Yi[d % 2]
                    if d <= 7:
                        V(lambda e: e.tensor_copy(out=CLs[0:64, :], in_=cr[0:64, :]), [cr], [CLs])
                        V(lambda e: e.tensor_scalar(out=CLs[64:128, :], in0=ci_[64:128, :], scalar1=-1.0, scalar2=None,
                                                    op0=ALU.mult), [ci_], [CLs], "pool")
                        for j0 in (0, 4):
                            p_ = pw.next()
                            for jj in range(4):
                                j = j0 + jj
                                cx.op("pe", lambda e: e.matmul(out=p_[:, jj, :], lhsT=Bs[:, j * P:(j + 1) * P],
                                                               rhs=CLs[:, j * P:(j + 1) * P], start=True, stop=True),
                                      reads=[Bs.b, CLs.b], writes=[p_.b], inc=(jj == 3))
                            for jj in range(4):
                                j = j0 + jj
                                if d == 0:
                                    V(lambda e: e.tensor_tensor(out=ktmp[:], in0=p_[:, jj, :], in1=bmask[:], op=ALU.mult),
                                      [p_, bmask], [ktmp])
                                    V(lambda e: e.scalar_tensor_tensor(out=Kb[:, j, 0, :], in0=self.identf[:],
                                                                       scalar=dcol[:, j:j + 1], in1=ktmp[:],
                                                                       op0=ALU.mult, op1=ALU.add),
                                      [ktmp, dcol], [Kb])
                                else:
                                    V(lambda e: e.tensor_tensor(out=Kb[:, j, d, :], in0=p_[:, jj, :], in1=bmask[:],
                                                                op=ALU.mult), [p_, bmask], [Kb])
                    if d >= 1:
                        t = d - 1
                        c4r = cr[:].rearrange("p (pr e c) -> p pr e c", e=2, c=16)
                        c4i = ci_[:].rearrange("p (pr e c) -> p pr e c", e=2, c=16)
                        for e2 in range(2):
                            V(lambda e: e.tensor_copy(out=wc6[0:64, :, e2, t, e2, :], in_=c4r[0:64, :, e2, :]), [cr], [WC])
                            V(lambda e: e.tensor_scalar(out=wc6[64:128, :, e2, t, e2, :], in0=c4i[64:128, :, e2, :],
                                                        scalar1=-1.0, scalar2=None, op0=ALU.mult), [ci_], [WC], "pool")
                    if d < 8:
                        cmul(Yr[(d + 1) % 2], Yi[(d + 1) % 2], cr, ci_, ar, ai, [PWr, PWi])
                if with_a:
                    items, cx.rec = cx.rec, None
                    per = len(items) // 56 + 1
                    if DEBUG_NO_INTERLEAVE:
                        cx.pump(items, len(items))
                    self._ssm_a_body(li, sb2, ps2, lambda: cx.pump(items, per))
                    cx.pump(items, len(items))
            cx.barrier()

            with ExitStack() as s3:
                sb3, ps3 = self.mk(s3)
                uT = [sb3(f"uT{i}", [P, S], BF16) for i in range(3)]
                yT = [sb3(f"yT{i}", [P, S], BF16) for i in range(2)]
                Sb = [[sb3(f"Sb{k}{i}", [P, NK + 1], BF16) for i in range(8)] for k in range(3)]
                Rr = Ring([sb3(f"R{i}", [P, P], BF16) for i in range(12)])
                Rb1 = {r_: Buf("Rl") for r_ in Rr.items}
                pV = Ring([ps3(f"pV{i}", [P, NK]) for i in range(2)])
                pDb = Ring([ps3(f"pDb{i}", [P, NK]) for i in range(3)])
                pY = Ring([ps3(f"pY{i}", [P, NK]) for i in range(2)])
                for k_ in range(3):
                    for g_ in range(8):
                        cx.op("dve", lambda e: e.memset(Sb[k_][g_][:, 0:1], 0.0), writes=[Sb[k_][g_].b])

                def stage_b(j):
                    u = uT[j % 3]
                    u3 = u[:].rearrange("p (l k) -> p k l", l=L)
                    out = []

                    def load():
                        cx.dma("sp", u[:].rearrange("p (l k) -> p l k", l=L), self.uT_d.ap()[j * P:(j + 1) * P],
                               u.b, writes=[u.b])
                    out.append(load)
                    for g_ in range(8):
                        def grp(g_=g_):
                            q, eo = g_ // 2, g_ % 2
                            rows = slice(32 * q, 32 * q + 32)
                            pv = pV.next()
                            sg = Sb[j % 3][g_]
                            for tau in range(L):
                                cx.op("pe", lambda e: e.matmul(out=pv[:], lhsT=WB[eo][rows, j, tau, :], rhs=u3[rows, :, tau],
                                                               start=(tau == 0), stop=(tau == L - 1),
                                                               tile_position=(32 * q, 0)),
                                      reads=[WB[eo].b, u.b], writes=[pv.b], inc=(tau == L - 1))
                            cx.op("act", lambda e: e.copy(out=sg[:, 1:NK + 1], in_=pv[:]), reads=[pv.b], writes=[sg.b])
                        out.append(grp)
                    return out

                def stage_c(j):
                    out = []
                    for jj in range(J):
                        def step(jj=jj):
                            dsh = 2 ** jj
                            for g_ in range(8):
                                g = 8 * j + g_
                                R = Rr.next()
                                cx.op("pool", lambda e: e.tensor_scalar(out=R[:, 0:64], in0=I2[:, 0:64],
                                                                        scalar1=a1[:, jj, g:g + 1], scalar2=1.0,
                                                                        op0=ALU.mult, op1=ALU.mult),
                                      reads=[I2.b, a1.b], writes=[Rb1[R]])
                                cx.op("act", lambda e: e.activation(out=R[:, 64:128], in_=I2[:, 64:128], func=AF.Copy,
                                                                    scale=a2[:, jj, g:g + 1]),
                                      reads=[I2.b, a2.b], writes=[R.b])
                                pd = pDb.next()
                                sg = Sb[j % 3][g_]
                                cx.op("pe", lambda e: e.matmul(out=pd[:, 0:NK - dsh], lhsT=R[:], rhs=sg[:, 1:NK + 1 - dsh],
                                                               start=True, stop=True),
                                      reads=[R.b, Rb1[R], sg.b], writes=[pd.b])
                                cx.op("dve", lambda e: e.tensor_tensor(out=sg[:, 1 + dsh:NK + 1], in0=pd[:, 0:NK - dsh],
                                                                       in1=sg[:, 1 + dsh:NK + 1], op=ALU.add),
                                      reads=[pd.b, sg.b], writes=[sg.b])
                        out.append(step)
                    return out

                def stage_ad(j):
                    u = uT[j % 3]
                    y = yT[j % 2]
                    u3 = u[:].rearrange("p (l k) -> p k l", l=L)
                    y3 = y[:].rearrange("p (k l) -> p k l", l=L)
                    out = []
                    for t in range(L):
                        def pos(t=t):
                            py = pY.next()
                            for tau in range(t + 1):
                                cx.op("pe", lambda e: e.matmul(out=py[:], lhsT=Kb[:, j, t - tau, :], rhs=u3[:, :, tau],
                                                               start=(tau == 0), stop=False),
                                      reads=[Kb.b, u.b], writes=[py.b], inc=False)
                            for g_ in range(8):
                                g = 8 * j + g_
                                q = g_ // 2
                                sg = Sb[j % 3][g_]
                                cx.op("pe", lambda e: e.matmul(out=py[32 * q:32 * q + 32, :], lhsT=WC[:, g, t, :],
                                                               rhs=sg[:, 0:NK], start=False, stop=(g_ == 7),
                                                               tile_position=(0, 32 * q)),
                                      reads=[WC.b, sg.b], writes=[py.b], inc=(g_ == 7))
                            cx.op("act", lambda e: e.activation(out=y3[:, :, t], in_=py[:], func=AF.Gelu_apprx_tanh),
                                  reads=[py.b], writes=[y.b])
                        out.append(pos)

                    def store():
                        cx.dma("sp", self.gT_d.ap()[j * P:(j + 1) * P, :], y[:], y.b, reads=[y.b])
                    out.append(store)
                    return out

                for j in range(-1, 9):
                    lb = stage_b(j + 1) if 0 <= j + 1 < 8 else []
                    lc = stage_c(j) if 0 <= j < 8 else []
                    la = stage_ad(j - 1) if 0 <= j - 1 < 8 else []
                    n = max(len(lb), len(lc), len(la))
                    for i in range(n):
                        if i < len(lc):
                            lc[i]()
                        if i < len(lb):
                            lb[i]()
                        if i < len(la):
                            la[i]()

    def ssm_c(self, li):
        nc, cx, S = self.nc, self.cx, self.S
        T = 512
        NS = S // T
        with ExitStack() as st:
            sb, ps = self.mk(st)
            wg = sb("wglu", [P, 8, 2 * D], BF16)
            gt = [sb(f"gt{i}", [P, 8, T], BF16) for i in range(2)]
            xs = [sb(f"xs{i}", [P, D]) for i in range(4)]
            sg = Ring([sb(f"sg{i}", [P, 512]) for i in range(2)])
            tm = Ring([sb(f"tm{i}", [P, 512]) for i in range(2)])
            pA = Ring([ps(f"pA{i}", [P, 512]) for i in range(3)])
            pB = Ring([ps(f"pB{i}", [P, 512]) for i in range(3)])
            self.load_cast(wg, self.w["w_glu"].ap()[li].rearrange("(k p) n -> p k n", p=P), 8)
            for s in range(NS):
                tsl = slice(s * T, (s + 1) * T)
                g = gt[s % 2]
                cx.dma("sp", g[:], self.gT_d.ap()[:, tsl].rearrange("(k p) t -> p k t", p=P), g.b, writes=[g.b])
                for i in range(4):
                    src, sbuf = self.src_x(s * 4 + i)
                    cx.dma("sp", xs[i][:], src, xs[i].b, reads=[sbuf], writes=[xs[i].b])
                for i in range(4):
                    t = s * 4 + i
                    x = xs[i]
                    for hh in range(2):
                        pa, pb = pA.next(), pB.next()
                        for (pp, c0) in ((pa, hh * 512), (pb, D + hh * 512)):
                            for kc in range(8):
                                cx.op("pe", lambda e: e.matmul(out=pp[:], lhsT=g[:, kc, i * P:(i + 1) * P],
                                                               rhs=wg[:, kc, c0:c0 + 512], start=(kc == 0), stop=(kc == 7)),
                                      reads=[g.b, wg.b], writes=[pp.b], inc=(kc == 7))
                        s1 = sg.next()
                        cx.op("act", lambda e: e.activation(out=s1[:], in_=pb[:], func=AF.Sigmoid), reads=[pb.b], writes=[s1.b])
                        m = tm.next()
                        cx.op("dve", lambda e: e.tensor_tensor(out=m[:], in0=pa[:], in1=s1[:], op=ALU.mult),
                              reads=[pa.b, s1.b], writes=[m.b])
                        cx.op("pool", lambda e: e.tensor_tensor(out=x[:, hh * 512:(hh + 1) * 512],
                                                                in0=x[:, hh * 512:(hh + 1) * 512], in1=m[:], op=ALU.add),
                              reads=[m.b, x.b], writes=[x.b])
                    cx.dma("sp", self.out.ap()[t * P:(t + 1) * P, :], x[:], x.b, reads=[x.b], writes=[self.xt[t]])
        self.first = False


WEIGHT_SHAPES = {
    "attn_norm": (2, 1024), "mix_w_in": (2, 1024, 2080), "cq_norm": (2, 256), "ckv_norm": (2, 256),
    "w_uq": (2, 256, 768), "w_ukv": (2, 256, 1024), "q_gain": (2, 96), "k_gain": (2, 96),
    "sconv_w": (2, 3, 512), "mix_w_out": (2, 1024, 1024), "ssm_norm": (2, 1024), "ssm_w_in": (2, 1024, 1024),
    "lambda_re": (2, 64, 64), "lambda_im": (2, 64, 64), "log_step": (2, 64),
    "b_re": (2, 64, 64, 16), "b_im": (2, 64, 64, 16), "c_re": (2, 64, 16, 64), "c_im": (2, 64, 16, 64),
    "d_skip": (2, 1024), "w_glu": (2, 1024, 2048), "ffn_norm": (4, 1024), "ffn_w_up": (4, 1024, 5632),
    "ffn_conv_w": (4, 3, 5632), "ffn_w_down": (4, 2816, 1024),
}


def host_consts(S):
    rot = np.zeros((96, 96), np.float32)
    for i in range(16):
        rot[80 + i, 64 + i] = -1.0
        rot[64 + i, 80 + i] = 1.0
    kk = np.arange(P)[:, None]
    qq = np.arange(P)[None, :]
    tri = np.where(kk <= qq, 0.0, -30000.0).astype(np.float32)
    inv_freq = (1.0 / (10000.0 ** (np.arange(0, 32, 2, dtype=np.float32) / np.float32(32)))).astype(np.float32)
    ang = (np.arange(S, dtype=np.float32)[None, :] * inv_freq[:, None]).astype(np.float32)
    cos_t = np.ones((96, S), np.float32)
    sin_t = np.zeros((96, S), np.float32)
    cos_t[64:80] = np.cos(ang)
    cos_t[80:96] = np.cos(ang)
    sin_t[64:80] = np.sin(ang)
    sin_t[80:96] = np.sin(ang)
    i2 = np.tile(np.eye(64, dtype=np.float32), (2, 2))
    pidx = np.arange(P)
    pmask = np.stack([((pidx // 16) % 2 == 0), ((pidx // 16) % 2 == 1)], axis=1).astype(np.float32)
    bmask = ((pidx[:, None] // 16) == (pidx[None, :] // 16)).astype(np.float32)
    return {"ident": np.eye(P, dtype=np.float32), "rot": rot, "tri": tri, "cos_t": cos_t, "sin_t": sin_t,
            "i2": i2, "pmask": pmask, "bmask": bmask}


def all_phases():
    ph = []
    for l in range(DEPTH):
        if l % 2 == 0:
            ph += [("mixa", l), ("mixb", l), ("mixc", l)]
        else:
            ph += [("ssmab", l), ("ssmc", l)]
        ph += [("ffn", l)]
    return ph


def kernel(**inputs):
    x = np.ascontiguousarray(inputs["x"], dtype=np.float32)
    B, S, _ = x.shape
    prog = Prog(S, all_phases())
    nc = prog.build()
    consts = host_consts(S)
    shared = {k: np.ascontiguousarray(inputs[k], dtype=np.float32) for k in WEIGHT_SHAPES}
    in_maps = []
    for c in range(B):
        m = {"x": x[c]}
        m.update(shared)
        m.update(consts)
        in_maps.append(m)
    res = run_bass_kernel_spmd(nc, in_maps, core_ids=list(range(B)))
    return np.stack([r["out"] for r in res.results], axis=0)
```
